# Optimizing a Trainium2 kernel written in Bass

```python
import jax, jax.numpy as jnp
from jax import lax
import numpy as np

D_MODEL = 1024
BATCH = 8
SEQ = 2048
DEPTH = 1
DEC_BATCH = 128
DEC_SEQ = 1
PAST_LEN = 16384
PAGE_SIZE = 128

CHUNK = 128
W_A = D_MODEL
A_GROUPS = 8
A_GROUP_DIM = W_A // A_GROUPS
W_B = D_MODEL
B_HEADS = 16
B_HEAD_DIM = W_B // B_HEADS
CONV_WIDTH = 4
LRU_C = 8.0
D_FF = int(round(8 * D_MODEL / 3 / 64)) * 64
EPS = 1e-6
SPLITS = (W_A, 2 * W_A, 2 * W_A + W_B, 2 * W_A + 2 * W_B, 2 * W_A + 2 * W_B + D_MODEL)
W_IN_COLS = 2 * W_A + 2 * W_B + 2 * D_MODEL

kernel_name = "macaron_gmlp_rglru_gated_hybrid_step"


def rms_norm(x, g):
    xf = x.astype(jnp.float32)
    y = xf * lax.rsqrt(jnp.mean(xf * xf, axis=-1, keepdims=True) + EPS)
    return (y * g.astype(jnp.float32)).astype(x.dtype)


def swiglu(x, w_gate, w_up, w_down):
    return (jax.nn.silu(x @ w_gate) * (x @ w_up)) @ w_down


def chunk_spatial_gate(u, v, w_s, b_s):
    B, T, _ = v.shape
    L = CHUNK if T >= CHUNK else T
    pad = (-T) % L
    vp = jnp.pad(v, ((0, 0), (0, pad), (0, 0)))
    n = (T + pad) // L
    vc = vp.reshape(B, n, L, A_GROUPS, A_GROUP_DIM)
    mask = jnp.tril(jnp.ones((L, L), dtype=bool))
    w = jnp.where(mask, w_s[:, :L, :L], 0).astype(v.dtype)
    s = jnp.einsum('gts,bnsgd->bntgd', w, vc) + b_s[:, :L].T[None, None, :, :, None].astype(v.dtype)
    s = s.reshape(B, n * L, W_A)[:, :T]
    return u * s


def causal_conv(xb, conv_prev, w, b):
    T = xb.shape[1]
    xp = jnp.concatenate([conv_prev.astype(xb.dtype), xb], axis=1)
    y = b.astype(xb.dtype)
    for k in range(CONV_WIDTH):
        y = y + w[k] * xp[:, k:k + T]
    return y, xp[:, T:]


def rg_lru(x, h0, w_r, b_r, w_i, b_i, lam):
    B, T, _ = x.shape
    xf = x.astype(jnp.float32)
    xh = xf.reshape(B, T, B_HEADS, B_HEAD_DIM)
    r = jax.nn.sigmoid(jnp.einsum('bthi,hij->bthj', xh, w_r.astype(jnp.float32)) + b_r.astype(jnp.float32)).reshape(B, T, W_B)
    i = jax.nn.sigmoid(jnp.einsum('bthi,hij->bthj', xh, w_i.astype(jnp.float32)) + b_i.astype(jnp.float32)).reshape(B, T, W_B)
    log_a = -LRU_C * r * jax.nn.softplus(-lam.astype(jnp.float32))
    a = jnp.exp(log_a)
    b_in = jnp.sqrt(-jnp.expm1(2.0 * log_a)) * (i * xf)

    def combine(left, right):
        a1, b1 = left
        a2, b2 = right
        return a1 * a2, a2 * b1 + b2

    a_cum, b_cum = lax.associative_scan(combine, (a, b_in), axis=1)
    h = a_cum * h0.astype(jnp.float32)[:, None, :] + b_cum
    return h.astype(x.dtype), h[:, -1].astype(x.dtype)


def decoder_layer(x, h0, conv_prev,
                  ffn1_norm, ffn1_w_gate, ffn1_w_up, ffn1_w_down,
                  mix_norm, w_in, gmlp_v_norm, spatial_w, spatial_b,
                  conv_w, conv_b, lru_w_r, lru_b_r, lru_w_i, lru_b_i, lru_lambda,
                  proj_a, proj_b, w_out,
                  ffn2_norm, ffn2_w_gate, ffn2_w_up, ffn2_w_down):
    x = x + 0.5 * swiglu(rms_norm(x, ffn1_norm), ffn1_w_gate, ffn1_w_up, ffn1_w_down)
    h = rms_norm(x, mix_norm)
    z = h @ w_in
    u, v, xb, gate_br, g_a, g_b = jnp.split(z, SPLITS, axis=-1)
    u = jax.nn.gelu(u)
    v = rms_norm(jax.nn.gelu(v), gmlp_v_norm)
    y_a = chunk_spatial_gate(u, v, spatial_w, spatial_b)
    xc, conv_new = causal_conv(xb, conv_prev, conv_w, conv_b)
    hl, h_last = rg_lru(xc, h0, lru_w_r, lru_b_r, lru_w_i, lru_b_i, lru_lambda)
    y_b = hl * jax.nn.gelu(gate_br)
    m = jax.nn.sigmoid(g_a) * (y_a @ proj_a) + jax.nn.sigmoid(g_b) * (y_b @ proj_b)
    x = x + m @ w_out
    x = x + 0.5 * swiglu(rms_norm(x, ffn2_norm), ffn2_w_gate, ffn2_w_up, ffn2_w_down)
    return x, h_last, conv_new, v


def setup_inputs(seed: int = 0) -> dict:
    key = jax.random.key(seed)
    ks = jax.random.split(key, 32)
    f32 = jnp.float32

    def nrm(k, shape, scale):
        return jax.random.normal(k, shape, f32) * scale

    def gain(k, shape):
        return 1.0 + 0.01 * jax.random.normal(k, shape, f32)

    u_a = jax.random.uniform(ks[20], (DEPTH, W_B), f32, 0.9, 0.999)
    a0 = u_a ** (1.0 / LRU_C)
    lam = jnp.log(a0) - jnp.log1p(-a0)

    return {
        "x_prompt": nrm(ks[0], (BATCH, SEQ, D_MODEL), 1.0),
        "x_sample": nrm(ks[1], (DEC_BATCH, DEC_SEQ, D_MODEL), 1.0),
        "state_lru_h": nrm(ks[2], (DEPTH, DEC_BATCH, W_B), 0.5),
        "state_conv": nrm(ks[3], (DEPTH, DEC_BATCH, CONV_WIDTH - 1, W_B), 1.0),
        "ffn1_norm": gain(ks[4], (DEPTH, D_MODEL)),
        "ffn1_w_gate": nrm(ks[5], (DEPTH, D_MODEL, D_FF), D_MODEL ** -0.5),
        "ffn1_w_up": nrm(ks[6], (DEPTH, D_MODEL, D_FF), D_MODEL ** -0.5),
        "ffn1_w_down": nrm(ks[7], (DEPTH, D_FF, D_MODEL), D_FF ** -0.5),
        "mix_norm": gain(ks[8], (DEPTH, D_MODEL)),
        "w_in": nrm(ks[9], (DEPTH, D_MODEL, W_IN_COLS), D_MODEL ** -0.5),
        "gmlp_v_norm": gain(ks[10], (DEPTH, W_A)),
        "spatial_w": nrm(ks[11], (DEPTH, A_GROUPS, CHUNK, CHUNK), CHUNK ** -0.5),
        "spatial_b": 1.0 + 0.1 * jax.random.normal(ks[12], (DEPTH, A_GROUPS, CHUNK), f32),
        "conv_w": nrm(ks[13], (DEPTH, CONV_WIDTH, W_B), CONV_WIDTH ** -0.5),
        "conv_b": nrm(ks[14], (DEPTH, W_B), 0.01),
        "lru_w_r": nrm(ks[15], (DEPTH, B_HEADS, B_HEAD_DIM, B_HEAD_DIM), B_HEAD_DIM ** -0.5),
        "lru_b_r": nrm(ks[16], (DEPTH, B_HEADS, B_HEAD_DIM), 0.01),
        "lru_w_i": nrm(ks[17], (DEPTH, B_HEADS, B_HEAD_DIM, B_HEAD_DIM), B_HEAD_DIM ** -0.5),
        "lru_b_i": nrm(ks[18], (DEPTH, B_HEADS, B_HEAD_DIM), 0.01),
        "lru_lambda": lam,
        "proj_a": nrm(ks[21], (DEPTH, W_A, D_MODEL), W_A ** -0.5),
        "proj_b": nrm(ks[22], (DEPTH, W_B, D_MODEL), W_B ** -0.5),
        "w_out": nrm(ks[23], (DEPTH, D_MODEL, D_MODEL), D_MODEL ** -0.5),
        "ffn2_norm": gain(ks[24], (DEPTH, D_MODEL)),
        "ffn2_w_gate": nrm(ks[25], (DEPTH, D_MODEL, D_FF), D_MODEL ** -0.5),
        "ffn2_w_up": nrm(ks[26], (DEPTH, D_MODEL, D_FF), D_MODEL ** -0.5),
        "ffn2_w_down": nrm(ks[27], (DEPTH, D_FF, D_MODEL), D_FF ** -0.5),
        "final_norm": gain(ks[28], (D_MODEL,)),
    }


def reference(x_prompt, x_sample, state_lru_h, state_conv,
              ffn1_norm, ffn1_w_gate, ffn1_w_up, ffn1_w_down,
              mix_norm, w_in, gmlp_v_norm, spatial_w, spatial_b,
              conv_w, conv_b, lru_w_r, lru_b_r, lru_w_i, lru_b_i, lru_lambda,
              proj_a, proj_b, w_out,
              ffn2_norm, ffn2_w_gate, ffn2_w_up, ffn2_w_down, final_norm):
    xp = x_prompt
    xs = x_sample
    h_p_list, c_p_list, h_s_list, c_s_list, v_s_list = [], [], [], [], []
    for l in range(DEPTH):
        weights = (ffn1_norm[l], ffn1_w_gate[l], ffn1_w_up[l], ffn1_w_down[l],
                   mix_norm[l], w_in[l], gmlp_v_norm[l], spatial_w[l], spatial_b[l],
                   conv_w[l], conv_b[l], lru_w_r[l], lru_b_r[l], lru_w_i[l], lru_b_i[l], lru_lambda[l],
                   proj_a[l], proj_b[l], w_out[l],
                   ffn2_norm[l], ffn2_w_gate[l], ffn2_w_up[l], ffn2_w_down[l])
        h0_p = jnp.zeros((xp.shape[0], W_B), xp.dtype)
        c0_p = jnp.zeros((xp.shape[0], CONV_WIDTH - 1, W_B), xp.dtype)
        xp, h_p, c_p, _ = decoder_layer(xp, h0_p, c0_p, *weights)
        xs, h_s, c_s, v_s = decoder_layer(xs, state_lru_h[l], state_conv[l], *weights)
        h_p_list.append(h_p)
        c_p_list.append(c_p)
        h_s_list.append(h_s)
        c_s_list.append(c_s)
        v_s_list.append(v_s)
    y_prompt = rms_norm(xp, final_norm)
    y_sample = rms_norm(xs, final_norm)
    new_lru_h_prompt = jnp.stack(h_p_list, axis=0)
    new_conv_prompt = jnp.stack(c_p_list, axis=0)
    new_lru_h_sample = jnp.stack(h_s_list, axis=0)
    new_conv_sample = jnp.stack(c_s_list, axis=0)
    new_chunk_v_sample = jnp.stack(v_s_list, axis=0)
    return (y_prompt, y_sample, new_lru_h_prompt, new_conv_prompt, new_lru_h_sample, new_conv_sample, new_chunk_v_sample)
```

```python
import contextlib
import numpy as np
import concourse.bass as bass
import concourse.mybir as mybir
from concourse.bass_utils import run_bass_kernel_spmd

F32 = mybir.dt.float32
BF16 = mybir.dt.bfloat16
AF = mybir.ActivationFunctionType
ALU = mybir.AluOpType

D = 1024
DFF = 2752
SEQ = 2048
NSAMP = 16
HALF = 1024
NTW = HALF + NSAMP
EPS = 1e-6
NV = 11
NATIVE_GELU = True
GK = 0.7978845608028654
GC = 0.044715 * GK


class Op:
    __slots__ = ("idx", "eng", "fn", "dmas", "deps", "has_dep", "sig", "key")


class Prog:
    tie = 0.0
    lat = 0.12

    def __init__(self):
        self.ops = []
        self.res = {}
        self.bank = 0

    def _add(self, op, reads, writes):
        op.idx = len(self.ops)
        self.ops.append(op)
        deps = set()
        extra = set()
        for r in list(reads) + list(writes):
            if isinstance(r, tuple):
                if r[0] in ("hid", "vtm", "ya", "yb", "m"):
                    extra.add("ARENA")
                if r[0] in ("wb", "L", "xb", "xbh"):
                    extra.add("WBMEM")
        for r in list(reads) + list(extra):
            st = self.res.setdefault(r, [[], []])
            deps.update(st[0])
            st[1].append(op.idx)
        for w in writes:
            st = self.res.setdefault(w, [[], []])
            deps.update(st[0])
            deps.update(st[1])
            st[0] = [op.idx]
            st[1] = []
        deps.discard(op.idx)
        op.deps = deps
        op.has_dep = False
        return op

    def op(self, eng, fn, r=(), w=()):
        o = Op()
        o.eng = eng
        o.fn = fn
        o.dmas = None
        o.key = None
        o.sig = None
        return self._add(o, r, w)

    def dma(self, eng, pairs, r=(), w=(), key=None, **kw):
        o = Op()
        o.eng = eng
        o.fn = kw
        o.dmas = pairs
        o.key = key
        o.sig = None
        return self._add(o, r, w)

    def fence(self, name):
        st = self.res.setdefault(name, [[], []])
        o = Op()
        o.eng = None
        o.fn = None
        o.dmas = None
        o.key = None
        o.sig = None
        o.idx = len(self.ops)
        self.ops.append(o)
        o.deps = set(st[0]) | set(st[1])
        o.has_dep = False
        st[0] = [o.idx]
        st[1] = []

    def nb(self):
        b = self.bank
        self.bank = (b + 1) % 8
        return b

    def schedule(self, window=600):
        AFT = AF
        tblsets = {AFT.Gelu_apprx_tanh: {11}, AFT.Tanh: {0, 11, 18, 2}, AFT.Exp: {0}, AFT.Sqrt: {3}, AFT.Silu: {18},
                   AFT.Sigmoid: {2}}
        n = len(self.ops)
        succ = [[] for _ in range(n)]
        indeg = [0] * n
        for o in self.ops:
            indeg[o.idx] = len(o.deps)
            for d in o.deps:
                succ[d].append(o.idx)
        finish = [0.0] * n
        est = [0.0] * n
        engs = ("pe", "act", "dve", "pool", "sp")
        def ocost(o):
            if o.eng is None:
                return 0.0
            if o.dmas is not None:
                return 2.5
            return getattr(o.fn, "cost", 0.1) if o.fn is not None else 0.02
        bl = [0.0] * n
        for i in range(n - 1, -1, -1):
            m_ = 0.0
            for j in succ[i]:
                if bl[j] > m_:
                    m_ = bl[j]
            bl[i] = m_ + ocost(self.ops[i])
        TIE = self.tie
        free = {e: 0.0 for e in engs}
        ready = {e: [] for e in engs}
        order = {e: [] for e in engs}
        cur_tbl = [None]
        done = [False] * n
        low = 0
        nsched = [0]
        LAT = self.lat

        def release(i):
            for j in succ[i]:
                oj = self.ops[j]
                t_ = finish[i] + LAT
                if t_ > est[j]:
                    est[j] = t_
                indeg[j] -= 1
                if indeg[j] == 0:
                    if oj.eng is None:
                        finish[j] = est[j] - LAT
                        done[j] = True
                        nsched[0] += 1
                        release(j)
                    else:
                        ready[oj.eng].append(j)

        for o in list(self.ops):
            if indeg[o.idx] == 0 and not done[o.idx]:
                if o.eng is None:
                    done[o.idx] = True
                    nsched[0] += 1
                    release(o.idx)
                else:
                    ready[o.eng].append(o.idx)
        while nsched[0] < n:
            while low < n and done[low]:
                low += 1
            lim = low + window
            best = None
            for e in engs:
                fe = free[e]
                for i in ready[e]:
                    if i >= lim:
                        continue
                    st = est[i] if est[i] > fe else fe
                    pen = 0.0
                    if e == "act":
                        fn = self.ops[i].fn
                        tb = getattr(fn, "tbl", None) if fn is not None else None
                        acc = tblsets.get(tb)
                        if acc is not None and cur_tbl[0] not in acc:
                            pen = 1.28
                    key = (st + pen, i)
                    if best is None or key[0] < best[0][0] - TIE or \
                            (key[0] <= best[0][0] + TIE and i < best[2]):
                        best = (key, e, i, st, pen)
            if best is None:
                raise RuntimeError("scheduler stuck")
            _, e, i, st, pen = best
            o = self.ops[i]
            ready[e].remove(i)
            if o.dmas is not None:
                nbytes = 0
                for (dst, src) in o.dmas:
                    nb_ = 1
                    for d_ in src.shape:
                        nb_ *= int(d_)
                    nbytes += nb_ * 4
                occ = 0.06 * len(o.dmas)
                dur = 4.0 + nbytes / 100000.0
            else:
                c = getattr(o.fn, "cost", 0.1) if o.fn is not None else 0.02
                occ = c + pen
                dur = occ
                if e == "act" and pen > 0:
                    cur_tbl[0] = min(tblsets.get(o.fn.tbl))
            free[e] = st + occ
            finish[i] = st + dur
            done[i] = True
            nsched[0] += 1
            order[e].append(i)
            release(i)
        self.makespan = max(finish) if n else 0.0
        return order

    def emit(self, nc, stack, sched=True):
        engs = ("pe", "act", "dve", "pool", "sp")
        sems = {}

        def sem(name):
            if name not in sems:
                sems[name] = stack.enter_context(nc.semaphore(name))
            return sems[name]

        if sched:
            order = self.schedule()
        else:
            order = {e: [o.idx for o in self.ops if o.eng == e] for e in engs}
        pos = {}
        for e in engs:
            for p_, i in enumerate(order[e]):
                pos[i] = p_
        vcache = {}

        def frontier(idxs):
            best = {}
            dm = set()
            for d in idxs:
                dd = self.ops[d]
                if dd.eng is None:
                    if d not in vcache:
                        vcache[d] = frontier(dd.deps)
                    b2, d2 = vcache[d]
                    for e_, i_ in b2.items():
                        if e_ not in best or pos[best[e_]] < pos[i_]:
                            best[e_] = i_
                    dm |= d2
                elif dd.dmas is not None:
                    dm.add(d)
                else:
                    if dd.eng not in best or pos[best[dd.eng]] < pos[d]:
                        best[dd.eng] = d
            return best, dm

        import sys
        sys.setrecursionlimit(10000)
        eff = {}
        for o in self.ops:
            if o.eng is None:
                continue
            b_, d_ = frontier(o.deps)
            lst = []
            for e_, i_ in b_.items():
                if e_ == "pe" and o.eng == "pe" and o.dmas is None:
                    continue
                lst.append(i_)
            lst.extend(d_)
            eff[o.idx] = lst
            for i_ in lst:
                self.ops[i_].has_dep = True
        seq = {e: 0 for e in engs}
        kcnt = {}
        for o in [self.ops[i] for e in engs for i in order[e]]:
            if o.dmas is not None:
                k = "d_" + o.key
                kcnt[k] = kcnt.get(k, 0) + 16 * len(o.dmas)
                o.sig = (k, kcnt[k])
                sem(k)
            elif o.has_dep:
                seq[o.eng] += 1
                o.sig = ("e_" + o.eng, seq[o.eng])
                sem("e_" + o.eng)
        block = stack.enter_context(nc.Block())
        deco = {"pe": block.tensor, "act": block.scalar, "dve": block.vector,
                "pool": block.gpsimd, "sp": block.sync}
        for en in engs:
            myops = [self.ops[i] for i in order[en]]
            if not myops:
                continue

            def body(e, myops=myops, en=en):
                waited = {}
                for o in myops:
                    need = {}
                    for d in eff[o.idx]:
                        s_, v = self.ops[d].sig
                        if need.get(s_, 0) < v:
                            need[s_] = v
                    for s_, v in need.items():
                        if waited.get(s_, 0) >= v:
                            continue
                        e.wait_ge(sems[s_], v)
                        waited[s_] = v
                    if o.dmas is not None:
                        for (dst, src) in o.dmas:
                            e.dma_start(out=dst, in_=src, **o.fn).then_inc(sems[o.sig[0]], 16)
                    elif o.fn is not None:
                        last = o.fn(e)
                        if o.sig is not None:
                            last.then_inc(sems[o.sig[0]], 1)

            deco[en](body)


def _nfree(ap):
    n = 1
    for d in ap.shape[1:]:
        n *= int(d)
    return n


def _mk(f, cost, tbl=None, scan=False):
    f.cost = cost
    f.tbl = tbl
    return f


def ACT(out, in_, func, **kw):
    return _mk(lambda e: e.activation(out=out, in_=in_, func=func, **kw), 0.22 + _nfree(out) / 1200.0, tbl=func)


def TT(out, in0, in1, op):
    return _mk(lambda e: e.tensor_tensor(out=out, in0=in0, in1=in1, op=op), 0.07 + _nfree(out) / 960.0)


def STT(out, in0, scalar, in1, op0, op1):
    return _mk(lambda e: e.scalar_tensor_tensor(out=out, in0=in0, scalar=scalar, in1=in1, op0=op0, op1=op1),
               0.07 + _nfree(out) / 960.0)


def TS(out, in0, s1, s2, op0, op1):
    return _mk(lambda e: e.tensor_scalar(out=out, in0=in0, scalar1=s1, scalar2=s2, op0=op0, op1=op1),
               0.07 + _nfree(out) / 960.0)


def TS1(out, in_, s, op):
    return _mk(lambda e: e.tensor_single_scalar(out=out, in_=in_, scalar=s, op=op), 0.07 + _nfree(out) / 960.0)


def CP(out, in_):
    return _mk(lambda e: e.tensor_copy(out=out, in_=in_), 0.07 + _nfree(out) / 960.0)


def PCP(out, in_):
    return _mk(lambda e: e.tensor_copy(out=out, in_=in_), 0.12 + _nfree(out) / 480.0)


def PSTT(out, in0, scalar, in1, op0, op1):
    return _mk(lambda e: e.scalar_tensor_tensor(out=out, in0=in0, scalar=scalar, in1=in1, op0=op0, op1=op1),
               0.12 + _nfree(out) / 480.0)


def MSET(ap, v):
    return _mk(lambda e: e.memset(ap, v), 0.07 + _nfree(ap) / 1900.0)


def MM(out, pairs):
    def f(e):
        n = len(pairs)
        last = None
        for i, (l, r) in enumerate(pairs):
            last = e.matmul(out, l, r, start=(i == 0), stop=(i == n - 1))
        return last
    nn = max(_nfree(out), 64)
    return _mk(f, len(pairs) * (nn / 2300.0 + 0.006))


def build(stage=99, debug=False):
    nc = bass.Bass("TRN2", target_bir_lowering=False)

    def din(name, shape):
        return nc.dram_tensor(name, list(shape), F32, kind="ExternalInput").ap()

    def dout(name, shape):
        return nc.dram_tensor(name, list(shape), F32, kind="ExternalOutput").ap()

    xp = din("xp", (SEQ, D))
    xs = din("xs", (NSAMP, D))
    sh = din("sh", (NSAMP, D))
    sc = din("sc", (NSAMP, 3, D))
    vecs = din("vecs", (NV, D))
    gvn = din("gvn", (1, D))
    fnm = din("fnm", (1, D))
    spb = din("spb", (1, D))
    spw = din("spw", (8, 128, 128))
    wr = din("wr", (16, 64, 64))
    wi = din("wi", (16, 64, 64))
    f1g = din("f1g", (D, DFF))
    f1u = din("f1u", (D, DFF))
    f1d = din("f1d", (DFF, D))
    win = din("win", (D, 6 * D))
    pa = din("pa", (D, D))
    pb = din("pb", (D, D))
    wo = din("wo", (D, D))
    f2g = din("f2g", (D, DFF))
    f2u = din("f2u", (D, DFF))
    f2d = din("f2d", (DFF, D))

    y_p = dout("y_p", (SEQ, D))
    y_s = dout("y_s", (NSAMP, D))
    h_p = dout("h_p", (1, D))
    c_p = dout("c_p", (3, D))
    h_s = dout("h_s", (NSAMP, D))
    c_s = dout("c_s", (NSAMP, 3, D))
    v_s = dout("v_s", (NSAMP, D))
    if debug:
        dbg = dout("dbg", (128, 8 * NTW))

    st_ = contextlib.ExitStack()
    with st_ as stack:
        def sb(name, shape, dt):
            return stack.enter_context(nc.sbuf_tensor(name, list(shape), dt))

        xT = sb("xT", (128, 8, NTW), F32)
        hT = sb("hT", (128, 8, NTW), BF16)
        arena = sb("arena", (128, 25856), BF16)
        wa = sb("wa", (128, 4, 8, 512), BF16)
        wbm = sb("wbm", (128, 8768), F32)
        stg = sb("stg", (128, 2, 1024), F32)
        vg = sb("vg", (128, 1024), F32)
        actS = sb("actS", (128, 2, 512), F32)
        rstd = sb("rstd", (128, 512), F32)
        ident = sb("ident", (128, 128), F32)
        maskf = sb("maskf", (128, 128), F32)
        onesb = sb("onesb", (128, 128), BF16)
        onesrow = sb("onesrow", (3, 128), BF16)
        WsT = sb("WsT", (128, 8, 128), BF16)
        wrbd = sb("wrbd", (128, 8, 128), BF16)
        wibd = sb("wibd", (128, 8, 128), BF16)
        vT = sb("vT", (128, 8, 16), F32)
        tv = sb("tv", (128, 8, 8), F32)
        nbuf = sb("nbuf", (128, 1024), F32)
        gvnb = nbuf
        fnb = nbuf
        stT = sb("stT", (128, 8, 64), F32)
        bhi = sb("bhi", (3, 1024), BF16)
        blo = sb("blo", (1, 1024), BF16)
        ws00 = sb("ws00", (16, 8), F32)
        Dg = sb("Dg", (16, 8, 16), BF16)
        ssv2 = sb("ssv2", (128, 2, 4), F32)
        ssq = sb("ssq", (128, 3, 9), F32)
        hc = sb("hc", (128, 8), F32)
        cv = sb("cv", (128, 8, 3), F32)
        fin = sb("fin", (128, 8, 36), F32)
        ps = stack.enter_context(nc.psum_tensor("ps", [128, 8, 512], F32))

        hid = arena[:, 0:22 * NTW].rearrange("p (f n) -> p f n", n=NTW)
        vtm = arena[:, 0:8192].rearrange("p (i n) -> p i n", n=1024)
        vtms = arena[:, 8192:9216]
        ybv = arena[:, 0:8 * NTW].rearrange("p (j n) -> p j n", n=NTW)
        ya = arena[:, 9216:9216 + 8 * NTW].rearrange("p (j n) -> p j n", n=NTW)
        mb_ = arena[:, 17536:17536 + 8 * NTW].rearrange("p (j n) -> p j n", n=NTW)
        wbb = wbm[:, 0:8448].bitcast(BF16)
        wb = wbb.rearrange("p (s f n) -> p s f n", s=3, f=22)
        XBW = 1056
        xb = wbm[:, 0:2 * XBW].rearrange("p (s n) -> p s n", n=XBW)
        Lsets = []
        xcbs = []
        for S_ in range(2):
            base = 2 * XBW + S_ * 3328
            Lt = wbm[:, base:base + 6 * 512].rearrange("p (s n) -> p s n", n=512)
            Lsets.append(tuple(Lt[:, i, :] for i in range(6)))
            xcbs.append(wbm[:, base + 3072:base + 3328].bitcast(BF16))
        Lsm = sb("Lsm", (128, 7, NSAMP), F32)
        Lsets.append(tuple(Lsm[:, i, :] for i in range(6)))
        xcbs.append(Lsm[:, 6, :].bitcast(BF16))

        if debug:
            try:
                print("SBUF remaining", nc.sbuf_bytes_remaining, "top", nc.sbuf_top, "base", nc.sbuf_base,
                      "part", nc.SBUF_PARTITION_SIZE_BYTES)
            except Exception as ex:
                print("sbuf introspection failed", ex)
        P = Prog()
        A = P.op
        actSf = actS[:].rearrange("p s n -> p (s n)")
        SLOT = [stg[:, 0, :], stg[:, 1, :], vg[:, :], actSf]
        SLOTR = [[("stg", 0)], [("stg", 1)], [("vg", 0), ("vg", 1)], [("actS", 0), ("actS", 1)]]

        wa_ctr = [0]

        def wa_next():
            s = wa_ctr[0]
            wa_ctr[0] = (s + 1) % 4
            return s

        wb_ctr = [0]

        def wb_next():
            s = wb_ctr[0]
            wb_ctr[0] = (s + 1) % 3
            return s

        as_ctr = [0]

        def as_next():
            s = as_ctr[0]
            as_ctr[0] = (s + 1) % 2
            return s

        def load_wa(src3, c0, cw):
            s = wa_next()
            P.dma("pool", [(wa[:, s, :, 0:cw], src3[:, :, c0:c0 + cw])], w=[("wa", s)], key="wa%d" % s)
            return s

        def kview(w):
            return w.rearrange("(k p) n -> p k n", p=128)

        A("pool", MSET(ident[:], 1.0), w=[("c", "ident")])
        A("pool", lambda e: e.affine_select(out=ident[:], in_=ident[:], pattern=[[-1, 128]],
                                            compare_op=ALU.is_equal, fill=0.0, base=0, channel_multiplier=1),
          r=[("c", "ident")], w=[("c", "ident")])
        A("pool", MSET(maskf[:], 1.0), w=[("c", "maskf")])
        A("pool", lambda e: e.affine_select(out=maskf[:], in_=maskf[:], pattern=[[1, 128]],
                                            compare_op=ALU.is_ge, fill=0.0, base=0, channel_multiplier=-1),
          r=[("c", "maskf")], w=[("c", "maskf")])
        A("pool", MSET(onesb[:], 1.0 / 1024.0), w=[("c", "onesb")])
        A("pool", MSET(onesrow[:], 1.0), w=[("c", "onesrow")])
        A("pool", MSET(wrbd[:], 0.0), w=[("c", "wrbd")])
        A("pool", MSET(wibd[:], 0.0), w=[("c", "wibd")])
        A("pool", MSET(hc[:], 0.0), w=[("hc",)])
        A("pool", MSET(ssq[:], 1.0), w=[("ssq", i) for i in range(9)] + [("ssq", "n"), ("ssq", "r")])
        for (wsrc, wdst, nm) in ((wr, wrbd, "wrbd"), (wi, wibd, "wibd")):
            v = wsrc.rearrange("(j e) i o -> e i j o", e=2)
            P.dma("pool", [(wdst[0:64, :, 0:64], v[0]), (wdst[64:128, :, 64:128], v[1])],
                  r=[], w=[("c", nm)], key=nm)
        P.dma("sp", [(stg[0:NV, 0, :], vecs[:, :])], w=[("stg", 0)], key="stg0")
        bsrc = vg[0:1, :]
        bhf = stg[0:1, 1, :]
        P.dma("sp", [(ws00[:, :], bass.AP(spw.tensor, 0, [[0, 16], [16384, 8]]))], w=[("c", "ws00")],
              key="ws00", allow_slow_non_contiguous=True)
        b = P.nb()

        def tr_vecs(e, b=b):
            last = None
            for j in range(8):
                last = e.transpose(ps[:, b, j * 16:j * 16 + NV], stg[0:NV, 0, j * 128:(j + 1) * 128],
                                   ident[0:NV, 0:NV])
            return last
        A("pe", _mk(tr_vecs, 1.0), r=[("stg", 0), ("c", "ident")], w=[("ps", b)])
        A("act", ACT(vT[:, :, 0:NV], ps[:, b, 0:128].rearrange("p (j n) -> p j n", n=16)[:, :, 0:NV], AF.Copy),
          r=[("ps", b)], w=[("vT",)])
        def late_setup():
            lam = vT[:, :, 10]
            t0, t1, t2, t3, t4 = (tv[:, :, i] for i in range(5))
            A("dve", TS1(t0, lam, -1.0, ALU.mult), r=[("vT",)], w=[("tv",)])
            A("dve", TT(t0, t0, lam, ALU.max), r=[("vT",), ("tv",)], w=[("tv",)])
            A("act", ACT(t0, t0, AF.Exp, scale=-1.0), r=[("tv",)], w=[("tv",)])
            A("dve", TS1(t1, t0, 2.0, ALU.add), r=[("tv",)], w=[("tv",)])
            A("dve", lambda e: e.reciprocal(out=t1, in_=t1), r=[("tv",)], w=[("tv",)])
            A("dve", TT(t1, t0, t1, ALU.mult), r=[("tv",)], w=[("tv",)])
            A("dve", TT(t2, t1, t1, ALU.mult), r=[("tv",)], w=[("tv",)])
            A("dve", TS(t3, t2, 1.0 / 9.0, 1.0 / 7.0, ALU.mult, ALU.add), r=[("tv",)], w=[("tv",)])
            for cst in (1.0 / 5.0, 1.0 / 3.0, 1.0):
                A("dve", TT(t3, t3, t2, ALU.mult), r=[("tv",)], w=[("tv",)])
                A("dve", TS1(t3, t3, cst, ALU.add), r=[("tv",)], w=[("tv",)])
            A("dve", STT(t3, t1, 2.0, t3, ALU.mult, ALU.mult), r=[("tv",)], w=[("tv",)])
            A("dve", TS(t4, lam, -1.0, 0.0, ALU.mult, ALU.max), r=[("tv",), ("vT",)], w=[("tv",)])
            A("dve", TT(t3, t3, t4, ALU.add), r=[("tv",)], w=[("tv",)])
            A("dve", TS1(vT[:, :, 11], t3, -8.0, ALU.mult), r=[("tv",)], w=[("vT",)])
            A("dve", TS1(vT[:, :, 12], t3, -4.0, ALU.mult), r=[("tv",)], w=[("vT",)])
            A("dve", TS1(vT[:, :, 13], t3, -2.0, ALU.mult), r=[("tv",)], w=[("vT",)])
            A("dve", TS1(vT[:, :, 14], vT[:, :, 8], 0.5, ALU.mult), r=[("vT",)], w=[("vT",)])
            A("dve", TS1(vT[:, :, 15], vT[:, :, 9], 0.5, ALU.mult), r=[("vT",)], w=[("vT",)])
            P.dma("sp", [(bsrc, spb[:, :])], w=[("vg", 0), ("vg", 1)], key="bsrc")
            A("dve", CP(bhi[0:1, :], bsrc), r=[("vg", 0), ("vg", 1)], w=[("c", "bhi")])
            A("dve", CP(bhf, bhi[0:1, :]), r=[("c", "bhi")], w=[("stg", 1)])
            A("dve", TT(bsrc, bsrc, bhf, ALU.subtract), r=[("vg", 0), ("vg", 1), ("stg", 1)], w=[("vg", 0), ("vg", 1)])
            A("dve", CP(blo[:], bsrc), r=[("vg", 0), ("vg", 1)], w=[("c", "blo")])
            P.dma("sp", [(bhi[1:2, :], blo[:])], r=[("c", "blo")], w=[("c", "bhi1")], key="bmid")
            A("dve", CP(bhf, blo[:]), r=[("c", "blo")], w=[("stg", 1)])
            A("dve", TT(bsrc, bsrc, bhf, ALU.subtract), r=[("vg", 0), ("vg", 1), ("stg", 1)], w=[("vg", 0), ("vg", 1)])
            A("dve", CP(blo[:], bsrc), r=[("vg", 0), ("vg", 1), ("c", "bhi1")], w=[("c", "blo")])
            P.dma("sp", [(bhi[2:3, :], blo[:])], r=[("c", "blo")], w=[("c", "bhi2")], key="blo2")
            for g in range(8):
                A("dve", TS1(Dg[:, g, :], ident[0:16, 0:16], ws00[:, g:g + 1], ALU.mult),
                  r=[("c", "ident"), ("c", "ws00")], w=[("c", "Dg")])
            P.dma("sp", [(stg[:, 1, :].rearrange("p (g s) -> p g s", g=8), spw.rearrange("g t s -> t g s"))],
                  w=[("stg", 1)], key="stg1")
            for hh in range(2):
                b = P.nb()

                def tr_sp(e, b=b, hh=hh):
                    last = None
                    for q in range(4):
                        g = hh * 4 + q
                        last = e.transpose(ps[:, b, q * 128:(q + 1) * 128], stg[:, 1, g * 128:(g + 1) * 128], ident[:])
                    return last
                A("pe", _mk(tr_sp, 1.6), r=[("stg", 1), ("c", "ident")], w=[("ps", b)])
                for q in range(4):
                    A("dve", TT(WsT[:, hh * 4 + q, :], ps[:, b, q * 128:(q + 1) * 128], maskf[:], ALU.mult),
                      r=[("ps", b), ("c", "maskf")], w=[("c", "WsT")])
            P.dma("sp", [(stg[0:16, 0, :], sh[:, :]), (stg[16:64, 0, :], sc.rearrange("t k d -> (t k) d"))],
                  r=[], w=[("stg", 0)], key="stg0")
            b = P.nb()

            def tr_st(e, b=b):
                last = None
                for j in range(8):
                    last = e.transpose(ps[:, b, j * 64:(j + 1) * 64], stg[0:64, 0, j * 128:(j + 1) * 128],
                                       ident[0:64, 0:64])
                return last
            A("pe", _mk(tr_st, 1.6), r=[("stg", 0), ("c", "ident")], w=[("ps", b)])
            A("act", ACT(stT[:], ps[:, b, :].rearrange("p (j n) -> p j n", n=64), AF.Copy), r=[("ps", b)], w=[("stT",)])
            P.dma("sp", [(c_s[:, 0:2, :], sc[:, 1:3, :])], r=[], w=[("out", "cs01")], key="cs01")

        out_res = [("out", "cs01")]

        def hTr(t):
            return [("hT", j, t) for j in range(8)]

        def xTr(t):
            return [("xT", j, t) for j in range(8)]

        def rmsnorm(TTl, col):
            for t, (off, n) in enumerate(TTl):
                A("act", ACT(hT[:, :, off:off + n], xT[:, :, off:off + n], AF.Square), r=xTr(t), w=hTr(t))
                b = P.nb()
                A("pe", MM(ps[:, b, :n], [(onesb[:], hT[:, j, off:off + n]) for j in range(8)]),
                  r=hTr(t) + [("c", "onesb")], w=[("ps", b)])
                A("act", ACT(rstd[:, :n], ps[:, b, :n], AF.Sqrt, bias=epsb[:, :], scale=1.0),
                  r=[("ps", b), ("c", "epsb")], w=[("rstd",)])
                A("dve", _mk(lambda e, n=n: e.reciprocal(out=rstd[:, :n], in_=rstd[:, :n]), 0.07 + n / 960.0),
                  r=[("rstd",)], w=[("rstd",)])
                for j in range(8):
                    A("dve", STT(hT[:, j, off:off + n], xT[:, j, off:off + n], vT[:, j, col:col + 1],
                                 rstd[:, :n], ALU.mult, ALU.mult),
                      r=[("xT", j, t), ("rstd",), ("vT",)], w=[("hT", j, t)])

        def gelu_to(dst, src, n_res_r, n_res_w):
            if NATIVE_GELU:
                A("act", ACT(dst, src, AF.Gelu_apprx_tanh), r=n_res_r, w=n_res_w)
                return 1.0
            A("act", ACT(dst, src, AF.Square), r=n_res_r, w=n_res_w)
            A("act", ACT(dst, dst, AF.Identity, scale=GC, bias=gkb[0:dst.shape[0], :]), r=n_res_w + [("c", "gkb")], w=n_res_w)
            A("dve", TT(dst, dst, src, ALU.mult), r=n_res_r + n_res_w, w=n_res_w)
            A("act", ACT(dst, dst, AF.Tanh), r=n_res_w, w=n_res_w)
            A("dve", STT(dst, dst, 1.0, src, ALU.add, ALU.mult), r=n_res_r + n_res_w, w=n_res_w)
            return 2.0

        def ffn(TTl, wg, wu, wd):
            wgv, wuv = kview(wg), kview(wu)
            groups = [(0, 512), (512, 512), (1024, 512), (1536, 512), (2048, 512), (2560, 192)]
            for (c0, cw) in groups:
                sg_ = load_wa(wgv, c0, cw)
                su_ = load_wa(wuv, c0, cw)
                for fl in range((cw + 127) // 128):
                    f = c0 // 128 + fl
                    fw = min(128, cw - fl * 128)
                    for t, (off, n) in enumerate(TTl):
                        bg, bu = P.nb(), P.nb()
                        A("pe", MM(ps[:fw, bg, :n], [(wa[:, sg_, k, fl * 128:fl * 128 + fw], hT[:, k, off:off + n])
                                                     for k in range(8)]), r=hTr(t) + [("wa", sg_)], w=[("ps", bg)])
                        A("pe", MM(ps[:fw, bu, :n], [(wa[:, su_, k, fl * 128:fl * 128 + fw], hT[:, k, off:off + n])
                                                     for k in range(8)]), r=hTr(t) + [("wa", su_)], w=[("ps", bu)])
                        sl = as_next()
                        A("act", ACT(actS[:fw, sl, :n], ps[:fw, bg, :n], AF.Silu), r=[("ps", bg)], w=[("actS", sl)])
                        A("dve", TT(hid[:fw, f, off:off + n], actS[:fw, sl, :n], ps[:fw, bu, :n], ALU.mult),
                          r=[("actS", sl), ("ps", bu)], w=[("hid", f, t)])
            wd3 = wd[0:2688, :].rearrange("(f p) n -> p f n", p=128)
            passes = [[0], list(range(1, len(TTl)))]
            for tl in passes:
                for mp in range(4):
                    s = wb_next()
                    P.dma("pool", [(wb[:, s, 0:21, :], wd3[:, :, mp * 256:(mp + 1) * 256]),
                                   (wb[0:64, s, 21, :], wd[2688:2752, mp * 256:(mp + 1) * 256])],
                          w=[("wb", s)], key="wb%d" % s)
                    for ml in range(2):
                        m = mp * 2 + ml
                        for t in tl:
                            off, n = TTl[t]
                            b = P.nb()
                            pairs = [(wb[:, s, f, ml * 128:(ml + 1) * 128], hid[:, f, off:off + n]) for f in range(21)]
                            pairs.append((wb[0:64, s, 21, ml * 128:(ml + 1) * 128], hid[0:64, 21, off:off + n]))
                            A("pe", MM(ps[:, b, :n], pairs), r=[("wb", s)] + [("hid", f, t) for f in range(22)],
                              w=[("ps", b)])
                            A("dve", STT(xT[:, m, off:off + n], ps[:, b, :n], 0.5, xT[:, m, off:off + n],
                                         ALU.mult, ALU.add), r=[("ps", b), ("xT", m, t)], w=[("xT", m, t)])

        gkb = sb("gkb", (128, 1), F32)
        A("pool", MSET(gkb[:], GK), w=[("c", "gkb")])

        def mixer(hf, TTl):
            winv = kview(win)
            pav, pbv, wov = kview(pa), kview(pb), kview(wo)
            P.dma("sp", [(nbuf[:, :], bass.AP(gvn.tensor, 0, [[0, 128], [1, 1024]]))], w=[("c", "nbuf")], key="nbuf")
            sv0 = load_wa(winv, 1024, 512)
            sv1 = load_wa(winv, 1536, 512)
            tiles = list(range(8)) + ([8] if hf == 0 else [])
            for i in tiles:
                npt = 128 if i < 8 else NSAMP
                off = i * 128 if i < 8 else HALF
                t = off // 512
                b0, b1 = P.nb(), P.nb()

                def fv(e, b0=b0, b1=b1, off=off, npt=npt):
                    last = None
                    for k in range(8):
                        for (bb, sv) in ((b0, sv0), (b1, sv1)):
                            last = e.matmul(ps[:npt, bb, :], hT[:, k, off:off + npt], wa[:, sv, k, :],
                                            start=(k == 0), stop=(k == 7))
                    return last
                A("pe", _mk(fv, 4.2), r=hTr(t) + [("wa", sv0), ("wa", sv1)], w=[("ps", b0), ("ps", b1)])
                buf = SLOT[2 + (i % 2)]
                bufr = SLOTR[2 + (i % 2)]
                gs = 1.0
                for c, bb in enumerate((b0, b1)):
                    gs = gelu_to(buf[:npt, c * 512:(c + 1) * 512], ps[:npt, bb, :], [("ps", bb)], [bufr[c]])
                vdst = vtm[:, i, :] if i < 8 else vtms[0:NSAMP, :]
                A("act", ACT(mb_[:npt, 0, 0:1024], buf[:npt, :], AF.Square, accum_out=ssq[:npt, 0, i:i + 1]),
                  r=bufr, w=[("m", 0, 0), ("m", 0, 1), ("m", 0, 2), ("ssq", i)])
                A("dve", TT(vdst, buf[:npt, :], gvnb[:npt, :], ALU.mult), r=bufr + [("c", "nbuf")], w=[("vtm", i)])
            nt_ = len(tiles)
            A("dve", TS(ssq[:, 1, 0:nt_], ssq[:, 0, 0:nt_], 1.0 / 1024.0, EPS * gs * gs, ALU.mult, ALU.add),
              r=[("ssq", i) for i in tiles], w=[("ssq", "n")])
            A("act", ACT(ssq[:, 2, 0:nt_], ssq[:, 1, 0:nt_], AF.Sqrt), r=[("ssq", "n")], w=[("ssq", "r")])
            A("dve", lambda e, nt_=nt_: e.reciprocal(out=ssq[:, 2, 0:nt_], in_=ssq[:, 2, 0:nt_]), r=[("ssq", "r")],
              w=[("ssq", "r")])
            for i in tiles:
                if i < 8:
                    A("act", ACT(vtm[:, i, :], vtm[:, i, :], AF.Copy, scale=ssq[:, 2, i:i + 1]),
                      r=[("ssq", "r"), ("vtm", i)], w=[("vtm", i)])
                else:
                    buf = SLOT[2 + (i % 2)]
                    bufr = SLOTR[2 + (i % 2)]
                    A("act", ACT(vtms[0:NSAMP, :], vtms[0:NSAMP, :], AF.Copy, scale=ssq[0:NSAMP, 2, i:i + 1]),
                      r=[("ssq", "r"), ("vtm", i)], w=[("vtm", i)])
                    A("dve", STT(stg[0:NSAMP, 1, :], buf[0:NSAMP, :], ssq[0:NSAMP, 2, i:i + 1], gvnb[0:NSAMP, :],
                                 ALU.mult, ALU.mult), r=bufr + [("ssq", "r"), ("c", "nbuf")], w=[("stg", 1)])
                    P.dma("sp", [(v_s[:, :], stg[0:NSAMP, 1, :])], r=[("stg", 1)], w=[("out", "vs")], key="so1")
                    out_res.append(("out", "vs"))
            su = None
            for g in range(8):
                if g % 4 == 0:
                    su = load_wa(winv, (g // 4) * 512, 512)
                gl = g % 4
                for t, (off, n) in enumerate(TTl):
                    bu, bs = P.nb(), P.nb()
                    A("pe", MM(ps[:, bu, :n], [(wa[:, su, k, gl * 128:(gl + 1) * 128], hT[:, k, off:off + n])
                                               for k in range(8)]), r=hTr(t) + [("wa", su)], w=[("ps", bu)])
                    if n == 512:
                        def fs(e, bs=bs, g=g, off=off):
                            last = None
                            for c in range(4):
                                i = off // 128 + c
                                e.matmul(ps[:, bs, c * 128:(c + 1) * 128], onesrow[0:3, :],
                                         bhi[0:3, g * 128:(g + 1) * 128], start=True, stop=False)
                                last = e.matmul(ps[:, bs, c * 128:(c + 1) * 128], vtm[:, i, g * 128:(g + 1) * 128],
                                                WsT[:, g, :], start=False, stop=True)
                            return last
                        rr = [("vtm", off // 128 + c) for c in range(4)]
                    else:
                        def fs(e, bs=bs, g=g):
                            e.matmul(ps[:, bs, :NSAMP], onesrow[0:3, :],
                                     bhi[0:3, g * 128:g * 128 + 1].to_broadcast([3, NSAMP]), start=True, stop=False)
                            return e.matmul(ps[:, bs, :NSAMP], vtms[0:NSAMP, g * 128:(g + 1) * 128], Dg[:, g, :],
                                            start=False, stop=True)
                        rr = [("vtm", 8)]
                    A("pe", _mk(fs, 0.6), r=rr + [("c", "WsT"), ("c", "bhi"), ("c", "bhi1"), ("c", "bhi2"), ("c", "Dg"), ("c", "onesrow")],
                      w=[("ps", bs)])
                    sl = as_next()
                    gs = gelu_to(actS[:, sl, :n], ps[:, bu, :n], [("ps", bu)], [("actS", sl)])
                    A("dve", STT(ya[:, g, off:off + n], actS[:, sl, :n], 1.0 / gs, ps[:, bs, :n], ALU.mult, ALU.mult),
                      r=[("actS", sl), ("ps", bs)], w=[("ya", g, t)])
            P.fence("ARENA")
            def c_chain(j, spa, sga):
                jl = j % 4
                for t, (off, n) in enumerate(TTl):
                    bp, bg = P.nb(), P.nb()
                    A("pe", MM(ps[:, bp, :n], [(wa[:, spa, k, jl * 128:(jl + 1) * 128], ya[:, k, off:off + n])
                                               for k in range(8)]),
                      r=[("ya", k, t) for k in range(8)] + [("wa", spa)], w=[("ps", bp)])
                    A("pe", MM(ps[:, bg, :n], [(wa[:, sga, k, jl * 128:(jl + 1) * 128], hT[:, k, off:off + n])
                                               for k in range(8)]), r=hTr(t) + [("wa", sga)], w=[("ps", bg)])
                    yield
                    sl = as_next()
                    A("act", ACT(actS[:, sl, :n], ps[:, bg, :n], AF.Tanh, scale=0.5), r=[("ps", bg)], w=[("actS", sl)])
                    yield
                    A("dve", STT(mb_[:, j, off:off + n], actS[:, sl, :n], 1.0, ps[:, bp, :n], ALU.add, ALU.mult),
                      r=[("actS", sl), ("ps", bp)], w=[("m", j, t)])
                    yield
                    yield
                    yield
            def lru_chain(j, t, off, n, S, sx, sgb):
                jl = j % 4
                jb = j % 2
                L_xc, L_tr, L_ti, L_e, L_th, L_g = Lsets[S]
                xcb_ = xcbs[S]
                ydst = ybv

                def Ln(nm):
                    return ("L", S, nm)
                samp = (n == NSAMP)
                bx, bgt = P.nb(), P.nb()
                A("pe", MM(ps[:, bx, :n], [(wa[:, sx, k, jl * 128:(jl + 1) * 128], hT[:, k, off:off + n])
                                           for k in range(8)]), r=hTr(t) + [("wa", sx)], w=[("ps", bx)])
                A("pe", MM(ps[:, bgt, :n], [(wa[:, sgb, k, jl * 128:(jl + 1) * 128], hT[:, k, off:off + n])
                                            for k in range(8)]), r=hTr(t) + [("wa", sgb)], w=[("ps", bgt)])
                yield
                A("act", ACT(xb[:, jb, 3 + off:3 + off + n], ps[:, bx, :n], AF.Copy),
                  r=[("ps", bx)], w=[("xb", jb, t)])
                yield
                gs = gelu_to(L_g[:, :n], ps[:, bgt, :n], [("ps", bgt)], [Ln("g")])
                yield
                A("dve", TS(L_xc[:, :n], xb[:, jb, 3 + off:3 + off + n], vT[:, j, 6:7], vT[:, j, 7:8],
                            ALU.mult, ALU.add), r=[("xb", jb, t), ("vT",)], w=[Ln("xc")])
                yield
                for k in range(3):
                    if samp:
                        src = stT[:, j, 16 + k:64:3]
                        rr = [("stT",)]
                    else:
                        src = xb[:, jb, off + k:off + k + n]
                        rr = [("xb", jb, t)] + ([("xbh", jb)] if t == 0 else [("xb", jb, t - 1)])
                    A("dve", STT(L_xc[:, :n], src, vT[:, j, 3 + k:4 + k], L_xc[:, :n], ALU.mult, ALU.add),
                      r=rr + [Ln("xc"), ("vT",)], w=[Ln("xc")])
                    yield
                A("act", ACT(xcb_[:, :n], L_xc[:, :n], AF.Copy), r=[Ln("xc")], w=[Ln("xcb")])
                yield
                br, bi = P.nb(), P.nb()
                A("pe", MM(ps[:, br, :n], [(wrbd[:, j, :], xcb_[:, :n])]), r=[Ln("xcb"), ("c", "wrbd")],
                  w=[("ps", br)])
                A("pe", MM(ps[:, bi, :n], [(wibd[:, j, :], xcb_[:, :n])]), r=[Ln("xcb"), ("c", "wibd")],
                  w=[("ps", bi)])
                yield
                A("act", ACT(L_tr[:, :n], ps[:, br, :n], AF.Tanh, scale=0.5, bias=vT[:, j, 14:15]),
                  r=[("ps", br), ("vT",)], w=[Ln("tr")])
                yield
                A("act", ACT(L_ti[:, :n], ps[:, bi, :n], AF.Tanh, scale=0.5, bias=vT[:, j, 15:16]),
                  r=[("ps", bi), ("vT",)], w=[Ln("ti")])
                yield
                A("act", ACT(L_e[:, :n], L_tr[:, :n], AF.Exp, scale=vT[:, j, 12:13], bias=vT[:, j, 12:13]),
                  r=[Ln("tr"), ("vT",)], w=[Ln("e")])
                yield
                A("act", ACT(L_th[:, :n], L_tr[:, :n], AF.Tanh, scale=vT[:, j, 13:14], bias=vT[:, j, 13:14]),
                  r=[Ln("tr"), ("vT",)], w=[Ln("th")])
                yield
                A("dve", STT(L_ti[:, :n], L_ti[:, :n], 1.0, L_xc[:, :n], ALU.add, ALU.mult),
                  r=[Ln("ti"), Ln("xc")], w=[Ln("ti")])
                yield
                A("dve", STT(L_e[:, :n], L_e[:, :n], 1.0, L_th[:, :n], ALU.add, ALU.mult),
                  r=[Ln("e"), Ln("th")], w=[Ln("e")])
                yield
                A("dve", STT(L_th[:, :n], L_e[:, :n], 2.0, L_e[:, :n], ALU.add, ALU.mult),
                  r=[Ln("e"), Ln("th")], w=[Ln("th")])
                yield
                A("act", ACT(L_th[:, :n], L_th[:, :n], AF.Sqrt, scale=-0.25), r=[Ln("th")], w=[Ln("th")])
                yield
                A("dve", TS1(L_e[:, :n], L_e[:, :n], 1.0, ALU.add), r=[Ln("e")], w=[Ln("e")])
                yield
                A("dve", TT(L_ti[:, :n], L_ti[:, :n], L_th[:, :n], ALU.mult), r=[Ln("ti"), Ln("th")], w=[Ln("ti")])
                yield
                if not samp:
                    if t == 0:
                        init = 0.0 if hf == 0 else hc[:, j:j + 1]
                        ir = [("hc",)]
                    else:
                        init = Lsets[0][1][:, 511:512]
                        ir = [("L", 0, "tr")]
                    A("dve", _mk(lambda e, n=n, init=init: e.tensor_tensor_scan(
                        out=L_tr[:, :n], data0=L_e[:, :n], data1=L_ti[:, :n], initial=init,
                        op0=ALU.mult, op1=ALU.add), 0.07 + 2.2 * n / 960.0),
                      r=[Ln("e"), Ln("ti"), Ln("tr")] + ir, w=[Ln("tr")])
                    yield
                    if t == 1 and hf == 0:
                        A("act", ACT(hc[:, j:j + 1], L_tr[:, n - 1:n], AF.Copy), r=[Ln("tr")], w=[("hc",)])
                        A("act", ACT(cv[:, j, :], xb[:, jb, 3 + HALF - 3:3 + HALF], AF.Copy),
                          r=[("xb", jb, t)], w=[("cv", j)])
                    if t == 1 and hf == 1:
                        A("act", ACT(fin[:, j, 0:1], L_tr[:, n - 1:n], AF.Copy), r=[Ln("tr")], w=[("fin", j)])
                        A("act", ACT(fin[:, j, 1:4], xb[:, jb, 3 + HALF - 3:3 + HALF], AF.Copy),
                          r=[("xb", jb, t)], w=[("fin", j)])
                else:
                    A("dve", TT(L_tr[:, :n], L_e[:, :n], stT[:, j, 0:NSAMP], ALU.mult),
                      r=[Ln("e"), ("stT",), Ln("tr")], w=[Ln("tr")])
                    A("dve", TT(L_tr[:, :n], L_tr[:, :n], L_ti[:, :n], ALU.add), r=[Ln("tr"), Ln("ti")],
                      w=[Ln("tr")])
                    A("act", ACT(fin[:, j, 4:20], L_tr[:, :n], AF.Copy), r=[Ln("tr")], w=[("fin", j)])
                    A("act", ACT(fin[:, j, 20:36], xb[:, jb, 3 + HALF:3 + HALF + NSAMP], AF.Copy),
                      r=[("xb", jb, t)], w=[("fin", j)])
                yield
                A("dve", STT(ydst[:, j, off:off + n], L_g[:, :n], 1.0 / gs, L_tr[:, :n], ALU.mult, ALU.mult),
                  r=[Ln("g"), Ln("tr")], w=[("yb", j, t)])
                yield

            def run_interleaved(gens):
                gens = list(gens)
                while gens:
                    for g_ in list(gens):
                        try:
                            next(g_)
                        except StopIteration:
                            gens.remove(g_)

            for j in range(8):
                if j % 4 == 0:
                    sx = load_wa(winv, 2048 + (j // 4) * 512, 512)
                    sgb = load_wa(winv, 3072 + (j // 4) * 512, 512)
                    spa = load_wa(pav, (j // 4) * 512, 512)
                    sga = load_wa(winv, 4096 + (j // 4) * 512, 512)
                jb = j % 2
                if hf == 0:
                    A("dve", MSET(xb[:, jb, 0:3], 0.0), w=[("xbh", jb)])
                else:
                    A("dve", CP(xb[:, jb, 0:3], cv[:, j, :]), r=[("cv", j)], w=[("xbh", jb)])
                gl_ = [lru_chain(j, 0, 0, 512, 0, sx, sgb), lru_chain(j, 1, 512, 512, 1, sx, sgb)]
                if hf == 0:
                    gl_.append(lru_chain(j, 2, HALF, NSAMP, 2, sx, sgb))
                gl_.append(c_chain(j, spa, sga))
                run_interleaved(gl_)
            c0_, c1_ = (4, 36) if hf == 0 else (0, 4)
            ncol = c1_ - c0_
            b0, b1 = P.nb(), P.nb()

            def tr_fin(e, b0=b0, b1=b1, c0_=c0_, c1_=c1_, ncol=ncol):
                last = None
                for j in range(8):
                    bb = b0 if j < 4 else b1
                    last = e.transpose(ps[:ncol, bb, (j % 4) * 128:(j % 4 + 1) * 128], fin[:, j, c0_:c1_], ident[:])
                return last
            A("pe", _mk(tr_fin, 1.0), r=[("fin", j) for j in range(8)] + [("c", "ident")], w=[("ps", b0), ("ps", b1)])
            A("act", ACT(stg[:ncol, 0, 0:512], ps[:ncol, b0, :], AF.Copy), r=[("ps", b0)], w=[("stg", 0)])
            A("act", ACT(stg[:ncol, 0, 512:1024], ps[:ncol, b1, :], AF.Copy), r=[("ps", b1), ("stg", 0)],
              w=[("stg", 0)])
            if hf == 0:
                P.dma("sp", [(h_s[:, :], stg[0:16, 0, :]), (c_s[:, 2, :], stg[16:32, 0, :])], r=[("stg", 0)],
                      w=[("out", "hs")], key="so0")
                out_res.append(("out", "hs"))
            else:
                P.dma("sp", [(h_p[:, :], stg[0:1, 0, :]), (c_p[:, :], stg[1:4, 0, :])], r=[("stg", 0)],
                      w=[("out", "hp")], key="so0")
                out_res.append(("out", "hp"))
            for j in range(8):
                if j % 4 == 0:
                    spb_ = load_wa(pbv, (j // 4) * 512, 512)
                    sgb2 = load_wa(winv, 5120 + (j // 4) * 512, 512)
                jl = j % 4
                for t, (off, n) in enumerate(TTl):
                    bp, bg = P.nb(), P.nb()
                    A("pe", MM(ps[:, bp, :n], [(wa[:, spb_, k, jl * 128:(jl + 1) * 128], ybv[:, k, off:off + n])
                                               for k in range(8)]),
                      r=[("yb", k, t) for k in range(8)] + [("wa", spb_)], w=[("ps", bp)])
                    A("pe", MM(ps[:, bg, :n], [(wa[:, sgb2, k, jl * 128:(jl + 1) * 128], hT[:, k, off:off + n])
                                               for k in range(8)]), r=hTr(t) + [("wa", sgb2)], w=[("ps", bg)])
                    sl = as_next()
                    A("act", ACT(actS[:, sl, :n], ps[:, bg, :n], AF.Tanh, scale=0.5), r=[("ps", bg)], w=[("actS", sl)])
                    A("dve", STT(actS[:, sl, :n], actS[:, sl, :n], 1.0, ps[:, bp, :n], ALU.add, ALU.mult),
                      r=[("actS", sl), ("ps", bp)], w=[("actS", sl)])
                    A("dve", TT(mb_[:, j, off:off + n], actS[:, sl, :n], mb_[:, j, off:off + n], ALU.add),
                      r=[("actS", sl), ("m", j, t)], w=[("m", j, t)])
            so0 = load_wa(wov, 0, 512)
            so1 = load_wa(wov, 512, 512)
            for t, (off, n) in enumerate(TTl):
                for j in range(8):
                    so = so0 if j < 4 else so1
                    jl = j % 4
                    b = P.nb()
                    A("pe", MM(ps[:, b, :n], [(wa[:, so, k, jl * 128:(jl + 1) * 128], mb_[:, k, off:off + n])
                                              for k in range(8)]),
                      r=[("m", k, t) for k in range(8)] + [("wa", so)], w=[("ps", b)])
                    A("dve", STT(xT[:, j, off:off + n], ps[:, b, :n], 0.5, xT[:, j, off:off + n], ALU.mult, ALU.add),
                      r=[("ps", b), ("xT", j, t)], w=[("xT", j, t)])

        oneb = sb("oneb", (128, 1), F32)
        A("pool", MSET(oneb[:], 1.0), w=[("c", "oneb")])
        epsb = sb("epsb", (128, 1), F32)
        A("pool", MSET(epsb[:], EPS), w=[("c", "epsb")])

        for hf in range(2):
            TTl = [(0, 512), (512, 512)] + ([(HALF, NSAMP)] if hf == 0 else [])
            tiles = list(range(8)) + ([8] if hf == 0 else [])
            for i in tiles:
                npt = 128 if i < 8 else NSAMP
                off = i * 128 if i < 8 else HALF
                t = off // 512
                s = (i % 4) if hf == 0 else 2 + (i % 2)
                src = xp[hf * HALF + i * 128:hf * HALF + (i + 1) * 128, :] if i < 8 else xs[:, :]
                P.dma("sp", [(SLOT[s][:npt, :], src)], w=SLOTR[s], key="stg%d" % s)
                for hh in range(2):
                    b = P.nb()

                    def tr_x(e, b=b, hh=hh, s=s, npt=npt):
                        last = None
                        for q in range(4):
                            j = hh * 4 + q
                            last = e.transpose(ps[:, b, q * npt:(q + 1) * npt], SLOT[s][:npt, j * 128:(j + 1) * 128],
                                               ident[:npt, :npt])
                        return last
                    A("pe", _mk(tr_x, 1.2), r=SLOTR[s] + [("c", "ident")], w=[("ps", b)])
                    A("act" if hh == 0 else "dve",
                      (ACT(xT[:, hh * 4:(hh + 1) * 4, off:off + npt],
                           ps[:, b, 0:4 * npt].rearrange("p (q n) -> p q n", q=4), AF.Copy) if hh == 0 else
                       CP(xT[:, hh * 4:(hh + 1) * 4, off:off + npt],
                          ps[:, b, 0:4 * npt].rearrange("p (q n) -> p q n", q=4))),
                      r=[("ps", b)], w=[("xT", j, t) for j in range(hh * 4, hh * 4 + 4)])
            if stage >= 1:
                rmsnorm(TTl, 0)
            if hf == 0:
                late_setup()
            if stage >= 1:
                ffn(TTl, f1g, f1u, f1d)
            if stage >= 2:
                P.fence("ARENA")
                P.fence("WBMEM")
                rmsnorm(TTl, 1)
                mixer(hf, TTl)
                P.fence("ARENA")
                P.fence("WBMEM")
            if stage >= 3:
                rmsnorm(TTl, 2)
                ffn(TTl, f2g, f2u, f2d)
            if debug and hf == 0:
                P.dma("sp", [(dbg[:, :], xT[:].rearrange("p j n -> p (j n)"))], r=xTr(0) + xTr(1) + xTr(2),
                      w=[("out", "dbg")], key="dbg")
                out_res.append(("out", "dbg"))
            P.dma("sp", [(nbuf[:, :], bass.AP(fnm.tensor, 0, [[0, 128], [1, 1024]]))], w=[("c", "nbuf")], key="nbuf")
            for i in tiles:
                npt = 128 if i < 8 else NSAMP
                off = i * 128 if i < 8 else HALF
                t = off // 512
                s = (i % 2) if hf == 0 else (i % 4)
                q_ = i % 2
                sv_ = ssv2[:, q_, :]
                b0, b1 = P.nb(), P.nb()

                def tr_o(e, b0=b0, b1=b1, off=off, npt=npt):
                    last = None
                    for j in range(8):
                        bb = b0 if j < 4 else b1
                        last = e.transpose(ps[:npt, bb, (j % 4) * 128:(j % 4 + 1) * 128], xT[:, j, off:off + npt],
                                           ident[:])
                    return last
                A("pe", _mk(tr_o, 2.0), r=xTr(t) + [("c", "ident")], w=[("ps", b0), ("ps", b1)])
                A("act", ACT(rstd[:npt, :], ps[:npt, b0, :], AF.Square, accum_out=sv_[:npt, 0:1]),
                  r=[("ps", b0)], w=[("rstd",), ("ssv", q_)])
                A("act", ACT(rstd[:npt, :], ps[:npt, b1, :], AF.Square, accum_out=sv_[:npt, 1:2]),
                  r=[("ps", b1)], w=[("rstd",), ("ssv", q_)])
                A("dve", TT(sv_[:npt, 2:3], sv_[:npt, 0:1], sv_[:npt, 1:2], ALU.add), r=[("ssv", q_)], w=[("ssv", q_)])
                A("dve", TS(sv_[:npt, 2:3], sv_[:npt, 2:3], 1.0 / 1024.0, EPS, ALU.mult, ALU.add), r=[("ssv", q_)],
                  w=[("ssv", q_)])
                A("act", ACT(sv_[:npt, 3:4], sv_[:npt, 2:3], AF.Sqrt), r=[("ssv", q_)], w=[("ssv", q_)])
                A("dve", lambda e, npt=npt, sv_=sv_: e.reciprocal(out=sv_[:npt, 3:4], in_=sv_[:npt, 3:4]),
                  r=[("ssv", q_)], w=[("ssv", q_)])
                A("dve", STT(SLOT[s][:npt, 0:512], ps[:npt, b0, :], sv_[:npt, 3:4], fnb[:npt, 0:512], ALU.mult,
                             ALU.mult), r=[("ps", b0), ("ssv", q_), ("c", "nbuf")], w=SLOTR[s])
                A("dve", STT(SLOT[s][:npt, 512:1024], ps[:npt, b1, :], sv_[:npt, 3:4], fnb[:npt, 512:1024], ALU.mult,
                             ALU.mult), r=[("ps", b1), ("ssv", q_), ("c", "nbuf")] + SLOTR[s], w=SLOTR[s])
                dst = y_p[hf * HALF + i * 128:hf * HALF + (i + 1) * 128, :] if i < 8 else y_s[:, :]
                rk = ("out", "y%d_%d" % (hf, i))
                P.dma("sp", [(dst, SLOT[s][:npt, :])], r=SLOTR[s], w=[rk], key="so%d" % s)
                out_res.append(rk)
        A("sp", None, r=out_res)
        P.emit(nc, stack)
    return nc


_CACHE = {}


def make_in_maps(inp):
    f = lambda a: np.ascontiguousarray(np.asarray(a, dtype=np.float32))
    vec = np.stack([f(inp["ffn1_norm"])[0], f(inp["mix_norm"])[0], f(inp["ffn2_norm"])[0],
                    f(inp["conv_w"])[0, 0], f(inp["conv_w"])[0, 1], f(inp["conv_w"])[0, 2], f(inp["conv_w"])[0, 3],
                    f(inp["conv_b"])[0], f(inp["lru_b_r"])[0].reshape(-1), f(inp["lru_b_i"])[0].reshape(-1),
                    f(inp["lru_lambda"])[0]], axis=0)
    shared = {
        "vecs": f(vec), "gvn": f(inp["gmlp_v_norm"]).reshape(1, D), "fnm": f(inp["final_norm"]).reshape(1, D),
        "spb": f(inp["spatial_b"]).reshape(1, D), "spw": f(inp["spatial_w"])[0],
        "wr": f(inp["lru_w_r"])[0], "wi": f(inp["lru_w_i"])[0],
        "f1g": f(inp["ffn1_w_gate"])[0], "f1u": f(inp["ffn1_w_up"])[0], "f1d": f(inp["ffn1_w_down"])[0],
        "win": f(inp["w_in"])[0], "pa": f(inp["proj_a"])[0], "pb": f(inp["proj_b"])[0], "wo": f(inp["w_out"])[0],
        "f2g": f(inp["ffn2_w_gate"])[0], "f2u": f(inp["ffn2_w_up"])[0], "f2d": f(inp["ffn2_w_down"])[0],
    }
    xpr = f(inp["x_prompt"])
    xsm = f(inp["x_sample"])
    slh = f(inp["state_lru_h"])
    scv = f(inp["state_conv"])
    maps = []
    for c in range(8):
        m = dict(shared)
        m["xp"] = xpr[c]
        m["xs"] = f(xsm[c * NSAMP:(c + 1) * NSAMP, 0, :])
        m["sh"] = f(slh[0, c * NSAMP:(c + 1) * NSAMP, :])
        m["sc"] = f(scv[0, c * NSAMP:(c + 1) * NSAMP])
        maps.append(m)
    return maps


def kernel(**inputs):
    if "nc" not in _CACHE:
        _CACHE["nc"] = build()
    nc = _CACHE["nc"]
    maps = make_in_maps(inputs)
    res = run_bass_kernel_spmd(nc, maps, core_ids=list(range(8)))
    R = res.results
    y_prompt = np.stack([R[c]["y_p"] for c in range(8)], axis=0)
    y_sample = np.concatenate([R[c]["y_s"] for c in range(8)], axis=0)[:, None, :]
    h_pr = np.concatenate([R[c]["h_p"] for c in range(8)], axis=0)[None]
    c_pr = np.stack([R[c]["c_p"] for c in range(8)], axis=0)[None]
    h_sm = np.concatenate([R[c]["h_s"] for c in range(8)], axis=0)[None]
    c_sm = np.concatenate([R[c]["c_s"] for c in range(8)], axis=0)[None]
    v_sm = np.concatenate([R[c]["v_s"] for c in range(8)], axis=0)[None, :, None, :]
    return (y_prompt.astype(np.float32), y_sample.astype(np.float32), h_pr.astype(np.float32),
            c_pr.astype(np.float32), h_sm.astype(np.float32), c_sm.astype(np.float32), v_sm.astype(np.float32))
```

```python
import contextlib
import numpy as np
import concourse.bass as bass
import concourse.mybir as mybir
from concourse.bass_utils import run_bass_kernel_spmd

F32 = mybir.dt.float32
BF16 = mybir.dt.bfloat16
AF = mybir.ActivationFunctionType
ALU = mybir.AluOpType

D = 1024
DFF = 2752
SEQ = 2048
NSAMP = 16
HALF = 1024
NTW = HALF + NSAMP
EPS = 1e-6
NV = 11
NATIVE_GELU = True
PE_RATE = 2000.0
EW_SCALE = 1.0
GK = 0.7978845608028654
GC = 0.044715 * GK


class Op:
    __slots__ = ("idx", "eng", "fn", "dmas", "deps", "has_dep", "sig", "key")


class Prog:
    tie = 0.0
    lat = 0.12
    dma_lat = 2.2
    dma_bw = 150000.0
    window = 150

    def __init__(self):
        self.ops = []
        self.res = {}
        self.bank = 0

    def _add(self, op, reads, writes):
        op.idx = len(self.ops)
        self.ops.append(op)
        deps = set()
        extra = set()
        for r in list(reads) + list(writes):
            if isinstance(r, tuple):
                if r[0] in ("hid", "vtm", "ya", "yb", "m"):
                    extra.add("ARENA")
                if r[0] in ("wb", "L", "xb", "xbh"):
                    extra.add("WBMEM")
        for r in list(reads) + list(extra):
            st = self.res.setdefault(r, [[], []])
            deps.update(st[0])
            st[1].append(op.idx)
        for w in writes:
            st = self.res.setdefault(w, [[], []])
            deps.update(st[0])
            deps.update(st[1])
            st[0] = [op.idx]
            st[1] = []
        deps.discard(op.idx)
        op.deps = deps
        op.has_dep = False
        return op

    def op(self, eng, fn, r=(), w=()):
        o = Op()
        o.eng = eng
        o.fn = fn
        o.dmas = None
        o.key = None
        o.sig = None
        return self._add(o, r, w)

    def dma(self, eng, pairs, r=(), w=(), key=None, **kw):
        o = Op()
        o.eng = eng
        o.fn = kw
        o.dmas = pairs
        o.key = key
        o.sig = None
        return self._add(o, r, w)

    def fence(self, name):
        st = self.res.setdefault(name, [[], []])
        o = Op()
        o.eng = None
        o.fn = None
        o.dmas = None
        o.key = None
        o.sig = None
        o.idx = len(self.ops)
        self.ops.append(o)
        o.deps = set(st[0]) | set(st[1])
        o.has_dep = False
        st[0] = [o.idx]
        st[1] = []

    def nb(self):
        b = self.bank
        self.bank = (b + 1) % 8
        return b

    def schedule(self, window=600):
        AFT = AF
        tblsets = {AFT.Gelu_apprx_tanh: {11}, AFT.Tanh: {0, 11, 18, 2}, AFT.Exp: {0}, AFT.Sqrt: {3}, AFT.Silu: {18},
                   AFT.Sigmoid: {2}}
        n = len(self.ops)
        succ = [[] for _ in range(n)]
        indeg = [0] * n
        for o in self.ops:
            indeg[o.idx] = len(o.deps)
            for d in o.deps:
                succ[d].append(o.idx)
        finish = [0.0] * n
        est = [0.0] * n
        engs = ("pe", "act", "dve", "pool", "sp")
        def ocost(o):
            if o.eng is None:
                return 0.0
            if o.dmas is not None:
                return 2.5
            return getattr(o.fn, "cost", 0.1) if o.fn is not None else 0.02
        bl = [0.0] * n
        for i in range(n - 1, -1, -1):
            m_ = 0.0
            for j in succ[i]:
                if bl[j] > m_:
                    m_ = bl[j]
            bl[i] = m_ + ocost(self.ops[i])
        TIE = self.tie
        free = {e: 0.0 for e in engs}
        ready = {e: [] for e in engs}
        order = {e: [] for e in engs}
        cur_tbl = [None]
        done = [False] * n
        low = 0
        nsched = [0]
        LAT = self.lat

        def release(i):
            for j in succ[i]:
                oj = self.ops[j]
                t_ = finish[i] + LAT
                if t_ > est[j]:
                    est[j] = t_
                indeg[j] -= 1
                if indeg[j] == 0:
                    if oj.eng is None:
                        finish[j] = est[j] - LAT
                        done[j] = True
                        nsched[0] += 1
                        release(j)
                    else:
                        ready[oj.eng].append(j)

        for o in list(self.ops):
            if indeg[o.idx] == 0 and not done[o.idx]:
                if o.eng is None:
                    done[o.idx] = True
                    nsched[0] += 1
                    release(o.idx)
                else:
                    ready[o.eng].append(o.idx)
        while nsched[0] < n:
            while low < n and done[low]:
                low += 1
            lim = low + window
            best = None
            for e in engs:
                fe = free[e]
                for i in ready[e]:
                    if i >= lim:
                        continue
                    st = est[i] if est[i] > fe else fe
                    pen = 0.0
                    if e == "act":
                        fn = self.ops[i].fn
                        tb = getattr(fn, "tbl", None) if fn is not None else None
                        acc = tblsets.get(tb)
                        if acc is not None and cur_tbl[0] not in acc:
                            pen = 1.28
                    key = (st + pen, i)
                    if best is None or key[0] < best[0][0] - TIE or \
                            (key[0] <= best[0][0] + TIE and i < best[2]):
                        best = (key, e, i, st, pen)
            if best is None:
                raise RuntimeError("scheduler stuck")
            _, e, i, st, pen = best
            o = self.ops[i]
            ready[e].remove(i)
            if o.dmas is not None:
                nbytes = 0
                for (dst, src) in o.dmas:
                    nb_ = 1
                    for d_ in src.shape:
                        nb_ *= int(d_)
                    nbytes += nb_ * 4
                occ = 0.06 * len(o.dmas)
                dur = self.dma_lat + nbytes / self.dma_bw
            else:
                c = getattr(o.fn, "cost", 0.1) if o.fn is not None else 0.02
                if e in ("act", "dve"):
                    c = c * EW_SCALE
                occ = c + pen
                dur = occ
                if e == "act" and pen > 0:
                    cur_tbl[0] = min(tblsets.get(o.fn.tbl))
            free[e] = st + occ
            finish[i] = st + dur
            done[i] = True
            nsched[0] += 1
            order[e].append(i)
            release(i)
        self.makespan = max(finish) if n else 0.0
        return order

    def emit(self, nc, stack, sched=True):
        engs = ("pe", "act", "dve", "pool", "sp")
        sems = {}

        def sem(name):
            if name not in sems:
                sems[name] = stack.enter_context(nc.semaphore(name))
            return sems[name]

        if sched:
            order = self.schedule(self.window)
        else:
            order = {e: [o.idx for o in self.ops if o.eng == e] for e in engs}
        pos = {}
        for e in engs:
            for p_, i in enumerate(order[e]):
                pos[i] = p_
        vcache = {}

        def frontier(idxs):
            best = {}
            dm = set()
            for d in idxs:
                dd = self.ops[d]
                if dd.eng is None:
                    if d not in vcache:
                        vcache[d] = frontier(dd.deps)
                    b2, d2 = vcache[d]
                    for e_, i_ in b2.items():
                        if e_ not in best or pos[best[e_]] < pos[i_]:
                            best[e_] = i_
                    dm |= d2
                elif dd.dmas is not None:
                    dm.add(d)
                else:
                    if dd.eng not in best or pos[best[dd.eng]] < pos[d]:
                        best[dd.eng] = d
            return best, dm

        import sys
        sys.setrecursionlimit(10000)
        eff = {}
        for o in self.ops:
            if o.eng is None:
                continue
            b_, d_ = frontier(o.deps)
            lst = []
            for e_, i_ in b_.items():
                if e_ == "pe" and o.eng == "pe" and o.dmas is None:
                    continue
                lst.append(i_)
            lst.extend(d_)
            eff[o.idx] = lst
            for i_ in lst:
                self.ops[i_].has_dep = True
        seq = {e: 0 for e in engs}
        kcnt = {}
        for o in [self.ops[i] for e in engs for i in order[e]]:
            if o.dmas is not None:
                k = "d_" + o.key
                kcnt[k] = kcnt.get(k, 0) + 16 * len(o.dmas)
                o.sig = (k, kcnt[k])
                sem(k)
            elif o.has_dep:
                seq[o.eng] += 1
                o.sig = ("e_" + o.eng, seq[o.eng])
                sem("e_" + o.eng)
        block = stack.enter_context(nc.Block())
        deco = {"pe": block.tensor, "act": block.scalar, "dve": block.vector,
                "pool": block.gpsimd, "sp": block.sync}
        for en in engs:
            myops = [self.ops[i] for i in order[en]]
            if not myops:
                continue

            def body(e, myops=myops, en=en):
                waited = {}
                for o in myops:
                    need = {}
                    for d in eff[o.idx]:
                        s_, v = self.ops[d].sig
                        if need.get(s_, 0) < v:
                            need[s_] = v
                    for s_, v in need.items():
                        if waited.get(s_, 0) >= v:
                            continue
                        e.wait_ge(sems[s_], v)
                        waited[s_] = v
                    if o.dmas is not None:
                        for (dst, src) in o.dmas:
                            e.dma_start(out=dst, in_=src, **o.fn).then_inc(sems[o.sig[0]], 16)
                    elif o.fn is not None:
                        last = o.fn(e)
                        if o.sig is not None:
                            last.then_inc(sems[o.sig[0]], 1)

            deco[en](body)


def _nfree(ap):
    n = 1
    for d in ap.shape[1:]:
        n *= int(d)
    return n


def _mk(f, cost, tbl=None, scan=False):
    f.cost = cost
    f.tbl = tbl
    return f


def ACT(out, in_, func, **kw):
    return _mk(lambda e: e.activation(out=out, in_=in_, func=func, **kw), 0.22 + _nfree(out) / 1200.0, tbl=func)


def TT(out, in0, in1, op):
    return _mk(lambda e: e.tensor_tensor(out=out, in0=in0, in1=in1, op=op), 0.07 + _nfree(out) / 960.0)


def STT(out, in0, scalar, in1, op0, op1):
    return _mk(lambda e: e.scalar_tensor_tensor(out=out, in0=in0, scalar=scalar, in1=in1, op0=op0, op1=op1),
               0.07 + _nfree(out) / 960.0)


def TS(out, in0, s1, s2, op0, op1):
    return _mk(lambda e: e.tensor_scalar(out=out, in0=in0, scalar1=s1, scalar2=s2, op0=op0, op1=op1),
               0.07 + _nfree(out) / 960.0)


def TS1(out, in_, s, op):
    return _mk(lambda e: e.tensor_single_scalar(out=out, in_=in_, scalar=s, op=op), 0.07 + _nfree(out) / 960.0)


def CP(out, in_):
    return _mk(lambda e: e.tensor_copy(out=out, in_=in_), 0.07 + _nfree(out) / 960.0)


def PCP(out, in_):
    return _mk(lambda e: e.tensor_copy(out=out, in_=in_), 0.12 + _nfree(out) / 480.0)


def PSTT(out, in0, scalar, in1, op0, op1):
    return _mk(lambda e: e.scalar_tensor_tensor(out=out, in0=in0, scalar=scalar, in1=in1, op0=op0, op1=op1),
               0.12 + _nfree(out) / 480.0)


def MSET(ap, v):
    return _mk(lambda e: e.memset(ap, v), 0.07 + _nfree(ap) / 1900.0)


def MM(out, pairs):
    def f(e):
        n = len(pairs)
        last = None
        for i, (l, r) in enumerate(pairs):
            last = e.matmul(out, l, r, start=(i == 0), stop=(i == n - 1))
        return last
    nn = max(_nfree(out), 64)
    return _mk(f, len(pairs) * (nn / PE_RATE + 0.004))


def build(stage=99, debug=False):
    nc = bass.Bass("TRN2", target_bir_lowering=False)

    def din(name, shape):
        return nc.dram_tensor(name, list(shape), F32, kind="ExternalInput").ap()

    def dout(name, shape):
        return nc.dram_tensor(name, list(shape), F32, kind="ExternalOutput").ap()

    xp = din("xp", (SEQ, D))
    xs = din("xs", (NSAMP, D))
    sh = din("sh", (NSAMP, D))
    sc = din("sc", (NSAMP, 3, D))
    vecs = din("vecs", (NV, D))
    gvn = din("gvn", (1, D))
    fnm = din("fnm", (1, D))
    spb = din("spb", (1, D))
    spw = din("spw", (8, 128, 128))
    wr = din("wr", (16, 64, 64))
    wi = din("wi", (16, 64, 64))
    f1g = din("f1g", (D, DFF))
    f1u = din("f1u", (D, DFF))
    f1d = din("f1d", (DFF, D))
    win = din("win", (D, 6 * D))
    pa = din("pa", (D, D))
    pb = din("pb", (D, D))
    wo = din("wo", (D, D))
    f2g = din("f2g", (D, DFF))
    f2u = din("f2u", (D, DFF))
    f2d = din("f2d", (DFF, D))

    y_p = dout("y_p", (SEQ, D))
    y_s = dout("y_s", (NSAMP, D))
    h_p = dout("h_p", (1, D))
    c_p = dout("c_p", (3, D))
    h_s = dout("h_s", (NSAMP, D))
    c_s = dout("c_s", (NSAMP, 3, D))
    v_s = dout("v_s", (NSAMP, D))
    if debug:
        dbg = dout("dbg", (128, 8 * NTW))

    st_ = contextlib.ExitStack()
    with st_ as stack:
        def sb(name, shape, dt):
            return stack.enter_context(nc.sbuf_tensor(name, list(shape), dt))

        xT = sb("xT", (128, 8, NTW), F32)
        hT = sb("hT", (128, 8, NTW), BF16)
        arena = sb("arena", (128, 25856), BF16)
        wa = sb("wa", (128, 4, 8, 512), BF16)
        wbm = sb("wbm", (128, 8768), F32)
        stg = sb("stg", (128, 2, 1024), F32)
        vg = sb("vg", (128, 1024), F32)
        actS = sb("actS", (128, 2, 512), F32)
        rstd = sb("rstd", (128, 512), F32)
        ident = sb("ident", (128, 128), F32)
        maskf = sb("maskf", (128, 128), F32)
        onesb = sb("onesb", (128, 128), BF16)
        onesrow = sb("onesrow", (3, 128), BF16)
        WsT = sb("WsT", (128, 8, 128), BF16)
        wrbd = sb("wrbd", (128, 8, 128), BF16)
        wibd = sb("wibd", (128, 8, 128), BF16)
        vT = sb("vT", (128, 8, 16), F32)
        tv = sb("tv", (128, 8, 8), F32)
        nbuf = sb("nbuf", (128, 1024), F32)
        gvnb = nbuf
        fnb = nbuf
        stT = sb("stT", (128, 8, 64), F32)
        bhi = sb("bhi", (3, 1024), BF16)
        blo = sb("blo", (1, 1024), BF16)
        ws00 = sb("ws00", (16, 8), F32)
        Dg = sb("Dg", (16, 8, 16), BF16)
        ssv2 = sb("ssv2", (128, 2, 4), F32)
        ssq = sb("ssq", (128, 3, 9), F32)
        hc = sb("hc", (128, 8), F32)
        cv = sb("cv", (128, 8, 3), F32)
        fin = sb("fin", (128, 8, 36), F32)
        ps = stack.enter_context(nc.psum_tensor("ps", [128, 8, 512], F32))

        hid = arena[:, 0:22 * NTW].rearrange("p (f n) -> p f n", n=NTW)
        vtm = arena[:, 0:8192].rearrange("p (i n) -> p i n", n=1024)
        vtms = arena[:, 8192:9216]
        ybv = arena[:, 0:8 * NTW].rearrange("p (j n) -> p j n", n=NTW)
        ya = arena[:, 9216:9216 + 8 * NTW].rearrange("p (j n) -> p j n", n=NTW)
        mb_ = arena[:, 17536:17536 + 8 * NTW].rearrange("p (j n) -> p j n", n=NTW)
        wbb = wbm[:, 0:8448].bitcast(BF16)
        wb = wbb.rearrange("p (s f n) -> p s f n", s=3, f=22)
        XBW = 1056
        xb = wbm[:, 0:2 * XBW].rearrange("p (s n) -> p s n", n=XBW)
        Lsets = []
        xcbs = []
        for S_ in range(2):
            base = 2 * XBW + S_ * 3328
            Lt = wbm[:, base:base + 6 * 512].rearrange("p (s n) -> p s n", n=512)
            Lsets.append(tuple(Lt[:, i, :] for i in range(6)))
            xcbs.append(wbm[:, base + 3072:base + 3328].bitcast(BF16))
        Lsm = sb("Lsm", (128, 7, NSAMP), F32)
        Lsets.append(tuple(Lsm[:, i, :] for i in range(6)))
        xcbs.append(Lsm[:, 6, :].bitcast(BF16))

        if debug:
            try:
                print("SBUF remaining", nc.sbuf_bytes_remaining, "top", nc.sbuf_top, "base", nc.sbuf_base,
                      "part", nc.SBUF_PARTITION_SIZE_BYTES)
            except Exception as ex:
                print("sbuf introspection failed", ex)
        P = Prog()
        A = P.op
        actSf = actS[:].rearrange("p s n -> p (s n)")
        SLOT = [stg[:, 0, :], stg[:, 1, :], vg[:, :], actSf]
        SLOTR = [[("stg", 0)], [("stg", 1)], [("vg", 0), ("vg", 1)], [("actS", 0), ("actS", 1)]]

        wa_ctr = [0]

        def wa_next():
            s = wa_ctr[0]
            wa_ctr[0] = (s + 1) % 4
            return s

        wb_ctr = [0]

        def wb_next():
            s = wb_ctr[0]
            wb_ctr[0] = (s + 1) % 3
            return s

        as_ctr = [0]

        def as_next():
            s = as_ctr[0]
            as_ctr[0] = (s + 1) % 2
            return s

        def load_wa(src3, c0, cw):
            s = wa_next()
            P.dma("pool", [(wa[:, s, :, 0:cw], src3[:, :, c0:c0 + cw])], w=[("wa", s)], key="wa%d" % s)
            return s

        def kview(w):
            return w.rearrange("(k p) n -> p k n", p=128)

        A("pool", MSET(ident[:], 1.0), w=[("c", "ident")])
        A("pool", lambda e: e.affine_select(out=ident[:], in_=ident[:], pattern=[[-1, 128]],
                                            compare_op=ALU.is_equal, fill=0.0, base=0, channel_multiplier=1),
          r=[("c", "ident")], w=[("c", "ident")])
        A("pool", MSET(maskf[:], 1.0), w=[("c", "maskf")])
        A("pool", lambda e: e.affine_select(out=maskf[:], in_=maskf[:], pattern=[[1, 128]],
                                            compare_op=ALU.is_ge, fill=0.0, base=0, channel_multiplier=-1),
          r=[("c", "maskf")], w=[("c", "maskf")])
        A("pool", MSET(onesb[:], 1.0 / 1024.0), w=[("c", "onesb")])
        A("pool", MSET(onesrow[:], 1.0), w=[("c", "onesrow")])
        A("pool", MSET(wrbd[:], 0.0), w=[("c", "wrbd")])
        A("pool", MSET(wibd[:], 0.0), w=[("c", "wibd")])
        A("pool", MSET(hc[:], 0.0), w=[("hc",)])
        A("pool", MSET(ssq[:], 1.0), w=[("ssq", i) for i in range(9)] + [("ssq", "n"), ("ssq", "r")])
        for (wsrc, wdst, nm) in ((wr, wrbd, "wrbd"), (wi, wibd, "wibd")):
            v = wsrc.rearrange("(j e) i o -> e i j o", e=2)
            P.dma("pool", [(wdst[0:64, :, 0:64], v[0]), (wdst[64:128, :, 64:128], v[1])],
                  r=[], w=[("c", nm)], key=nm)
        P.dma("sp", [(stg[0:NV, 0, :], vecs[:, :])], w=[("stg", 0)], key="stg0")
        bsrc = vg[0:1, :]
        bhf = stg[0:1, 1, :]
        P.dma("sp", [(ws00[:, :], bass.AP(spw.tensor, 0, [[0, 16], [16384, 8]]))], w=[("c", "ws00")],
              key="ws00", allow_slow_non_contiguous=True)
        b = P.nb()

        def tr_vecs(e, b=b):
            last = None
            for j in range(8):
                last = e.transpose(ps[:, b, j * 16:j * 16 + NV], stg[0:NV, 0, j * 128:(j + 1) * 128],
                                   ident[0:NV, 0:NV])
            return last
        A("pe", _mk(tr_vecs, 1.0), r=[("stg", 0), ("c", "ident")], w=[("ps", b)])
        A("act", ACT(vT[:, :, 0:NV], ps[:, b, 0:128].rearrange("p (j n) -> p j n", n=16)[:, :, 0:NV], AF.Copy),
          r=[("ps", b)], w=[("vT",)])
        def late_setup():
            lam = vT[:, :, 10]
            t0, t1, t2, t3, t4 = (tv[:, :, i] for i in range(5))
            A("dve", TS1(t0, lam, -1.0, ALU.mult), r=[("vT",)], w=[("tv",)])
            A("dve", TT(t0, t0, lam, ALU.max), r=[("vT",), ("tv",)], w=[("tv",)])
            A("act", ACT(t0, t0, AF.Exp, scale=-1.0), r=[("tv",)], w=[("tv",)])
            A("dve", TS1(t1, t0, 2.0, ALU.add), r=[("tv",)], w=[("tv",)])
            A("dve", lambda e: e.reciprocal(out=t1, in_=t1), r=[("tv",)], w=[("tv",)])
            A("dve", TT(t1, t0, t1, ALU.mult), r=[("tv",)], w=[("tv",)])
            A("dve", TT(t2, t1, t1, ALU.mult), r=[("tv",)], w=[("tv",)])
            A("dve", TS(t3, t2, 1.0 / 9.0, 1.0 / 7.0, ALU.mult, ALU.add), r=[("tv",)], w=[("tv",)])
            for cst in (1.0 / 5.0, 1.0 / 3.0, 1.0):
                A("dve", TT(t3, t3, t2, ALU.mult), r=[("tv",)], w=[("tv",)])
                A("dve", TS1(t3, t3, cst, ALU.add), r=[("tv",)], w=[("tv",)])
            A("dve", STT(t3, t1, 2.0, t3, ALU.mult, ALU.mult), r=[("tv",)], w=[("tv",)])
            A("dve", TS(t4, lam, -1.0, 0.0, ALU.mult, ALU.max), r=[("tv",), ("vT",)], w=[("tv",)])
            A("dve", TT(t3, t3, t4, ALU.add), r=[("tv",)], w=[("tv",)])
            A("dve", TS1(vT[:, :, 11], t3, -8.0, ALU.mult), r=[("tv",)], w=[("vT",)])
            A("dve", TS1(vT[:, :, 12], t3, -4.0, ALU.mult), r=[("tv",)], w=[("vT",)])
            A("dve", TS1(vT[:, :, 13], t3, -2.0, ALU.mult), r=[("tv",)], w=[("vT",)])
            A("dve", TS1(vT[:, :, 14], vT[:, :, 8], 0.5, ALU.mult), r=[("vT",)], w=[("vT",)])
            A("dve", TS1(vT[:, :, 15], vT[:, :, 9], 0.5, ALU.mult), r=[("vT",)], w=[("vT",)])
            P.dma("sp", [(bsrc, spb[:, :])], w=[("vg", 0), ("vg", 1)], key="bsrc")
            A("dve", CP(bhi[0:1, :], bsrc), r=[("vg", 0), ("vg", 1)], w=[("c", "bhi")])
            A("dve", CP(bhf, bhi[0:1, :]), r=[("c", "bhi")], w=[("stg", 1)])
            A("dve", TT(bsrc, bsrc, bhf, ALU.subtract), r=[("vg", 0), ("vg", 1), ("stg", 1)], w=[("vg", 0), ("vg", 1)])
            A("dve", CP(blo[:], bsrc), r=[("vg", 0), ("vg", 1)], w=[("c", "blo")])
            P.dma("sp", [(bhi[1:2, :], blo[:])], r=[("c", "blo")], w=[("c", "bhi1")], key="bmid")
            A("dve", CP(bhf, blo[:]), r=[("c", "blo")], w=[("stg", 1)])
            A("dve", TT(bsrc, bsrc, bhf, ALU.subtract), r=[("vg", 0), ("vg", 1), ("stg", 1)], w=[("vg", 0), ("vg", 1)])
            A("dve", CP(blo[:], bsrc), r=[("vg", 0), ("vg", 1), ("c", "bhi1")], w=[("c", "blo")])
            P.dma("sp", [(bhi[2:3, :], blo[:])], r=[("c", "blo")], w=[("c", "bhi2")], key="blo2")
            for g in range(8):
                A("dve", TS1(Dg[:, g, :], ident[0:16, 0:16], ws00[:, g:g + 1], ALU.mult),
                  r=[("c", "ident"), ("c", "ws00")], w=[("c", "Dg")])
            P.dma("sp", [(stg[:, 1, :].rearrange("p (g s) -> p g s", g=8), spw.rearrange("g t s -> t g s"))],
                  w=[("stg", 1)], key="stg1")
            for hh in range(2):
                b = P.nb()

                def tr_sp(e, b=b, hh=hh):
                    last = None
                    for q in range(4):
                        g = hh * 4 + q
                        last = e.transpose(ps[:, b, q * 128:(q + 1) * 128], stg[:, 1, g * 128:(g + 1) * 128], ident[:])
                    return last
                A("pe", _mk(tr_sp, 1.6), r=[("stg", 1), ("c", "ident")], w=[("ps", b)])
                for q in range(4):
                    A("dve", TT(WsT[:, hh * 4 + q, :], ps[:, b, q * 128:(q + 1) * 128], maskf[:], ALU.mult),
                      r=[("ps", b), ("c", "maskf")], w=[("c", "WsT")])
            P.dma("sp", [(stg[0:16, 0, :], sh[:, :]), (stg[16:64, 0, :], sc.rearrange("t k d -> (t k) d"))],
                  r=[], w=[("stg", 0)], key="stg0")
            b = P.nb()

            def tr_st(e, b=b):
                last = None
                for j in range(8):
                    last = e.transpose(ps[:, b, j * 64:(j + 1) * 64], stg[0:64, 0, j * 128:(j + 1) * 128],
                                       ident[0:64, 0:64])
                return last
            A("pe", _mk(tr_st, 1.6), r=[("stg", 0), ("c", "ident")], w=[("ps", b)])
            A("act", ACT(stT[:], ps[:, b, :].rearrange("p (j n) -> p j n", n=64), AF.Copy), r=[("ps", b)], w=[("stT",)])
            P.dma("sp", [(c_s[:, 0:2, :], sc[:, 1:3, :])], r=[], w=[("out", "cs01")], key="cs01")

        out_res = [("out", "cs01")]

        def hTr(t):
            return [("hT", j, t) for j in range(8)]

        def xTr(t):
            return [("xT", j, t) for j in range(8)]

        def rmsnorm(TTl, col):
            for t, (off, n) in enumerate(TTl):
                A("act", ACT(hT[:, :, off:off + n], xT[:, :, off:off + n], AF.Square), r=xTr(t), w=hTr(t))
                b = P.nb()
                A("pe", MM(ps[:, b, :n], [(onesb[:], hT[:, j, off:off + n]) for j in range(8)]),
                  r=hTr(t) + [("c", "onesb")], w=[("ps", b)])
                A("act", ACT(rstd[:, :n], ps[:, b, :n], AF.Sqrt, bias=epsb[:, :], scale=1.0),
                  r=[("ps", b), ("c", "epsb")], w=[("rstd",)])
                A("dve", _mk(lambda e, n=n: e.reciprocal(out=rstd[:, :n], in_=rstd[:, :n]), 0.07 + n / 960.0),
                  r=[("rstd",)], w=[("rstd",)])
                for j in range(8):
                    A("dve", STT(hT[:, j, off:off + n], xT[:, j, off:off + n], vT[:, j, col:col + 1],
                                 rstd[:, :n], ALU.mult, ALU.mult),
                      r=[("xT", j, t), ("rstd",), ("vT",)], w=[("hT", j, t)])

        def gelu_to(dst, src, n_res_r, n_res_w):
            if NATIVE_GELU:
                A("act", ACT(dst, src, AF.Gelu_apprx_tanh), r=n_res_r, w=n_res_w)
                return 1.0
            A("act", ACT(dst, src, AF.Square), r=n_res_r, w=n_res_w)
            A("act", ACT(dst, dst, AF.Identity, scale=GC, bias=gkb[0:dst.shape[0], :]), r=n_res_w + [("c", "gkb")], w=n_res_w)
            A("dve", TT(dst, dst, src, ALU.mult), r=n_res_r + n_res_w, w=n_res_w)
            A("act", ACT(dst, dst, AF.Tanh), r=n_res_w, w=n_res_w)
            A("dve", STT(dst, dst, 1.0, src, ALU.add, ALU.mult), r=n_res_r + n_res_w, w=n_res_w)
            return 2.0

        def ffn(TTl, wg, wu, wd):
            wgv, wuv = kview(wg), kview(wu)
            groups = [(0, 512), (512, 512), (1024, 512), (1536, 512), (2048, 512), (2560, 192)]
            for (c0, cw) in groups:
                sg_ = load_wa(wgv, c0, cw)
                su_ = load_wa(wuv, c0, cw)
                for fl in range((cw + 127) // 128):
                    f = c0 // 128 + fl
                    fw = min(128, cw - fl * 128)
                    for t, (off, n) in enumerate(TTl):
                        bg, bu = P.nb(), P.nb()
                        A("pe", MM(ps[:fw, bg, :n], [(wa[:, sg_, k, fl * 128:fl * 128 + fw], hT[:, k, off:off + n])
                                                     for k in range(8)]), r=hTr(t) + [("wa", sg_)], w=[("ps", bg)])
                        A("pe", MM(ps[:fw, bu, :n], [(wa[:, su_, k, fl * 128:fl * 128 + fw], hT[:, k, off:off + n])
                                                     for k in range(8)]), r=hTr(t) + [("wa", su_)], w=[("ps", bu)])
                        sl = as_next()
                        A("act", ACT(actS[:fw, sl, :n], ps[:fw, bg, :n], AF.Silu), r=[("ps", bg)], w=[("actS", sl)])
                        A("dve", TT(hid[:fw, f, off:off + n], actS[:fw, sl, :n], ps[:fw, bu, :n], ALU.mult),
                          r=[("actS", sl), ("ps", bu)], w=[("hid", f, t)])
            wd3 = wd[0:2688, :].rearrange("(f p) n -> p f n", p=128)
            passes = [[0], list(range(1, len(TTl)))]
            for tl in passes:
                for mp in range(4):
                    s = wb_next()
                    P.dma("pool", [(wb[:, s, 0:21, :], wd3[:, :, mp * 256:(mp + 1) * 256]),
                                   (wb[0:64, s, 21, :], wd[2688:2752, mp * 256:(mp + 1) * 256])],
                          w=[("wb", s)], key="wb%d" % s)
                    for ml in range(2):
                        m = mp * 2 + ml
                        for t in tl:
                            off, n = TTl[t]
                            b = P.nb()
                            pairs = [(wb[:, s, f, ml * 128:(ml + 1) * 128], hid[:, f, off:off + n]) for f in range(21)]
                            pairs.append((wb[0:64, s, 21, ml * 128:(ml + 1) * 128], hid[0:64, 21, off:off + n]))
                            A("pe", MM(ps[:, b, :n], pairs), r=[("wb", s)] + [("hid", f, t) for f in range(22)],
                              w=[("ps", b)])
                            A("dve", STT(xT[:, m, off:off + n], ps[:, b, :n], 0.5, xT[:, m, off:off + n],
                                         ALU.mult, ALU.add), r=[("ps", b), ("xT", m, t)], w=[("xT", m, t)])

        gkb = sb("gkb", (128, 1), F32)
        A("pool", MSET(gkb[:], GK), w=[("c", "gkb")])

        def mixer(hf, TTl):
            winv = kview(win)
            pav, pbv, wov = kview(pa), kview(pb), kview(wo)
            P.dma("sp", [(nbuf[:, :], bass.AP(gvn.tensor, 0, [[0, 128], [1, 1024]]))], w=[("c", "nbuf")], key="nbuf")
            sv0 = load_wa(winv, 1024, 512)
            sv1 = load_wa(winv, 1536, 512)
            tiles = list(range(8)) + ([8] if hf == 0 else [])
            for i in tiles:
                npt = 128 if i < 8 else NSAMP
                off = i * 128 if i < 8 else HALF
                t = off // 512
                b0, b1 = P.nb(), P.nb()

                def fv(e, b0=b0, b1=b1, off=off, npt=npt):
                    last = None
                    for k in range(8):
                        for (bb, sv) in ((b0, sv0), (b1, sv1)):
                            last = e.matmul(ps[:npt, bb, :], hT[:, k, off:off + npt], wa[:, sv, k, :],
                                            start=(k == 0), stop=(k == 7))
                    return last
                A("pe", _mk(fv, 4.2), r=hTr(t) + [("wa", sv0), ("wa", sv1)], w=[("ps", b0), ("ps", b1)])
                buf = SLOT[2 + (i % 2)]
                bufr = SLOTR[2 + (i % 2)]
                gs = 1.0
                for c, bb in enumerate((b0, b1)):
                    gs = gelu_to(buf[:npt, c * 512:(c + 1) * 512], ps[:npt, bb, :], [("ps", bb)], [bufr[c]])
                vdst = vtm[:, i, :] if i < 8 else vtms[0:NSAMP, :]
                A("act", ACT(mb_[:npt, 0, 0:1024], buf[:npt, :], AF.Square, accum_out=ssq[:npt, 0, i:i + 1]),
                  r=bufr, w=[("m", 0, 0), ("m", 0, 1), ("m", 0, 2), ("ssq", i)])
                A("dve", TT(vdst, buf[:npt, :], gvnb[:npt, :], ALU.mult), r=bufr + [("c", "nbuf")], w=[("vtm", i)])
            nt_ = len(tiles)
            A("dve", TS(ssq[:, 1, 0:nt_], ssq[:, 0, 0:nt_], 1.0 / 1024.0, EPS * gs * gs, ALU.mult, ALU.add),
              r=[("ssq", i) for i in tiles], w=[("ssq", "n")])
            A("act", ACT(ssq[:, 2, 0:nt_], ssq[:, 1, 0:nt_], AF.Sqrt), r=[("ssq", "n")], w=[("ssq", "r")])
            A("dve", lambda e, nt_=nt_: e.reciprocal(out=ssq[:, 2, 0:nt_], in_=ssq[:, 2, 0:nt_]), r=[("ssq", "r")],
              w=[("ssq", "r")])
            for i in tiles:
                if i < 8:
                    A("act", ACT(vtm[:, i, :], vtm[:, i, :], AF.Copy, scale=ssq[:, 2, i:i + 1]),
                      r=[("ssq", "r"), ("vtm", i)], w=[("vtm", i)])
                else:
                    buf = SLOT[2 + (i % 2)]
                    bufr = SLOTR[2 + (i % 2)]
                    A("act", ACT(vtms[0:NSAMP, :], vtms[0:NSAMP, :], AF.Copy, scale=ssq[0:NSAMP, 2, i:i + 1]),
                      r=[("ssq", "r"), ("vtm", i)], w=[("vtm", i)])
                    A("dve", STT(stg[0:NSAMP, 1, :], buf[0:NSAMP, :], ssq[0:NSAMP, 2, i:i + 1], gvnb[0:NSAMP, :],
                                 ALU.mult, ALU.mult), r=bufr + [("ssq", "r"), ("c", "nbuf")], w=[("stg", 1)])
                    P.dma("sp", [(v_s[:, :], stg[0:NSAMP, 1, :])], r=[("stg", 1)], w=[("out", "vs")], key="so1")
                    out_res.append(("out", "vs"))
            su = None
            for g in range(8):
                if g % 4 == 0:
                    su = load_wa(winv, (g // 4) * 512, 512)
                gl = g % 4
                for t, (off, n) in enumerate(TTl):
                    bu, bs = P.nb(), P.nb()
                    A("pe", MM(ps[:, bu, :n], [(wa[:, su, k, gl * 128:(gl + 1) * 128], hT[:, k, off:off + n])
                                               for k in range(8)]), r=hTr(t) + [("wa", su)], w=[("ps", bu)])
                    if n == 512:
                        def fs(e, bs=bs, g=g, off=off):
                            last = None
                            for c in range(4):
                                i = off // 128 + c
                                e.matmul(ps[:, bs, c * 128:(c + 1) * 128], onesrow[0:3, :],
                                         bhi[0:3, g * 128:(g + 1) * 128], start=True, stop=False)
                                last = e.matmul(ps[:, bs, c * 128:(c + 1) * 128], vtm[:, i, g * 128:(g + 1) * 128],
                                                WsT[:, g, :], start=False, stop=True)
                            return last
                        rr = [("vtm", off // 128 + c) for c in range(4)]
                    else:
                        def fs(e, bs=bs, g=g):
                            e.matmul(ps[:, bs, :NSAMP], onesrow[0:3, :],
                                     bhi[0:3, g * 128:g * 128 + 1].to_broadcast([3, NSAMP]), start=True, stop=False)
                            return e.matmul(ps[:, bs, :NSAMP], vtms[0:NSAMP, g * 128:(g + 1) * 128], Dg[:, g, :],
                                            start=False, stop=True)
                        rr = [("vtm", 8)]
                    A("pe", _mk(fs, 0.6), r=rr + [("c", "WsT"), ("c", "bhi"), ("c", "bhi1"), ("c", "bhi2"), ("c", "Dg"), ("c", "onesrow")],
                      w=[("ps", bs)])
                    sl = as_next()
                    gs = gelu_to(actS[:, sl, :n], ps[:, bu, :n], [("ps", bu)], [("actS", sl)])
                    A("dve", STT(ya[:, g, off:off + n], actS[:, sl, :n], 1.0 / gs, ps[:, bs, :n], ALU.mult, ALU.mult),
                      r=[("actS", sl), ("ps", bs)], w=[("ya", g, t)])
            P.fence("ARENA")
            def c_chain(j, spa, sga):
                jl = j % 4
                for t, (off, n) in enumerate(TTl):
                    bp, bg = P.nb(), P.nb()
                    A("pe", MM(ps[:, bp, :n], [(wa[:, spa, k, jl * 128:(jl + 1) * 128], ya[:, k, off:off + n])
                                               for k in range(8)]),
                      r=[("ya", k, t) for k in range(8)] + [("wa", spa)], w=[("ps", bp)])
                    A("pe", MM(ps[:, bg, :n], [(wa[:, sga, k, jl * 128:(jl + 1) * 128], hT[:, k, off:off + n])
                                               for k in range(8)]), r=hTr(t) + [("wa", sga)], w=[("ps", bg)])
                    yield
                    sl = as_next()
                    A("act", ACT(actS[:, sl, :n], ps[:, bg, :n], AF.Tanh, scale=0.5), r=[("ps", bg)], w=[("actS", sl)])
                    yield
                    A("dve", STT(mb_[:, j, off:off + n], actS[:, sl, :n], 1.0, ps[:, bp, :n], ALU.add, ALU.mult),
                      r=[("actS", sl), ("ps", bp)], w=[("m", j, t)])
                    yield
                    yield
                    yield
            def lru_chain(j, t, off, n, S, sx, sgb):
                jl = j % 4
                jb = j % 2
                L_xc, L_tr, L_ti, L_e, L_th, L_g = Lsets[S]
                xcb_ = xcbs[S]
                ydst = ybv

                def Ln(nm):
                    return ("L", S, nm)
                samp = (n == NSAMP)
                bx, bgt = P.nb(), P.nb()
                A("pe", MM(ps[:, bx, :n], [(wa[:, sx, k, jl * 128:(jl + 1) * 128], hT[:, k, off:off + n])
                                           for k in range(8)]), r=hTr(t) + [("wa", sx)], w=[("ps", bx)])
                A("pe", MM(ps[:, bgt, :n], [(wa[:, sgb, k, jl * 128:(jl + 1) * 128], hT[:, k, off:off + n])
                                            for k in range(8)]), r=hTr(t) + [("wa", sgb)], w=[("ps", bgt)])
                yield
                A("act", ACT(xb[:, jb, 3 + off:3 + off + n], ps[:, bx, :n], AF.Copy),
                  r=[("ps", bx)], w=[("xb", jb, t)])
                yield
                gs = gelu_to(L_g[:, :n], ps[:, bgt, :n], [("ps", bgt)], [Ln("g")])
                yield
                A("dve", TS(L_xc[:, :n], xb[:, jb, 3 + off:3 + off + n], vT[:, j, 6:7], vT[:, j, 7:8],
                            ALU.mult, ALU.add), r=[("xb", jb, t), ("vT",)], w=[Ln("xc")])
                yield
                for k in range(3):
                    if samp:
                        src = stT[:, j, 16 + k:64:3]
                        rr = [("stT",)]
                    else:
                        src = xb[:, jb, off + k:off + k + n]
                        rr = [("xb", jb, t)] + ([("xbh", jb)] if t == 0 else [("xb", jb, t - 1)])
                    A("dve", STT(L_xc[:, :n], src, vT[:, j, 3 + k:4 + k], L_xc[:, :n], ALU.mult, ALU.add),
                      r=rr + [Ln("xc"), ("vT",)], w=[Ln("xc")])
                    yield
                A("act", ACT(xcb_[:, :n], L_xc[:, :n], AF.Copy), r=[Ln("xc")], w=[Ln("xcb")])
                yield
                br, bi = P.nb(), P.nb()
                A("pe", MM(ps[:, br, :n], [(wrbd[:, j, :], xcb_[:, :n])]), r=[Ln("xcb"), ("c", "wrbd")],
                  w=[("ps", br)])
                A("pe", MM(ps[:, bi, :n], [(wibd[:, j, :], xcb_[:, :n])]), r=[Ln("xcb"), ("c", "wibd")],
                  w=[("ps", bi)])
                yield
                A("act", ACT(L_tr[:, :n], ps[:, br, :n], AF.Tanh, scale=0.5, bias=vT[:, j, 14:15]),
                  r=[("ps", br), ("vT",)], w=[Ln("tr")])
                yield
                A("act", ACT(L_ti[:, :n], ps[:, bi, :n], AF.Tanh, scale=0.5, bias=vT[:, j, 15:16]),
                  r=[("ps", bi), ("vT",)], w=[Ln("ti")])
                yield
                A("act", ACT(L_e[:, :n], L_tr[:, :n], AF.Exp, scale=vT[:, j, 12:13], bias=vT[:, j, 12:13]),
                  r=[Ln("tr"), ("vT",)], w=[Ln("e")])
                yield
                A("act", ACT(L_th[:, :n], L_tr[:, :n], AF.Tanh, scale=vT[:, j, 13:14], bias=vT[:, j, 13:14]),
                  r=[Ln("tr"), ("vT",)], w=[Ln("th")])
                yield
                A("dve", STT(L_ti[:, :n], L_ti[:, :n], 1.0, L_xc[:, :n], ALU.add, ALU.mult),
                  r=[Ln("ti"), Ln("xc")], w=[Ln("ti")])
                yield
                A("dve", STT(L_e[:, :n], L_e[:, :n], 1.0, L_th[:, :n], ALU.add, ALU.mult),
                  r=[Ln("e"), Ln("th")], w=[Ln("e")])
                yield
                A("dve", STT(L_th[:, :n], L_e[:, :n], 2.0, L_e[:, :n], ALU.add, ALU.mult),
                  r=[Ln("e"), Ln("th")], w=[Ln("th")])
                yield
                A("act", ACT(L_th[:, :n], L_th[:, :n], AF.Sqrt, scale=-0.25), r=[Ln("th")], w=[Ln("th")])
                yield
                A("dve", TS1(L_e[:, :n], L_e[:, :n], 1.0, ALU.add), r=[Ln("e")], w=[Ln("e")])
                yield
                A("dve", TT(L_ti[:, :n], L_ti[:, :n], L_th[:, :n], ALU.mult), r=[Ln("ti"), Ln("th")], w=[Ln("ti")])
                yield
                if not samp:
                    if t == 0:
                        init = 0.0 if hf == 0 else hc[:, j:j + 1]
                        ir = [("hc",)]
                    else:
                        init = Lsets[0][1][:, 511:512]
                        ir = [("L", 0, "tr")]
                    A("dve", _mk(lambda e, n=n, init=init: e.tensor_tensor_scan(
                        out=L_tr[:, :n], data0=L_e[:, :n], data1=L_ti[:, :n], initial=init,
                        op0=ALU.mult, op1=ALU.add), 0.07 + 2.2 * n / 960.0),
                      r=[Ln("e"), Ln("ti"), Ln("tr")] + ir, w=[Ln("tr")])
                    yield
                    if t == 1 and hf == 0:
                        A("act", ACT(hc[:, j:j + 1], L_tr[:, n - 1:n], AF.Copy), r=[Ln("tr")], w=[("hc",)])
                        A("act", ACT(cv[:, j, :], xb[:, jb, 3 + HALF - 3:3 + HALF], AF.Copy),
                          r=[("xb", jb, t)], w=[("cv", j)])
                    if t == 1 and hf == 1:
                        A("act", ACT(fin[:, j, 0:1], L_tr[:, n - 1:n], AF.Copy), r=[Ln("tr")], w=[("fin", j)])
                        A("act", ACT(fin[:, j, 1:4], xb[:, jb, 3 + HALF - 3:3 + HALF], AF.Copy),
                          r=[("xb", jb, t)], w=[("fin", j)])
                else:
                    A("dve", TT(L_tr[:, :n], L_e[:, :n], stT[:, j, 0:NSAMP], ALU.mult),
                      r=[Ln("e"), ("stT",), Ln("tr")], w=[Ln("tr")])
                    A("dve", TT(L_tr[:, :n], L_tr[:, :n], L_ti[:, :n], ALU.add), r=[Ln("tr"), Ln("ti")],
                      w=[Ln("tr")])
                    A("act", ACT(fin[:, j, 4:20], L_tr[:, :n], AF.Copy), r=[Ln("tr")], w=[("fin", j)])
                    A("act", ACT(fin[:, j, 20:36], xb[:, jb, 3 + HALF:3 + HALF + NSAMP], AF.Copy),
                      r=[("xb", jb, t)], w=[("fin", j)])
                yield
                A("dve", STT(ydst[:, j, off:off + n], L_g[:, :n], 1.0 / gs, L_tr[:, :n], ALU.mult, ALU.mult),
                  r=[Ln("g"), Ln("tr")], w=[("yb", j, t)])
                yield

            def run_interleaved(gens):
                gens = list(gens)
                while gens:
                    for g_ in list(gens):
                        try:
                            next(g_)
                        except StopIteration:
                            gens.remove(g_)

            for j in range(8):
                if j % 4 == 0:
                    sx = load_wa(winv, 2048 + (j // 4) * 512, 512)
                    sgb = load_wa(winv, 3072 + (j // 4) * 512, 512)
                    spa = load_wa(pav, (j // 4) * 512, 512)
                    sga = load_wa(winv, 4096 + (j // 4) * 512, 512)
                jb = j % 2
                if hf == 0:
                    A("dve", MSET(xb[:, jb, 0:3], 0.0), w=[("xbh", jb)])
                else:
                    A("dve", CP(xb[:, jb, 0:3], cv[:, j, :]), r=[("cv", j)], w=[("xbh", jb)])
                gl_ = [lru_chain(j, 0, 0, 512, 0, sx, sgb), lru_chain(j, 1, 512, 512, 1, sx, sgb)]
                if hf == 0:
                    gl_.append(lru_chain(j, 2, HALF, NSAMP, 2, sx, sgb))
                gl_.append(c_chain(j, spa, sga))
                run_interleaved(gl_)
            c0_, c1_ = (4, 36) if hf == 0 else (0, 4)
            ncol = c1_ - c0_
            b0, b1 = P.nb(), P.nb()

            def tr_fin(e, b0=b0, b1=b1, c0_=c0_, c1_=c1_, ncol=ncol):
                last = None
                for j in range(8):
                    bb = b0 if j < 4 else b1
                    last = e.transpose(ps[:ncol, bb, (j % 4) * 128:(j % 4 + 1) * 128], fin[:, j, c0_:c1_], ident[:])
                return last
            A("pe", _mk(tr_fin, 1.0), r=[("fin", j) for j in range(8)] + [("c", "ident")], w=[("ps", b0), ("ps", b1)])
            A("act", ACT(stg[:ncol, 0, 0:512], ps[:ncol, b0, :], AF.Copy), r=[("ps", b0)], w=[("stg", 0)])
            A("act", ACT(stg[:ncol, 0, 512:1024], ps[:ncol, b1, :], AF.Copy), r=[("ps", b1), ("stg", 0)],
              w=[("stg", 0)])
            if hf == 0:
                P.dma("sp", [(h_s[:, :], stg[0:16, 0, :]), (c_s[:, 2, :], stg[16:32, 0, :])], r=[("stg", 0)],
                      w=[("out", "hs")], key="so0")
                out_res.append(("out", "hs"))
            else:
                P.dma("sp", [(h_p[:, :], stg[0:1, 0, :]), (c_p[:, :], stg[1:4, 0, :])], r=[("stg", 0)],
                      w=[("out", "hp")], key="so0")
                out_res.append(("out", "hp"))
            for j in range(8):
                if j % 4 == 0:
                    spb_ = load_wa(pbv, (j // 4) * 512, 512)
                    sgb2 = load_wa(winv, 5120 + (j // 4) * 512, 512)
                jl = j % 4
                for t, (off, n) in enumerate(TTl):
                    bp, bg = P.nb(), P.nb()
                    A("pe", MM(ps[:, bp, :n], [(wa[:, spb_, k, jl * 128:(jl + 1) * 128], ybv[:, k, off:off + n])
                                               for k in range(8)]),
                      r=[("yb", k, t) for k in range(8)] + [("wa", spb_)], w=[("ps", bp)])
                    A("pe", MM(ps[:, bg, :n], [(wa[:, sgb2, k, jl * 128:(jl + 1) * 128], hT[:, k, off:off + n])
                                               for k in range(8)]), r=hTr(t) + [("wa", sgb2)], w=[("ps", bg)])
                    sl = as_next()
                    A("act", ACT(actS[:, sl, :n], ps[:, bg, :n], AF.Tanh, scale=0.5), r=[("ps", bg)], w=[("actS", sl)])
                    A("dve", STT(actS[:, sl, :n], actS[:, sl, :n], 1.0, ps[:, bp, :n], ALU.add, ALU.mult),
                      r=[("actS", sl), ("ps", bp)], w=[("actS", sl)])
                    A("dve", TT(mb_[:, j, off:off + n], actS[:, sl, :n], mb_[:, j, off:off + n], ALU.add),
                      r=[("actS", sl), ("m", j, t)], w=[("m", j, t)])
            so0 = load_wa(wov, 0, 512)
            so1 = load_wa(wov, 512, 512)
            for t, (off, n) in enumerate(TTl):
                for j in range(8):
                    so = so0 if j < 4 else so1
                    jl = j % 4
                    b = P.nb()
                    A("pe", MM(ps[:, b, :n], [(wa[:, so, k, jl * 128:(jl + 1) * 128], mb_[:, k, off:off + n])
                                              for k in range(8)]),
                      r=[("m", k, t) for k in range(8)] + [("wa", so)], w=[("ps", b)])
                    A("dve", STT(xT[:, j, off:off + n], ps[:, b, :n], 0.5, xT[:, j, off:off + n], ALU.mult, ALU.add),
                      r=[("ps", b), ("xT", j, t)], w=[("xT", j, t)])

        oneb = sb("oneb", (128, 1), F32)
        A("pool", MSET(oneb[:], 1.0), w=[("c", "oneb")])
        epsb = sb("epsb", (128, 1), F32)
        A("pool", MSET(epsb[:], EPS), w=[("c", "epsb")])

        for hf in range(2):
            TTl = [(0, 512), (512, 512)] + ([(HALF, NSAMP)] if hf == 0 else [])
            tiles = list(range(8)) + ([8] if hf == 0 else [])
            for i in tiles:
                npt = 128 if i < 8 else NSAMP
                off = i * 128 if i < 8 else HALF
                t = off // 512
                s = (i % 4) if hf == 0 else 2 + (i % 2)
                src = xp[hf * HALF + i * 128:hf * HALF + (i + 1) * 128, :] if i < 8 else xs[:, :]
                P.dma("sp", [(SLOT[s][:npt, :], src)], w=SLOTR[s], key="stg%d" % s)
                for hh in range(2):
                    b = P.nb()

                    def tr_x(e, b=b, hh=hh, s=s, npt=npt):
                        last = None
                        for q in range(4):
                            j = hh * 4 + q
                            last = e.transpose(ps[:, b, q * npt:(q + 1) * npt], SLOT[s][:npt, j * 128:(j + 1) * 128],
                                               ident[:npt, :npt])
                        return last
                    A("pe", _mk(tr_x, 1.2), r=SLOTR[s] + [("c", "ident")], w=[("ps", b)])
                    A("act" if hh == 0 else "dve",
                      (ACT(xT[:, hh * 4:(hh + 1) * 4, off:off + npt],
                           ps[:, b, 0:4 * npt].rearrange("p (q n) -> p q n", q=4), AF.Copy) if hh == 0 else
                       CP(xT[:, hh * 4:(hh + 1) * 4, off:off + npt],
                          ps[:, b, 0:4 * npt].rearrange("p (q n) -> p q n", q=4))),
                      r=[("ps", b)], w=[("xT", j, t) for j in range(hh * 4, hh * 4 + 4)])
            if stage >= 1:
                rmsnorm(TTl, 0)
            if hf == 0:
                late_setup()
            if stage >= 1:
                ffn(TTl, f1g, f1u, f1d)
            if stage >= 2:
                P.fence("ARENA")
                P.fence("WBMEM")
                rmsnorm(TTl, 1)
                mixer(hf, TTl)
                P.fence("ARENA")
                P.fence("WBMEM")
            if stage >= 3:
                rmsnorm(TTl, 2)
                ffn(TTl, f2g, f2u, f2d)
            if debug and hf == 0:
                P.dma("sp", [(dbg[:, :], xT[:].rearrange("p j n -> p (j n)"))], r=xTr(0) + xTr(1) + xTr(2),
                      w=[("out", "dbg")], key="dbg")
                out_res.append(("out", "dbg"))
            P.dma("sp", [(nbuf[:, :], bass.AP(fnm.tensor, 0, [[0, 128], [1, 1024]]))], w=[("c", "nbuf")], key="nbuf")
            for i in tiles:
                npt = 128 if i < 8 else NSAMP
                off = i * 128 if i < 8 else HALF
                t = off // 512
                s = (i % 2) if hf == 0 else (i % 4)
                q_ = i % 2
                sv_ = ssv2[:, q_, :]
                b0, b1 = P.nb(), P.nb()

                def tr_o(e, b0=b0, b1=b1, off=off, npt=npt):
                    last = None
                    for j in range(8):
                        bb = b0 if j < 4 else b1
                        last = e.transpose(ps[:npt, bb, (j % 4) * 128:(j % 4 + 1) * 128], xT[:, j, off:off + npt],
                                           ident[:])
                    return last
                A("pe", _mk(tr_o, 2.0), r=xTr(t) + [("c", "ident")], w=[("ps", b0), ("ps", b1)])
                A("act", ACT(rstd[:npt, :], ps[:npt, b0, :], AF.Square, accum_out=sv_[:npt, 0:1]),
                  r=[("ps", b0)], w=[("rstd",), ("ssv", q_)])
                A("act", ACT(rstd[:npt, :], ps[:npt, b1, :], AF.Square, accum_out=sv_[:npt, 1:2]),
                  r=[("ps", b1)], w=[("rstd",), ("ssv", q_)])
                A("dve", TT(sv_[:npt, 2:3], sv_[:npt, 0:1], sv_[:npt, 1:2], ALU.add), r=[("ssv", q_)], w=[("ssv", q_)])
                A("dve", TS(sv_[:npt, 2:3], sv_[:npt, 2:3], 1.0 / 1024.0, EPS, ALU.mult, ALU.add), r=[("ssv", q_)],
                  w=[("ssv", q_)])
                A("act", ACT(sv_[:npt, 3:4], sv_[:npt, 2:3], AF.Sqrt), r=[("ssv", q_)], w=[("ssv", q_)])
                A("dve", lambda e, npt=npt, sv_=sv_: e.reciprocal(out=sv_[:npt, 3:4], in_=sv_[:npt, 3:4]),
                  r=[("ssv", q_)], w=[("ssv", q_)])
                A("dve", STT(SLOT[s][:npt, 0:512], ps[:npt, b0, :], sv_[:npt, 3:4], fnb[:npt, 0:512], ALU.mult,
                             ALU.mult), r=[("ps", b0), ("ssv", q_), ("c", "nbuf")], w=SLOTR[s])
                A("dve", STT(SLOT[s][:npt, 512:1024], ps[:npt, b1, :], sv_[:npt, 3:4], fnb[:npt, 512:1024], ALU.mult,
                             ALU.mult), r=[("ps", b1), ("ssv", q_), ("c", "nbuf")] + SLOTR[s], w=SLOTR[s])
                dst = y_p[hf * HALF + i * 128:hf * HALF + (i + 1) * 128, :] if i < 8 else y_s[:, :]
                rk = ("out", "y%d_%d" % (hf, i))
                P.dma("sp", [(dst, SLOT[s][:npt, :])], r=SLOTR[s], w=[rk], key="so%d" % s)
                out_res.append(rk)
        A("sp", None, r=out_res)
        P.emit(nc, stack)
    return nc


_CACHE = {}


def make_in_maps(inp):
    f = lambda a: np.ascontiguousarray(np.asarray(a, dtype=np.float32))
    vec = np.stack([f(inp["ffn1_norm"])[0], f(inp["mix_norm"])[0], f(inp["ffn2_norm"])[0],
                    f(inp["conv_w"])[0, 0], f(inp["conv_w"])[0, 1], f(inp["conv_w"])[0, 2], f(inp["conv_w"])[0, 3],
                    f(inp["conv_b"])[0], f(inp["lru_b_r"])[0].reshape(-1), f(inp["lru_b_i"])[0].reshape(-1),
                    f(inp["lru_lambda"])[0]], axis=0)
    shared = {
        "vecs": f(vec), "gvn": f(inp["gmlp_v_norm"]).reshape(1, D), "fnm": f(inp["final_norm"]).reshape(1, D),
        "spb": f(inp["spatial_b"]).reshape(1, D), "spw": f(inp["spatial_w"])[0],
        "wr": f(inp["lru_w_r"])[0], "wi": f(inp["lru_w_i"])[0],
        "f1g": f(inp["ffn1_w_gate"])[0], "f1u": f(inp["ffn1_w_up"])[0], "f1d": f(inp["ffn1_w_down"])[0],
        "win": f(inp["w_in"])[0], "pa": f(inp["proj_a"])[0], "pb": f(inp["proj_b"])[0], "wo": f(inp["w_out"])[0],
        "f2g": f(inp["ffn2_w_gate"])[0], "f2u": f(inp["ffn2_w_up"])[0], "f2d": f(inp["ffn2_w_down"])[0],
    }
    xpr = f(inp["x_prompt"])
    xsm = f(inp["x_sample"])
    slh = f(inp["state_lru_h"])
    scv = f(inp["state_conv"])
    maps = []
    for c in range(8):
        m = dict(shared)
        m["xp"] = xpr[c]
        m["xs"] = f(xsm[c * NSAMP:(c + 1) * NSAMP, 0, :])
        m["sh"] = f(slh[0, c * NSAMP:(c + 1) * NSAMP, :])
        m["sc"] = f(scv[0, c * NSAMP:(c + 1) * NSAMP])
        maps.append(m)
    return maps


def kernel(**inputs):
    if "nc" not in _CACHE:
        _CACHE["nc"] = build()
    nc = _CACHE["nc"]
    maps = make_in_maps(inputs)
    res = run_bass_kernel_spmd(nc, maps, core_ids=list(range(8)))
    R = res.results
    y_prompt = np.stack([R[c]["y_p"] for c in range(8)], axis=0)
    y_sample = np.concatenate([R[c]["y_s"] for c in range(8)], axis=0)[:, None, :]
    h_pr = np.concatenate([R[c]["h_p"] for c in range(8)], axis=0)[None]
    c_pr = np.stack([R[c]["c_p"] for c in range(8)], axis=0)[None]
    h_sm = np.concatenate([R[c]["h_s"] for c in range(8)], axis=0)[None]
    c_sm = np.concatenate([R[c]["c_s"] for c in range(8)], axis=0)[None]
    v_sm = np.concatenate([R[c]["v_s"] for c in range(8)], axis=0)[None, :, None, :]
    return (y_prompt.astype(np.float32), y_sample.astype(np.float32), h_pr.astype(np.float32),
            c_pr.astype(np.float32), h_sm.astype(np.float32), c_sm.astype(np.float32), v_sm.astype(np.float32))
```

```python
import contextlib
import numpy as np
import concourse.bass as bass
import concourse.mybir as mybir
from concourse.bass_utils import run_bass_kernel_spmd

F32 = mybir.dt.float32
BF16 = mybir.dt.bfloat16
AF = mybir.ActivationFunctionType
ALU = mybir.AluOpType

D = 1024
DFF = 2752
SEQ = 2048
NSAMP = 16
HALF = 1024
NTW = HALF + NSAMP
EPS = 1e-6
NV = 11
NATIVE_GELU = True
PE_RATE = 2000.0
EW_SCALE = 1.0
TBL_PEN = 1.28
STAGGER = 0
GK = 0.7978845608028654
GC = 0.044715 * GK


class Op:
    __slots__ = ("idx", "eng", "fn", "dmas", "deps", "has_dep", "sig", "key")


class Prog:
    tie = 0.0
    lat = 0.12
    dma_lat = 2.2
    dma_bw = 150000.0
    window = 150

    def __init__(self):
        self.ops = []
        self.res = {}
        self.bank = 0

    def _add(self, op, reads, writes):
        op.idx = len(self.ops)
        self.ops.append(op)
        deps = set()
        extra = set()
        for r in list(reads) + list(writes):
            if isinstance(r, tuple):
                if r[0] in ("hid", "vtm", "ya", "yb", "m"):
                    extra.add("ARENA")
                if r[0] in ("wb", "L", "xb", "xbh"):
                    extra.add("WBMEM")
        for r in list(reads) + list(extra):
            st = self.res.setdefault(r, [[], []])
            deps.update(st[0])
            st[1].append(op.idx)
        for w in writes:
            st = self.res.setdefault(w, [[], []])
            deps.update(st[0])
            deps.update(st[1])
            st[0] = [op.idx]
            st[1] = []
        deps.discard(op.idx)
        op.deps = deps
        op.has_dep = False
        return op

    def op(self, eng, fn, r=(), w=()):
        o = Op()
        o.eng = eng
        o.fn = fn
        o.dmas = None
        o.key = None
        o.sig = None
        return self._add(o, r, w)

    def dma(self, eng, pairs, r=(), w=(), key=None, **kw):
        o = Op()
        o.eng = eng
        o.fn = kw
        o.dmas = pairs
        o.key = key
        o.sig = None
        return self._add(o, r, w)

    def fence(self, name):
        st = self.res.setdefault(name, [[], []])
        o = Op()
        o.eng = None
        o.fn = None
        o.dmas = None
        o.key = None
        o.sig = None
        o.idx = len(self.ops)
        self.ops.append(o)
        o.deps = set(st[0]) | set(st[1])
        o.has_dep = False
        st[0] = [o.idx]
        st[1] = []

    def nb(self):
        b = self.bank
        self.bank = (b + 1) % 8
        return b

    def schedule(self, window=600):
        AFT = AF
        tblsets = {AFT.Gelu_apprx_tanh: {11}, AFT.Tanh: {0, 11, 18, 2}, AFT.Exp: {0}, AFT.Sqrt: {3}, AFT.Silu: {18},
                   AFT.Sigmoid: {2}}
        n = len(self.ops)
        succ = [[] for _ in range(n)]
        indeg = [0] * n
        for o in self.ops:
            indeg[o.idx] = len(o.deps)
            for d in o.deps:
                succ[d].append(o.idx)
        finish = [0.0] * n
        est = [0.0] * n
        engs = ("pe", "act", "dve", "pool", "sp")
        def ocost(o):
            if o.eng is None:
                return 0.0
            if o.dmas is not None:
                return 2.5
            return getattr(o.fn, "cost", 0.1) if o.fn is not None else 0.02
        bl = [0.0] * n
        for i in range(n - 1, -1, -1):
            m_ = 0.0
            for j in succ[i]:
                if bl[j] > m_:
                    m_ = bl[j]
            bl[i] = m_ + ocost(self.ops[i])
        TIE = self.tie
        free = {e: 0.0 for e in engs}
        ready = {e: [] for e in engs}
        order = {e: [] for e in engs}
        cur_tbl = [None]
        done = [False] * n
        low = 0
        nsched = [0]
        LAT = self.lat

        def release(i):
            for j in succ[i]:
                oj = self.ops[j]
                t_ = finish[i] + LAT
                if t_ > est[j]:
                    est[j] = t_
                indeg[j] -= 1
                if indeg[j] == 0:
                    if oj.eng is None:
                        finish[j] = est[j] - LAT
                        done[j] = True
                        nsched[0] += 1
                        release(j)
                    else:
                        ready[oj.eng].append(j)

        for o in list(self.ops):
            if indeg[o.idx] == 0 and not done[o.idx]:
                if o.eng is None:
                    done[o.idx] = True
                    nsched[0] += 1
                    release(o.idx)
                else:
                    ready[o.eng].append(o.idx)
        while nsched[0] < n:
            while low < n and done[low]:
                low += 1
            lim = low + window
            best = None
            for e in engs:
                fe = free[e]
                for i in ready[e]:
                    if i >= lim:
                        continue
                    st = est[i] if est[i] > fe else fe
                    pen = 0.0
                    if e == "act":
                        fn = self.ops[i].fn
                        tb = getattr(fn, "tbl", None) if fn is not None else None
                        acc = tblsets.get(tb)
                        if acc is not None and cur_tbl[0] not in acc:
                            pen = TBL_PEN
                    key = (st + pen, i)
                    if best is None or key[0] < best[0][0] - TIE or \
                            (key[0] <= best[0][0] + TIE and i < best[2]):
                        best = (key, e, i, st, pen)
            if best is None:
                raise RuntimeError("scheduler stuck")
            _, e, i, st, pen = best
            o = self.ops[i]
            ready[e].remove(i)
            if o.dmas is not None:
                nbytes = 0
                for (dst, src) in o.dmas:
                    nb_ = 1
                    for d_ in src.shape:
                        nb_ *= int(d_)
                    nbytes += nb_ * 4
                occ = 0.06 * len(o.dmas)
                dur = self.dma_lat + nbytes / self.dma_bw
            else:
                c = getattr(o.fn, "cost", 0.1) if o.fn is not None else 0.02
                if e in ("act", "dve"):
                    c = c * EW_SCALE
                occ = c + pen
                dur = occ
                if e == "act" and pen > 0:
                    cur_tbl[0] = min(tblsets.get(o.fn.tbl))
            free[e] = st + occ
            finish[i] = st + dur
            done[i] = True
            nsched[0] += 1
            order[e].append(i)
            release(i)
        self.makespan = max(finish) if n else 0.0
        return order

    def emit(self, nc, stack, sched=True):
        engs = ("pe", "act", "dve", "pool", "sp")
        sems = {}

        def sem(name):
            if name not in sems:
                sems[name] = stack.enter_context(nc.semaphore(name))
            return sems[name]

        if sched:
            order = self.schedule(self.window)
        else:
            order = {e: [o.idx for o in self.ops if o.eng == e] for e in engs}
        pos = {}
        for e in engs:
            for p_, i in enumerate(order[e]):
                pos[i] = p_
        vcache = {}

        def frontier(idxs):
            best = {}
            dm = set()
            for d in idxs:
                dd = self.ops[d]
                if dd.eng is None:
                    if d not in vcache:
                        vcache[d] = frontier(dd.deps)
                    b2, d2 = vcache[d]
                    for e_, i_ in b2.items():
                        if e_ not in best or pos[best[e_]] < pos[i_]:
                            best[e_] = i_
                    dm |= d2
                elif dd.dmas is not None:
                    dm.add(d)
                else:
                    if dd.eng not in best or pos[best[dd.eng]] < pos[d]:
                        best[dd.eng] = d
            return best, dm

        import sys
        sys.setrecursionlimit(10000)
        eff = {}
        for o in self.ops:
            if o.eng is None:
                continue
            b_, d_ = frontier(o.deps)
            lst = []
            for e_, i_ in b_.items():
                if e_ == "pe" and o.eng == "pe" and o.dmas is None:
                    continue
                lst.append(i_)
            lst.extend(d_)
            eff[o.idx] = lst
            for i_ in lst:
                self.ops[i_].has_dep = True
        seq = {e: 0 for e in engs}
        kcnt = {}
        for o in [self.ops[i] for e in engs for i in order[e]]:
            if o.dmas is not None:
                k = "d_" + o.key
                kcnt[k] = kcnt.get(k, 0) + 16 * len(o.dmas)
                o.sig = (k, kcnt[k])
                sem(k)
            elif o.has_dep:
                seq[o.eng] += 1
                o.sig = ("e_" + o.eng, seq[o.eng])
                sem("e_" + o.eng)
        block = stack.enter_context(nc.Block())
        deco = {"pe": block.tensor, "act": block.scalar, "dve": block.vector,
                "pool": block.gpsimd, "sp": block.sync}
        for en in engs:
            myops = [self.ops[i] for i in order[en]]
            if not myops:
                continue

            def body(e, myops=myops, en=en):
                waited = {}
                for o in myops:
                    need = {}
                    for d in eff[o.idx]:
                        s_, v = self.ops[d].sig
                        if need.get(s_, 0) < v:
                            need[s_] = v
                    for s_, v in need.items():
                        if waited.get(s_, 0) >= v:
                            continue
                        e.wait_ge(sems[s_], v)
                        waited[s_] = v
                    if o.dmas is not None:
                        for (dst, src) in o.dmas:
                            e.dma_start(out=dst, in_=src, **o.fn).then_inc(sems[o.sig[0]], 16)
                    elif o.fn is not None:
                        last = o.fn(e)
                        if o.sig is not None:
                            last.then_inc(sems[o.sig[0]], 1)

            deco[en](body)


def _nfree(ap):
    n = 1
    for d in ap.shape[1:]:
        n *= int(d)
    return n


def _mk(f, cost, tbl=None, scan=False):
    f.cost = cost
    f.tbl = tbl
    return f


def ACT(out, in_, func, **kw):
    return _mk(lambda e: e.activation(out=out, in_=in_, func=func, **kw), 0.22 + _nfree(out) / 1200.0, tbl=func)


def TT(out, in0, in1, op):
    return _mk(lambda e: e.tensor_tensor(out=out, in0=in0, in1=in1, op=op), 0.07 + _nfree(out) / 960.0)


def STT(out, in0, scalar, in1, op0, op1):
    return _mk(lambda e: e.scalar_tensor_tensor(out=out, in0=in0, scalar=scalar, in1=in1, op0=op0, op1=op1),
               0.07 + _nfree(out) / 960.0)


def TS(out, in0, s1, s2, op0, op1):
    return _mk(lambda e: e.tensor_scalar(out=out, in0=in0, scalar1=s1, scalar2=s2, op0=op0, op1=op1),
               0.07 + _nfree(out) / 960.0)


def TS1(out, in_, s, op):
    return _mk(lambda e: e.tensor_single_scalar(out=out, in_=in_, scalar=s, op=op), 0.07 + _nfree(out) / 960.0)


def CP(out, in_):
    return _mk(lambda e: e.tensor_copy(out=out, in_=in_), 0.07 + _nfree(out) / 960.0)


def PCP(out, in_):
    return _mk(lambda e: e.tensor_copy(out=out, in_=in_), 0.12 + _nfree(out) / 480.0)


def PSTT(out, in0, scalar, in1, op0, op1):
    return _mk(lambda e: e.scalar_tensor_tensor(out=out, in0=in0, scalar=scalar, in1=in1, op0=op0, op1=op1),
               0.12 + _nfree(out) / 480.0)


def MSET(ap, v):
    return _mk(lambda e: e.memset(ap, v), 0.07 + _nfree(ap) / 1900.0)


def MM(out, pairs):
    def f(e):
        n = len(pairs)
        last = None
        for i, (l, r) in enumerate(pairs):
            last = e.matmul(out, l, r, start=(i == 0), stop=(i == n - 1))
        return last
    nn = max(_nfree(out), 64)
    return _mk(f, len(pairs) * (nn / PE_RATE + 0.004))


def build(stage=99, debug=False):
    nc = bass.Bass("TRN2", target_bir_lowering=False)

    def din(name, shape):
        return nc.dram_tensor(name, list(shape), F32, kind="ExternalInput").ap()

    def dout(name, shape):
        return nc.dram_tensor(name, list(shape), F32, kind="ExternalOutput").ap()

    xp = din("xp", (SEQ, D))
    xs = din("xs", (NSAMP, D))
    sh = din("sh", (NSAMP, D))
    sc = din("sc", (NSAMP, 3, D))
    vecs = din("vecs", (NV, D))
    gvn = din("gvn", (1, D))
    fnm = din("fnm", (1, D))
    spb = din("spb", (1, D))
    spw = din("spw", (8, 128, 128))
    wr = din("wr", (16, 64, 64))
    wi = din("wi", (16, 64, 64))
    f1g = din("f1g", (D, DFF))
    f1u = din("f1u", (D, DFF))
    f1d = din("f1d", (DFF, D))
    win = din("win", (D, 6 * D))
    pa = din("pa", (D, D))
    pb = din("pb", (D, D))
    wo = din("wo", (D, D))
    f2g = din("f2g", (D, DFF))
    f2u = din("f2u", (D, DFF))
    f2d = din("f2d", (DFF, D))

    y_p = dout("y_p", (SEQ, D))
    y_s = dout("y_s", (NSAMP, D))
    h_p = dout("h_p", (1, D))
    c_p = dout("c_p", (3, D))
    h_s = dout("h_s", (NSAMP, D))
    c_s = dout("c_s", (NSAMP, 3, D))
    v_s = dout("v_s", (NSAMP, D))
    if debug:
        dbg = dout("dbg", (128, 8 * NTW))

    st_ = contextlib.ExitStack()
    with st_ as stack:
        def sb(name, shape, dt):
            return stack.enter_context(nc.sbuf_tensor(name, list(shape), dt))

        xT = sb("xT", (128, 8, NTW), F32)
        hT = sb("hT", (128, 8, NTW), BF16)
        arena = sb("arena", (128, 25856), BF16)
        wa = sb("wa", (128, 4, 8, 512), BF16)
        wbm = sb("wbm", (128, 8768), F32)
        stg = sb("stg", (128, 2, 1024), F32)
        vg = sb("vg", (128, 1024), F32)
        actS = sb("actS", (128, 2, 512), F32)
        rstd = sb("rstd", (128, 512), F32)
        ident = sb("ident", (128, 128), F32)
        maskf = sb("maskf", (128, 128), F32)
        onesb = sb("onesb", (128, 128), BF16)
        onesrow = sb("onesrow", (3, 128), BF16)
        WsT = sb("WsT", (128, 8, 128), BF16)
        wrbd = sb("wrbd", (128, 8, 128), BF16)
        wibd = sb("wibd", (128, 8, 128), BF16)
        vT = sb("vT", (128, 8, 16), F32)
        tv = sb("tv", (128, 8, 8), F32)
        nbuf = sb("nbuf", (128, 1024), F32)
        gvnb = nbuf
        fnb = nbuf
        stT = sb("stT", (128, 8, 64), F32)
        bhi = sb("bhi", (3, 1024), BF16)
        blo = sb("blo", (1, 1024), BF16)
        ws00 = sb("ws00", (16, 8), F32)
        Dg = sb("Dg", (16, 8, 16), BF16)
        ssv2 = sb("ssv2", (128, 2, 4), F32)
        ssq = sb("ssq", (128, 3, 9), F32)
        hc = sb("hc", (128, 8), F32)
        cv = sb("cv", (128, 8, 3), F32)
        fin = sb("fin", (128, 8, 36), F32)
        ps = stack.enter_context(nc.psum_tensor("ps", [128, 8, 512], F32))

        hid = arena[:, 0:22 * NTW].rearrange("p (f n) -> p f n", n=NTW)
        vtm = arena[:, 0:8192].rearrange("p (i n) -> p i n", n=1024)
        vtms = arena[:, 8192:9216]
        ybv = arena[:, 0:8 * NTW].rearrange("p (j n) -> p j n", n=NTW)
        ya = arena[:, 9216:9216 + 8 * NTW].rearrange("p (j n) -> p j n", n=NTW)
        mb_ = arena[:, 17536:17536 + 8 * NTW].rearrange("p (j n) -> p j n", n=NTW)
        wbb = wbm[:, 0:8448].bitcast(BF16)
        wb = wbb.rearrange("p (s f n) -> p s f n", s=3, f=22)
        XBW = 1056
        xb = wbm[:, 0:2 * XBW].rearrange("p (s n) -> p s n", n=XBW)
        Lsets = []
        xcbs = []
        for S_ in range(2):
            base = 2 * XBW + S_ * 3328
            Lt = wbm[:, base:base + 6 * 512].rearrange("p (s n) -> p s n", n=512)
            Lsets.append(tuple(Lt[:, i, :] for i in range(6)))
            xcbs.append(wbm[:, base + 3072:base + 3328].bitcast(BF16))
        Lsm = sb("Lsm", (128, 7, NSAMP), F32)
        Lsets.append(tuple(Lsm[:, i, :] for i in range(6)))
        xcbs.append(Lsm[:, 6, :].bitcast(BF16))

        if debug:
            try:
                print("SBUF remaining", nc.sbuf_bytes_remaining, "top", nc.sbuf_top, "base", nc.sbuf_base,
                      "part", nc.SBUF_PARTITION_SIZE_BYTES)
            except Exception as ex:
                print("sbuf introspection failed", ex)
        P = Prog()
        A = P.op
        actSf = actS[:].rearrange("p s n -> p (s n)")
        SLOT = [stg[:, 0, :], stg[:, 1, :], vg[:, :], actSf]
        SLOTR = [[("stg", 0)], [("stg", 1)], [("vg", 0), ("vg", 1)], [("actS", 0), ("actS", 1)]]

        wa_ctr = [0]

        def wa_next():
            s = wa_ctr[0]
            wa_ctr[0] = (s + 1) % 4
            return s

        wb_ctr = [0]

        def wb_next():
            s = wb_ctr[0]
            wb_ctr[0] = (s + 1) % 3
            return s

        as_ctr = [0]

        def as_next():
            s = as_ctr[0]
            as_ctr[0] = (s + 1) % 2
            return s

        def load_wa(src3, c0, cw):
            s = wa_next()
            P.dma("pool", [(wa[:, s, :, 0:cw], src3[:, :, c0:c0 + cw])], w=[("wa", s)], key="wa%d" % s)
            return s

        def kview(w):
            return w.rearrange("(k p) n -> p k n", p=128)

        A("pool", MSET(ident[:], 1.0), w=[("c", "ident")])
        A("pool", lambda e: e.affine_select(out=ident[:], in_=ident[:], pattern=[[-1, 128]],
                                            compare_op=ALU.is_equal, fill=0.0, base=0, channel_multiplier=1),
          r=[("c", "ident")], w=[("c", "ident")])
        A("pool", MSET(maskf[:], 1.0), w=[("c", "maskf")])
        A("pool", lambda e: e.affine_select(out=maskf[:], in_=maskf[:], pattern=[[1, 128]],
                                            compare_op=ALU.is_ge, fill=0.0, base=0, channel_multiplier=-1),
          r=[("c", "maskf")], w=[("c", "maskf")])
        A("pool", MSET(onesb[:], 1.0 / 1024.0), w=[("c", "onesb")])
        A("pool", MSET(onesrow[:], 1.0), w=[("c", "onesrow")])
        A("pool", MSET(wrbd[:], 0.0), w=[("c", "wrbd")])
        A("pool", MSET(wibd[:], 0.0), w=[("c", "wibd")])
        A("pool", MSET(hc[:], 0.0), w=[("hc",)])
        A("pool", MSET(ssq[:], 1.0), w=[("ssq", i) for i in range(9)] + [("ssq", "n"), ("ssq", "r")])
        for (wsrc, wdst, nm) in ((wr, wrbd, "wrbd"), (wi, wibd, "wibd")):
            v = wsrc.rearrange("(j e) i o -> e i j o", e=2)
            P.dma("pool", [(wdst[0:64, :, 0:64], v[0]), (wdst[64:128, :, 64:128], v[1])],
                  r=[], w=[("c", nm)], key=nm)
        P.dma("sp", [(stg[0:NV, 0, :], vecs[:, :])], w=[("stg", 0)], key="stg0")
        bsrc = vg[0:1, :]
        bhf = stg[0:1, 1, :]
        P.dma("sp", [(ws00[:, :], bass.AP(spw.tensor, 0, [[0, 16], [16384, 8]]))], w=[("c", "ws00")],
              key="ws00", allow_slow_non_contiguous=True)
        b = P.nb()

        def tr_vecs(e, b=b):
            last = None
            for j in range(8):
                last = e.transpose(ps[:, b, j * 16:j * 16 + NV], stg[0:NV, 0, j * 128:(j + 1) * 128],
                                   ident[0:NV, 0:NV])
            return last
        A("pe", _mk(tr_vecs, 1.0), r=[("stg", 0), ("c", "ident")], w=[("ps", b)])
        A("act", ACT(vT[:, :, 0:NV], ps[:, b, 0:128].rearrange("p (j n) -> p j n", n=16)[:, :, 0:NV], AF.Copy),
          r=[("ps", b)], w=[("vT",)])
        def late_setup():
            lam = vT[:, :, 10]
            t0, t1, t2, t3, t4 = (tv[:, :, i] for i in range(5))
            A("dve", TS1(t0, lam, -1.0, ALU.mult), r=[("vT",)], w=[("tv",)])
            A("dve", TT(t0, t0, lam, ALU.max), r=[("vT",), ("tv",)], w=[("tv",)])
            A("act", ACT(t0, t0, AF.Exp, scale=-1.0), r=[("tv",)], w=[("tv",)])
            A("dve", TS1(t1, t0, 2.0, ALU.add), r=[("tv",)], w=[("tv",)])
            A("dve", lambda e: e.reciprocal(out=t1, in_=t1), r=[("tv",)], w=[("tv",)])
            A("dve", TT(t1, t0, t1, ALU.mult), r=[("tv",)], w=[("tv",)])
            A("dve", TT(t2, t1, t1, ALU.mult), r=[("tv",)], w=[("tv",)])
            A("dve", TS(t3, t2, 1.0 / 9.0, 1.0 / 7.0, ALU.mult, ALU.add), r=[("tv",)], w=[("tv",)])
            for cst in (1.0 / 5.0, 1.0 / 3.0, 1.0):
                A("dve", TT(t3, t3, t2, ALU.mult), r=[("tv",)], w=[("tv",)])
                A("dve", TS1(t3, t3, cst, ALU.add), r=[("tv",)], w=[("tv",)])
            A("dve", STT(t3, t1, 2.0, t3, ALU.mult, ALU.mult), r=[("tv",)], w=[("tv",)])
            A("dve", TS(t4, lam, -1.0, 0.0, ALU.mult, ALU.max), r=[("tv",), ("vT",)], w=[("tv",)])
            A("dve", TT(t3, t3, t4, ALU.add), r=[("tv",)], w=[("tv",)])
            A("dve", TS1(vT[:, :, 11], t3, -8.0, ALU.mult), r=[("tv",)], w=[("vT",)])
            A("dve", TS1(vT[:, :, 12], t3, -4.0, ALU.mult), r=[("tv",)], w=[("vT",)])
            A("dve", TS1(vT[:, :, 13], t3, -2.0, ALU.mult), r=[("tv",)], w=[("vT",)])
            A("dve", TS1(vT[:, :, 14], vT[:, :, 8], 0.5, ALU.mult), r=[("vT",)], w=[("vT",)])
            A("dve", TS1(vT[:, :, 15], vT[:, :, 9], 0.5, ALU.mult), r=[("vT",)], w=[("vT",)])
            P.dma("sp", [(bsrc, spb[:, :])], w=[("vg", 0), ("vg", 1)], key="bsrc")
            A("dve", CP(bhi[0:1, :], bsrc), r=[("vg", 0), ("vg", 1)], w=[("c", "bhi")])
            A("dve", CP(bhf, bhi[0:1, :]), r=[("c", "bhi")], w=[("stg", 1)])
            A("dve", TT(bsrc, bsrc, bhf, ALU.subtract), r=[("vg", 0), ("vg", 1), ("stg", 1)], w=[("vg", 0), ("vg", 1)])
            A("dve", CP(blo[:], bsrc), r=[("vg", 0), ("vg", 1)], w=[("c", "blo")])
            P.dma("sp", [(bhi[1:2, :], blo[:])], r=[("c", "blo")], w=[("c", "bhi1")], key="bmid")
            A("dve", CP(bhf, blo[:]), r=[("c", "blo")], w=[("stg", 1)])
            A("dve", TT(bsrc, bsrc, bhf, ALU.subtract), r=[("vg", 0), ("vg", 1), ("stg", 1)], w=[("vg", 0), ("vg", 1)])
            A("dve", CP(blo[:], bsrc), r=[("vg", 0), ("vg", 1), ("c", "bhi1")], w=[("c", "blo")])
            P.dma("sp", [(bhi[2:3, :], blo[:])], r=[("c", "blo")], w=[("c", "bhi2")], key="blo2")
            for g in range(8):
                A("dve", TS1(Dg[:, g, :], ident[0:16, 0:16], ws00[:, g:g + 1], ALU.mult),
                  r=[("c", "ident"), ("c", "ws00")], w=[("c", "Dg")])
            P.dma("sp", [(stg[:, 1, :].rearrange("p (g s) -> p g s", g=8), spw.rearrange("g t s -> t g s"))],
                  w=[("stg", 1)], key="stg1")
            for hh in range(2):
                b = P.nb()

                def tr_sp(e, b=b, hh=hh):
                    last = None
                    for q in range(4):
                        g = hh * 4 + q
                        last = e.transpose(ps[:, b, q * 128:(q + 1) * 128], stg[:, 1, g * 128:(g + 1) * 128], ident[:])
                    return last
                A("pe", _mk(tr_sp, 1.6), r=[("stg", 1), ("c", "ident")], w=[("ps", b)])
                for q in range(4):
                    A("dve", TT(WsT[:, hh * 4 + q, :], ps[:, b, q * 128:(q + 1) * 128], maskf[:], ALU.mult),
                      r=[("ps", b), ("c", "maskf")], w=[("c", "WsT")])
            P.dma("sp", [(stg[0:16, 0, :], sh[:, :]), (stg[16:64, 0, :], sc.rearrange("t k d -> (t k) d"))],
                  r=[], w=[("stg", 0)], key="stg0")
            b = P.nb()

            def tr_st(e, b=b):
                last = None
                for j in range(8):
                    last = e.transpose(ps[:, b, j * 64:(j + 1) * 64], stg[0:64, 0, j * 128:(j + 1) * 128],
                                       ident[0:64, 0:64])
                return last
            A("pe", _mk(tr_st, 1.6), r=[("stg", 0), ("c", "ident")], w=[("ps", b)])
            A("act", ACT(stT[:], ps[:, b, :].rearrange("p (j n) -> p j n", n=64), AF.Copy), r=[("ps", b)], w=[("stT",)])
            P.dma("sp", [(c_s[:, 0:2, :], sc[:, 1:3, :])], r=[], w=[("out", "cs01")], key="cs01")

        out_res = [("out", "cs01")]

        def hTr(t):
            return [("hT", j, t) for j in range(8)]

        def xTr(t):
            return [("xT", j, t) for j in range(8)]

        def rmsnorm(TTl, col):
            for t, (off, n) in enumerate(TTl):
                A("act", ACT(hT[:, :, off:off + n], xT[:, :, off:off + n], AF.Square), r=xTr(t), w=hTr(t))
                b = P.nb()
                A("pe", MM(ps[:, b, :n], [(onesb[:], hT[:, j, off:off + n]) for j in range(8)]),
                  r=hTr(t) + [("c", "onesb")], w=[("ps", b)])
                A("act", ACT(rstd[:, :n], ps[:, b, :n], AF.Sqrt, bias=epsb[:, :], scale=1.0),
                  r=[("ps", b), ("c", "epsb")], w=[("rstd",)])
                A("dve", _mk(lambda e, n=n: e.reciprocal(out=rstd[:, :n], in_=rstd[:, :n]), 0.07 + n / 960.0),
                  r=[("rstd",)], w=[("rstd",)])
                for j in range(8):
                    A("dve", STT(hT[:, j, off:off + n], xT[:, j, off:off + n], vT[:, j, col:col + 1],
                                 rstd[:, :n], ALU.mult, ALU.mult),
                      r=[("xT", j, t), ("rstd",), ("vT",)], w=[("hT", j, t)])

        def gelu_to(dst, src, n_res_r, n_res_w):
            if NATIVE_GELU:
                A("act", ACT(dst, src, AF.Gelu_apprx_tanh), r=n_res_r, w=n_res_w)
                return 1.0
            A("act", ACT(dst, src, AF.Square), r=n_res_r, w=n_res_w)
            A("act", ACT(dst, dst, AF.Identity, scale=GC, bias=gkb[0:dst.shape[0], :]), r=n_res_w + [("c", "gkb")], w=n_res_w)
            A("dve", TT(dst, dst, src, ALU.mult), r=n_res_r + n_res_w, w=n_res_w)
            A("act", ACT(dst, dst, AF.Tanh), r=n_res_w, w=n_res_w)
            A("dve", STT(dst, dst, 1.0, src, ALU.add, ALU.mult), r=n_res_r + n_res_w, w=n_res_w)
            return 2.0

        def ffn(TTl, wg, wu, wd):
            wgv, wuv = kview(wg), kview(wu)
            groups = [(0, 512), (512, 512), (1024, 512), (1536, 512), (2048, 512), (2560, 192)]
            for (c0, cw) in groups:
                sg_ = load_wa(wgv, c0, cw)
                su_ = load_wa(wuv, c0, cw)
                for fl in range((cw + 127) // 128):
                    f = c0 // 128 + fl
                    fw = min(128, cw - fl * 128)
                    for t, (off, n) in enumerate(TTl):
                        bg, bu = P.nb(), P.nb()
                        A("pe", MM(ps[:fw, bg, :n], [(wa[:, sg_, k, fl * 128:fl * 128 + fw], hT[:, k, off:off + n])
                                                     for k in range(8)]), r=hTr(t) + [("wa", sg_)], w=[("ps", bg)])
                        A("pe", MM(ps[:fw, bu, :n], [(wa[:, su_, k, fl * 128:fl * 128 + fw], hT[:, k, off:off + n])
                                                     for k in range(8)]), r=hTr(t) + [("wa", su_)], w=[("ps", bu)])
                        sl = as_next()
                        A("act", ACT(actS[:fw, sl, :n], ps[:fw, bg, :n], AF.Silu), r=[("ps", bg)], w=[("actS", sl)])
                        A("dve", TT(hid[:fw, f, off:off + n], actS[:fw, sl, :n], ps[:fw, bu, :n], ALU.mult),
                          r=[("actS", sl), ("ps", bu)], w=[("hid", f, t)])
            wd3 = wd[0:2688, :].rearrange("(f p) n -> p f n", p=128)
            passes = [[0], list(range(1, len(TTl)))]
            for tl in passes:
                for mp in range(4):
                    s = wb_next()
                    P.dma("pool", [(wb[:, s, 0:21, :], wd3[:, :, mp * 256:(mp + 1) * 256]),
                                   (wb[0:64, s, 21, :], wd[2688:2752, mp * 256:(mp + 1) * 256])],
                          w=[("wb", s)], key="wb%d" % s)
                    for ml in range(2):
                        m = mp * 2 + ml
                        for t in tl:
                            off, n = TTl[t]
                            b = P.nb()
                            pairs = [(wb[:, s, f, ml * 128:(ml + 1) * 128], hid[:, f, off:off + n]) for f in range(21)]
                            pairs.append((wb[0:64, s, 21, ml * 128:(ml + 1) * 128], hid[0:64, 21, off:off + n]))
                            A("pe", MM(ps[:, b, :n], pairs), r=[("wb", s)] + [("hid", f, t) for f in range(22)],
                              w=[("ps", b)])
                            A("dve", STT(xT[:, m, off:off + n], ps[:, b, :n], 0.5, xT[:, m, off:off + n],
                                         ALU.mult, ALU.add), r=[("ps", b), ("xT", m, t)], w=[("xT", m, t)])

        gkb = sb("gkb", (128, 1), F32)
        A("pool", MSET(gkb[:], GK), w=[("c", "gkb")])

        def mixer(hf, TTl):
            winv = kview(win)
            pav, pbv, wov = kview(pa), kview(pb), kview(wo)
            P.dma("sp", [(nbuf[:, :], bass.AP(gvn.tensor, 0, [[0, 128], [1, 1024]]))], w=[("c", "nbuf")], key="nbuf")
            sv0 = load_wa(winv, 1024, 512)
            sv1 = load_wa(winv, 1536, 512)
            tiles = list(range(8)) + ([8] if hf == 0 else [])
            for i in tiles:
                npt = 128 if i < 8 else NSAMP
                off = i * 128 if i < 8 else HALF
                t = off // 512
                b0, b1 = P.nb(), P.nb()

                def fv(e, b0=b0, b1=b1, off=off, npt=npt):
                    last = None
                    for k in range(8):
                        for (bb, sv) in ((b0, sv0), (b1, sv1)):
                            last = e.matmul(ps[:npt, bb, :], hT[:, k, off:off + npt], wa[:, sv, k, :],
                                            start=(k == 0), stop=(k == 7))
                    return last
                A("pe", _mk(fv, 4.2), r=hTr(t) + [("wa", sv0), ("wa", sv1)], w=[("ps", b0), ("ps", b1)])
                buf = SLOT[2 + (i % 2)]
                bufr = SLOTR[2 + (i % 2)]
                gs = 1.0
                for c, bb in enumerate((b0, b1)):
                    gs = gelu_to(buf[:npt, c * 512:(c + 1) * 512], ps[:npt, bb, :], [("ps", bb)], [bufr[c]])
                vdst = vtm[:, i, :] if i < 8 else vtms[0:NSAMP, :]
                A("act", ACT(mb_[:npt, 0, 0:1024], buf[:npt, :], AF.Square, accum_out=ssq[:npt, 0, i:i + 1]),
                  r=bufr, w=[("m", 0, 0), ("m", 0, 1), ("m", 0, 2), ("ssq", i)])
                A("dve", TT(vdst, buf[:npt, :], gvnb[:npt, :], ALU.mult), r=bufr + [("c", "nbuf")], w=[("vtm", i)])
            nt_ = len(tiles)
            A("dve", TS(ssq[:, 1, 0:nt_], ssq[:, 0, 0:nt_], 1.0 / 1024.0, EPS * gs * gs, ALU.mult, ALU.add),
              r=[("ssq", i) for i in tiles], w=[("ssq", "n")])
            A("act", ACT(ssq[:, 2, 0:nt_], ssq[:, 1, 0:nt_], AF.Sqrt), r=[("ssq", "n")], w=[("ssq", "r")])
            A("dve", lambda e, nt_=nt_: e.reciprocal(out=ssq[:, 2, 0:nt_], in_=ssq[:, 2, 0:nt_]), r=[("ssq", "r")],
              w=[("ssq", "r")])
            for i in tiles:
                if i < 8:
                    A("act", ACT(vtm[:, i, :], vtm[:, i, :], AF.Copy, scale=ssq[:, 2, i:i + 1]),
                      r=[("ssq", "r"), ("vtm", i)], w=[("vtm", i)])
                else:
                    buf = SLOT[2 + (i % 2)]
                    bufr = SLOTR[2 + (i % 2)]
                    A("act", ACT(vtms[0:NSAMP, :], vtms[0:NSAMP, :], AF.Copy, scale=ssq[0:NSAMP, 2, i:i + 1]),
                      r=[("ssq", "r"), ("vtm", i)], w=[("vtm", i)])
                    A("dve", STT(stg[0:NSAMP, 1, :], buf[0:NSAMP, :], ssq[0:NSAMP, 2, i:i + 1], gvnb[0:NSAMP, :],
                                 ALU.mult, ALU.mult), r=bufr + [("ssq", "r"), ("c", "nbuf")], w=[("stg", 1)])
                    P.dma("sp", [(v_s[:, :], stg[0:NSAMP, 1, :])], r=[("stg", 1)], w=[("out", "vs")], key="so1")
                    out_res.append(("out", "vs"))
            su = None
            for g in range(8):
                if g % 4 == 0:
                    su = load_wa(winv, (g // 4) * 512, 512)
                gl = g % 4
                for t, (off, n) in enumerate(TTl):
                    bu, bs = P.nb(), P.nb()
                    A("pe", MM(ps[:, bu, :n], [(wa[:, su, k, gl * 128:(gl + 1) * 128], hT[:, k, off:off + n])
                                               for k in range(8)]), r=hTr(t) + [("wa", su)], w=[("ps", bu)])
                    if n == 512:
                        def fs(e, bs=bs, g=g, off=off):
                            last = None
                            for c in range(4):
                                i = off // 128 + c
                                e.matmul(ps[:, bs, c * 128:(c + 1) * 128], onesrow[0:3, :],
                                         bhi[0:3, g * 128:(g + 1) * 128], start=True, stop=False)
                                last = e.matmul(ps[:, bs, c * 128:(c + 1) * 128], vtm[:, i, g * 128:(g + 1) * 128],
                                                WsT[:, g, :], start=False, stop=True)
                            return last
                        rr = [("vtm", off // 128 + c) for c in range(4)]
                    else:
                        def fs(e, bs=bs, g=g):
                            e.matmul(ps[:, bs, :NSAMP], onesrow[0:3, :],
                                     bhi[0:3, g * 128:g * 128 + 1].to_broadcast([3, NSAMP]), start=True, stop=False)
                            return e.matmul(ps[:, bs, :NSAMP], vtms[0:NSAMP, g * 128:(g + 1) * 128], Dg[:, g, :],
                                            start=False, stop=True)
                        rr = [("vtm", 8)]
                    A("pe", _mk(fs, 0.6), r=rr + [("c", "WsT"), ("c", "bhi"), ("c", "bhi1"), ("c", "bhi2"), ("c", "Dg"), ("c", "onesrow")],
                      w=[("ps", bs)])
                    sl = as_next()
                    gs = gelu_to(actS[:, sl, :n], ps[:, bu, :n], [("ps", bu)], [("actS", sl)])
                    A("dve", STT(ya[:, g, off:off + n], actS[:, sl, :n], 1.0 / gs, ps[:, bs, :n], ALU.mult, ALU.mult),
                      r=[("actS", sl), ("ps", bs)], w=[("ya", g, t)])
            P.fence("ARENA")
            def c_chain(j, spa, sga):
                jl = j % 4
                for t, (off, n) in enumerate(TTl):
                    bp, bg = P.nb(), P.nb()
                    A("pe", MM(ps[:, bp, :n], [(wa[:, spa, k, jl * 128:(jl + 1) * 128], ya[:, k, off:off + n])
                                               for k in range(8)]),
                      r=[("ya", k, t) for k in range(8)] + [("wa", spa)], w=[("ps", bp)])
                    A("pe", MM(ps[:, bg, :n], [(wa[:, sga, k, jl * 128:(jl + 1) * 128], hT[:, k, off:off + n])
                                               for k in range(8)]), r=hTr(t) + [("wa", sga)], w=[("ps", bg)])
                    yield
                    sl = as_next()
                    A("act", ACT(actS[:, sl, :n], ps[:, bg, :n], AF.Tanh, scale=0.5), r=[("ps", bg)], w=[("actS", sl)])
                    yield
                    A("dve", STT(mb_[:, j, off:off + n], actS[:, sl, :n], 1.0, ps[:, bp, :n], ALU.add, ALU.mult),
                      r=[("actS", sl), ("ps", bp)], w=[("m", j, t)])
                    yield
                    yield
                    yield
            def lru_chain(j, t, off, n, S, sx, sgb):
                jl = j % 4
                jb = j % 2
                L_xc, L_tr, L_ti, L_e, L_th, L_g = Lsets[S]
                xcb_ = xcbs[S]
                ydst = ybv

                def Ln(nm):
                    return ("L", S, nm)
                samp = (n == NSAMP)
                bx, bgt = P.nb(), P.nb()
                A("pe", MM(ps[:, bx, :n], [(wa[:, sx, k, jl * 128:(jl + 1) * 128], hT[:, k, off:off + n])
                                           for k in range(8)]), r=hTr(t) + [("wa", sx)], w=[("ps", bx)])
                A("pe", MM(ps[:, bgt, :n], [(wa[:, sgb, k, jl * 128:(jl + 1) * 128], hT[:, k, off:off + n])
                                            for k in range(8)]), r=hTr(t) + [("wa", sgb)], w=[("ps", bgt)])
                yield
                A("act", ACT(xb[:, jb, 3 + off:3 + off + n], ps[:, bx, :n], AF.Copy),
                  r=[("ps", bx)], w=[("xb", jb, t)])
                yield
                gs = gelu_to(L_g[:, :n], ps[:, bgt, :n], [("ps", bgt)], [Ln("g")])
                yield
                A("dve", TS(L_xc[:, :n], xb[:, jb, 3 + off:3 + off + n], vT[:, j, 6:7], vT[:, j, 7:8],
                            ALU.mult, ALU.add), r=[("xb", jb, t), ("vT",)], w=[Ln("xc")])
                yield
                for k in range(3):
                    if samp:
                        src = stT[:, j, 16 + k:64:3]
                        rr = [("stT",)]
                    else:
                        src = xb[:, jb, off + k:off + k + n]
                        rr = [("xb", jb, t)] + ([("xbh", jb)] if t == 0 else [("xb", jb, t - 1)])
                    A("dve", STT(L_xc[:, :n], src, vT[:, j, 3 + k:4 + k], L_xc[:, :n], ALU.mult, ALU.add),
                      r=rr + [Ln("xc"), ("vT",)], w=[Ln("xc")])
                    yield
                A("act", ACT(xcb_[:, :n], L_xc[:, :n], AF.Copy), r=[Ln("xc")], w=[Ln("xcb")])
                yield
                br, bi = P.nb(), P.nb()
                A("pe", MM(ps[:, br, :n], [(wrbd[:, j, :], xcb_[:, :n])]), r=[Ln("xcb"), ("c", "wrbd")],
                  w=[("ps", br)])
                A("pe", MM(ps[:, bi, :n], [(wibd[:, j, :], xcb_[:, :n])]), r=[Ln("xcb"), ("c", "wibd")],
                  w=[("ps", bi)])
                yield
                A("act", ACT(L_tr[:, :n], ps[:, br, :n], AF.Tanh, scale=0.5, bias=vT[:, j, 14:15]),
                  r=[("ps", br), ("vT",)], w=[Ln("tr")])
                yield
                A("act", ACT(L_ti[:, :n], ps[:, bi, :n], AF.Tanh, scale=0.5, bias=vT[:, j, 15:16]),
                  r=[("ps", bi), ("vT",)], w=[Ln("ti")])
                yield
                A("act", ACT(L_e[:, :n], L_tr[:, :n], AF.Exp, scale=vT[:, j, 12:13], bias=vT[:, j, 12:13]),
                  r=[Ln("tr"), ("vT",)], w=[Ln("e")])
                yield
                A("act", ACT(L_th[:, :n], L_tr[:, :n], AF.Tanh, scale=vT[:, j, 13:14], bias=vT[:, j, 13:14]),
                  r=[Ln("tr"), ("vT",)], w=[Ln("th")])
                yield
                A("dve", STT(L_ti[:, :n], L_ti[:, :n], 1.0, L_xc[:, :n], ALU.add, ALU.mult),
                  r=[Ln("ti"), Ln("xc")], w=[Ln("ti")])
                yield
                A("dve", STT(L_e[:, :n], L_e[:, :n], 1.0, L_th[:, :n], ALU.add, ALU.mult),
                  r=[Ln("e"), Ln("th")], w=[Ln("e")])
                yield
                A("dve", STT(L_th[:, :n], L_e[:, :n], 2.0, L_e[:, :n], ALU.add, ALU.mult),
                  r=[Ln("e"), Ln("th")], w=[Ln("th")])
                yield
                A("act", ACT(L_th[:, :n], L_th[:, :n], AF.Sqrt, scale=-0.25), r=[Ln("th")], w=[Ln("th")])
                yield
                A("dve", TS1(L_e[:, :n], L_e[:, :n], 1.0, ALU.add), r=[Ln("e")], w=[Ln("e")])
                yield
                A("dve", TT(L_ti[:, :n], L_ti[:, :n], L_th[:, :n], ALU.mult), r=[Ln("ti"), Ln("th")], w=[Ln("ti")])
                yield
                if not samp:
                    if t == 0:
                        init = 0.0 if hf == 0 else hc[:, j:j + 1]
                        ir = [("hc",)]
                    else:
                        init = Lsets[0][1][:, 511:512]
                        ir = [("L", 0, "tr")]
                    A("dve", _mk(lambda e, n=n, init=init: e.tensor_tensor_scan(
                        out=L_tr[:, :n], data0=L_e[:, :n], data1=L_ti[:, :n], initial=init,
                        op0=ALU.mult, op1=ALU.add), 0.07 + 2.2 * n / 960.0),
                      r=[Ln("e"), Ln("ti"), Ln("tr")] + ir, w=[Ln("tr")])
                    yield
                    if t == 1 and hf == 0:
                        A("act", ACT(hc[:, j:j + 1], L_tr[:, n - 1:n], AF.Copy), r=[Ln("tr")], w=[("hc",)])
                        A("act", ACT(cv[:, j, :], xb[:, jb, 3 + HALF - 3:3 + HALF], AF.Copy),
                          r=[("xb", jb, t)], w=[("cv", j)])
                    if t == 1 and hf == 1:
                        A("act", ACT(fin[:, j, 0:1], L_tr[:, n - 1:n], AF.Copy), r=[Ln("tr")], w=[("fin", j)])
                        A("act", ACT(fin[:, j, 1:4], xb[:, jb, 3 + HALF - 3:3 + HALF], AF.Copy),
                          r=[("xb", jb, t)], w=[("fin", j)])
                else:
                    A("dve", TT(L_tr[:, :n], L_e[:, :n], stT[:, j, 0:NSAMP], ALU.mult),
                      r=[Ln("e"), ("stT",), Ln("tr")], w=[Ln("tr")])
                    A("dve", TT(L_tr[:, :n], L_tr[:, :n], L_ti[:, :n], ALU.add), r=[Ln("tr"), Ln("ti")],
                      w=[Ln("tr")])
                    A("act", ACT(fin[:, j, 4:20], L_tr[:, :n], AF.Copy), r=[Ln("tr")], w=[("fin", j)])
                    A("act", ACT(fin[:, j, 20:36], xb[:, jb, 3 + HALF:3 + HALF + NSAMP], AF.Copy),
                      r=[("xb", jb, t)], w=[("fin", j)])
                yield
                A("dve", STT(ydst[:, j, off:off + n], L_g[:, :n], 1.0 / gs, L_tr[:, :n], ALU.mult, ALU.mult),
                  r=[Ln("g"), Ln("tr")], w=[("yb", j, t)])
                yield

            def run_interleaved(gens):
                gens = list(gens)
                for _ in range(STAGGER):
                    try:
                        next(gens[0])
                    except StopIteration:
                        gens.pop(0)
                        break
                while gens:
                    for g_ in list(gens):
                        try:
                            next(g_)
                        except StopIteration:
                            gens.remove(g_)

            for j in range(8):
                if j % 4 == 0:
                    sx = load_wa(winv, 2048 + (j // 4) * 512, 512)
                    sgb = load_wa(winv, 3072 + (j // 4) * 512, 512)
                    spa = load_wa(pav, (j // 4) * 512, 512)
                    sga = load_wa(winv, 4096 + (j // 4) * 512, 512)
                jb = j % 2
                if hf == 0:
                    A("dve", MSET(xb[:, jb, 0:3], 0.0), w=[("xbh", jb)])
                else:
                    A("dve", CP(xb[:, jb, 0:3], cv[:, j, :]), r=[("cv", j)], w=[("xbh", jb)])
                gl_ = [lru_chain(j, 0, 0, 512, 0, sx, sgb), lru_chain(j, 1, 512, 512, 1, sx, sgb)]
                if hf == 0:
                    gl_.append(lru_chain(j, 2, HALF, NSAMP, 2, sx, sgb))
                gl_.append(c_chain(j, spa, sga))
                run_interleaved(gl_)
            c0_, c1_ = (4, 36) if hf == 0 else (0, 4)
            ncol = c1_ - c0_
            b0, b1 = P.nb(), P.nb()

            def tr_fin(e, b0=b0, b1=b1, c0_=c0_, c1_=c1_, ncol=ncol):
                last = None
                for j in range(8):
                    bb = b0 if j < 4 else b1
                    last = e.transpose(ps[:ncol, bb, (j % 4) * 128:(j % 4 + 1) * 128], fin[:, j, c0_:c1_], ident[:])
                return last
            A("pe", _mk(tr_fin, 1.0), r=[("fin", j) for j in range(8)] + [("c", "ident")], w=[("ps", b0), ("ps", b1)])
            A("act", ACT(stg[:ncol, 0, 0:512], ps[:ncol, b0, :], AF.Copy), r=[("ps", b0)], w=[("stg", 0)])
            A("act", ACT(stg[:ncol, 0, 512:1024], ps[:ncol, b1, :], AF.Copy), r=[("ps", b1), ("stg", 0)],
              w=[("stg", 0)])
            if hf == 0:
                P.dma("sp", [(h_s[:, :], stg[0:16, 0, :]), (c_s[:, 2, :], stg[16:32, 0, :])], r=[("stg", 0)],
                      w=[("out", "hs")], key="so0")
                out_res.append(("out", "hs"))
            else:
                P.dma("sp", [(h_p[:, :], stg[0:1, 0, :]), (c_p[:, :], stg[1:4, 0, :])], r=[("stg", 0)],
                      w=[("out", "hp")], key="so0")
                out_res.append(("out", "hp"))
            for j in range(8):
                if j % 4 == 0:
                    spb_ = load_wa(pbv, (j // 4) * 512, 512)
                    sgb2 = load_wa(winv, 5120 + (j // 4) * 512, 512)
                jl = j % 4
                for t, (off, n) in enumerate(TTl):
                    bp, bg = P.nb(), P.nb()
                    A("pe", MM(ps[:, bp, :n], [(wa[:, spb_, k, jl * 128:(jl + 1) * 128], ybv[:, k, off:off + n])
                                               for k in range(8)]),
                      r=[("yb", k, t) for k in range(8)] + [("wa", spb_)], w=[("ps", bp)])
                    A("pe", MM(ps[:, bg, :n], [(wa[:, sgb2, k, jl * 128:(jl + 1) * 128], hT[:, k, off:off + n])
                                               for k in range(8)]), r=hTr(t) + [("wa", sgb2)], w=[("ps", bg)])
                    sl = as_next()
                    A("act", ACT(actS[:, sl, :n], ps[:, bg, :n], AF.Tanh, scale=0.5), r=[("ps", bg)], w=[("actS", sl)])
                    A("dve", STT(actS[:, sl, :n], actS[:, sl, :n], 1.0, ps[:, bp, :n], ALU.add, ALU.mult),
                      r=[("actS", sl), ("ps", bp)], w=[("actS", sl)])
                    A("dve", TT(mb_[:, j, off:off + n], actS[:, sl, :n], mb_[:, j, off:off + n], ALU.add),
                      r=[("actS", sl), ("m", j, t)], w=[("m", j, t)])
            so0 = load_wa(wov, 0, 512)
            so1 = load_wa(wov, 512, 512)
            for t, (off, n) in enumerate(TTl):
                for j in range(8):
                    so = so0 if j < 4 else so1
                    jl = j % 4
                    b = P.nb()
                    A("pe", MM(ps[:, b, :n], [(wa[:, so, k, jl * 128:(jl + 1) * 128], mb_[:, k, off:off + n])
                                              for k in range(8)]),
                      r=[("m", k, t) for k in range(8)] + [("wa", so)], w=[("ps", b)])
                    A("dve", STT(xT[:, j, off:off + n], ps[:, b, :n], 0.5, xT[:, j, off:off + n], ALU.mult, ALU.add),
                      r=[("ps", b), ("xT", j, t)], w=[("xT", j, t)])

        oneb = sb("oneb", (128, 1), F32)
        A("pool", MSET(oneb[:], 1.0), w=[("c", "oneb")])
        epsb = sb("epsb", (128, 1), F32)
        A("pool", MSET(epsb[:], EPS), w=[("c", "epsb")])

        for hf in range(2):
            TTl = [(0, 512), (512, 512)] + ([(HALF, NSAMP)] if hf == 0 else [])
            tiles = list(range(8)) + ([8] if hf == 0 else [])
            for i in tiles:
                npt = 128 if i < 8 else NSAMP
                off = i * 128 if i < 8 else HALF
                t = off // 512
                s = (i % 4) if hf == 0 else 2 + (i % 2)
                src = xp[hf * HALF + i * 128:hf * HALF + (i + 1) * 128, :] if i < 8 else xs[:, :]
                P.dma("sp", [(SLOT[s][:npt, :], src)], w=SLOTR[s], key="stg%d" % s)
                for hh in range(2):
                    b = P.nb()

                    def tr_x(e, b=b, hh=hh, s=s, npt=npt):
                        last = None
                        for q in range(4):
                            j = hh * 4 + q
                            last = e.transpose(ps[:, b, q * npt:(q + 1) * npt], SLOT[s][:npt, j * 128:(j + 1) * 128],
                                               ident[:npt, :npt])
                        return last
                    A("pe", _mk(tr_x, 1.2), r=SLOTR[s] + [("c", "ident")], w=[("ps", b)])
                    A("act" if hh == 0 else "dve",
                      (ACT(xT[:, hh * 4:(hh + 1) * 4, off:off + npt],
                           ps[:, b, 0:4 * npt].rearrange("p (q n) -> p q n", q=4), AF.Copy) if hh == 0 else
                       CP(xT[:, hh * 4:(hh + 1) * 4, off:off + npt],
                          ps[:, b, 0:4 * npt].rearrange("p (q n) -> p q n", q=4))),
                      r=[("ps", b)], w=[("xT", j, t) for j in range(hh * 4, hh * 4 + 4)])
            if stage >= 1:
                rmsnorm(TTl, 0)
            if hf == 0:
                late_setup()
            if stage >= 1:
                ffn(TTl, f1g, f1u, f1d)
            if stage >= 2:
                P.fence("ARENA")
                P.fence("WBMEM")
                rmsnorm(TTl, 1)
                mixer(hf, TTl)
                P.fence("ARENA")
                P.fence("WBMEM")
            if stage >= 3:
                rmsnorm(TTl, 2)
                ffn(TTl, f2g, f2u, f2d)
            if debug and hf == 0:
                P.dma("sp", [(dbg[:, :], xT[:].rearrange("p j n -> p (j n)"))], r=xTr(0) + xTr(1) + xTr(2),
                      w=[("out", "dbg")], key="dbg")
                out_res.append(("out", "dbg"))
            P.dma("sp", [(nbuf[:, :], bass.AP(fnm.tensor, 0, [[0, 128], [1, 1024]]))], w=[("c", "nbuf")], key="nbuf")
            for i in tiles:
                npt = 128 if i < 8 else NSAMP
                off = i * 128 if i < 8 else HALF
                t = off // 512
                s = (i % 2) if hf == 0 else (i % 4)
                q_ = i % 2
                sv_ = ssv2[:, q_, :]
                b0, b1 = P.nb(), P.nb()

                def tr_o(e, b0=b0, b1=b1, off=off, npt=npt):
                    last = None
                    for j in range(8):
                        bb = b0 if j < 4 else b1
                        last = e.transpose(ps[:npt, bb, (j % 4) * 128:(j % 4 + 1) * 128], xT[:, j, off:off + npt],
                                           ident[:])
                    return last
                A("pe", _mk(tr_o, 2.0), r=xTr(t) + [("c", "ident")], w=[("ps", b0), ("ps", b1)])
                A("act", ACT(rstd[:npt, :], ps[:npt, b0, :], AF.Square, accum_out=sv_[:npt, 0:1]),
                  r=[("ps", b0)], w=[("rstd",), ("ssv", q_)])
                A("act", ACT(rstd[:npt, :], ps[:npt, b1, :], AF.Square, accum_out=sv_[:npt, 1:2]),
                  r=[("ps", b1)], w=[("rstd",), ("ssv", q_)])
                A("dve", TT(sv_[:npt, 2:3], sv_[:npt, 0:1], sv_[:npt, 1:2], ALU.add), r=[("ssv", q_)], w=[("ssv", q_)])
                A("dve", TS(sv_[:npt, 2:3], sv_[:npt, 2:3], 1.0 / 1024.0, EPS, ALU.mult, ALU.add), r=[("ssv", q_)],
                  w=[("ssv", q_)])
                A("act", ACT(sv_[:npt, 3:4], sv_[:npt, 2:3], AF.Sqrt), r=[("ssv", q_)], w=[("ssv", q_)])
                A("dve", lambda e, npt=npt, sv_=sv_: e.reciprocal(out=sv_[:npt, 3:4], in_=sv_[:npt, 3:4]),
                  r=[("ssv", q_)], w=[("ssv", q_)])
                A("dve", STT(SLOT[s][:npt, 0:512], ps[:npt, b0, :], sv_[:npt, 3:4], fnb[:npt, 0:512], ALU.mult,
                             ALU.mult), r=[("ps", b0), ("ssv", q_), ("c", "nbuf")], w=SLOTR[s])
                A("dve", STT(SLOT[s][:npt, 512:1024], ps[:npt, b1, :], sv_[:npt, 3:4], fnb[:npt, 512:1024], ALU.mult,
                             ALU.mult), r=[("ps", b1), ("ssv", q_), ("c", "nbuf")] + SLOTR[s], w=SLOTR[s])
                dst = y_p[hf * HALF + i * 128:hf * HALF + (i + 1) * 128, :] if i < 8 else y_s[:, :]
                rk = ("out", "y%d_%d" % (hf, i))
                P.dma("sp", [(dst, SLOT[s][:npt, :])], r=SLOTR[s], w=[rk], key="so%d" % s)
                out_res.append(rk)
        A("sp", None, r=out_res)
        P.emit(nc, stack)
    return nc


_CACHE = {}


def make_in_maps(inp):
    f = lambda a: np.ascontiguousarray(np.asarray(a, dtype=np.float32))
    vec = np.stack([f(inp["ffn1_norm"])[0], f(inp["mix_norm"])[0], f(inp["ffn2_norm"])[0],
                    f(inp["conv_w"])[0, 0], f(inp["conv_w"])[0, 1], f(inp["conv_w"])[0, 2], f(inp["conv_w"])[0, 3],
                    f(inp["conv_b"])[0], f(inp["lru_b_r"])[0].reshape(-1), f(inp["lru_b_i"])[0].reshape(-1),
                    f(inp["lru_lambda"])[0]], axis=0)
    shared = {
        "vecs": f(vec), "gvn": f(inp["gmlp_v_norm"]).reshape(1, D), "fnm": f(inp["final_norm"]).reshape(1, D),
        "spb": f(inp["spatial_b"]).reshape(1, D), "spw": f(inp["spatial_w"])[0],
        "wr": f(inp["lru_w_r"])[0], "wi": f(inp["lru_w_i"])[0],
        "f1g": f(inp["ffn1_w_gate"])[0], "f1u": f(inp["ffn1_w_up"])[0], "f1d": f(inp["ffn1_w_down"])[0],
        "win": f(inp["w_in"])[0], "pa": f(inp["proj_a"])[0], "pb": f(inp["proj_b"])[0], "wo": f(inp["w_out"])[0],
        "f2g": f(inp["ffn2_w_gate"])[0], "f2u": f(inp["ffn2_w_up"])[0], "f2d": f(inp["ffn2_w_down"])[0],
    }
    xpr = f(inp["x_prompt"])
    xsm = f(inp["x_sample"])
    slh = f(inp["state_lru_h"])
    scv = f(inp["state_conv"])
    maps = []
    for c in range(8):
        m = dict(shared)
        m["xp"] = xpr[c]
        m["xs"] = f(xsm[c * NSAMP:(c + 1) * NSAMP, 0, :])
        m["sh"] = f(slh[0, c * NSAMP:(c + 1) * NSAMP, :])
        m["sc"] = f(scv[0, c * NSAMP:(c + 1) * NSAMP])
        maps.append(m)
    return maps


def kernel(**inputs):
    if "nc" not in _CACHE:
        _CACHE["nc"] = build()
    nc = _CACHE["nc"]
    maps = make_in_maps(inputs)
    res = run_bass_kernel_spmd(nc, maps, core_ids=list(range(8)))
    R = res.results
    y_prompt = np.stack([R[c]["y_p"] for c in range(8)], axis=0)
    y_sample = np.concatenate([R[c]["y_s"] for c in range(8)], axis=0)[:, None, :]
    h_pr = np.concatenate([R[c]["h_p"] for c in range(8)], axis=0)[None]
    c_pr = np.stack([R[c]["c_p"] for c in range(8)], axis=0)[None]
    h_sm = np.concatenate([R[c]["h_s"] for c in range(8)], axis=0)[None]
    c_sm = np.concatenate([R[c]["c_s"] for c in range(8)], axis=0)[None]
    v_sm = np.concatenate([R[c]["v_s"] for c in range(8)], axis=0)[None, :, None, :]
    return (y_prompt.astype(np.float32), y_sample.astype(np.float32), h_pr.astype(np.float32),
            c_pr.astype(np.float32), h_sm.astype(np.float32), c_sm.astype(np.float32), v_sm.astype(np.float32))
```

```python
import contextlib
import numpy as np
import concourse.bass as bass
import concourse.mybir as mybir
from concourse.bass_utils import run_bass_kernel_spmd

F32 = mybir.dt.float32
BF16 = mybir.dt.bfloat16
AF = mybir.ActivationFunctionType
ALU = mybir.AluOpType

D = 1024
DFF = 2752
SEQ = 2048
NSAMP = 16
HALF = 1024
NTW = HALF + NSAMP
EPS = 1e-6
NV = 11
NATIVE_GELU = True
PE_RATE = 2000.0
EW_SCALE = 1.0
TBL_PEN = 1.28
STAGGER = 0
GK = 0.7978845608028654
GC = 0.044715 * GK


class Op:
    __slots__ = ("idx", "eng", "fn", "dmas", "deps", "has_dep", "sig", "key")


class Prog:
    tie = 0.0
    lat = 0.12
    dma_lat = 2.2
    dma_bw = 150000.0
    window = 150

    def __init__(self):
        self.ops = []
        self.res = {}
        self.bank = 0

    def _add(self, op, reads, writes):
        op.idx = len(self.ops)
        self.ops.append(op)
        deps = set()
        extra = set()
        for r in list(reads) + list(writes):
            if isinstance(r, tuple):
                if r[0] in ("hid", "vtm", "ya", "yb", "m", "arx"):
                    extra.add("ARENA")
                if r[0] in ("wb", "L", "xb", "xbh"):
                    extra.add("WBMEM")
        for r in list(reads) + list(extra):
            st = self.res.setdefault(r, [[], []])
            deps.update(st[0])
            st[1].append(op.idx)
        for w in writes:
            st = self.res.setdefault(w, [[], []])
            deps.update(st[0])
            deps.update(st[1])
            st[0] = [op.idx]
            st[1] = []
        deps.discard(op.idx)
        op.deps = deps
        op.has_dep = False
        return op

    def op(self, eng, fn, r=(), w=()):
        o = Op()
        o.eng = eng
        o.fn = fn
        o.dmas = None
        o.key = None
        o.sig = None
        return self._add(o, r, w)

    def dma(self, eng, pairs, r=(), w=(), key=None, **kw):
        o = Op()
        o.eng = eng
        o.fn = kw
        o.dmas = pairs
        o.key = key
        o.sig = None
        return self._add(o, r, w)

    def fence(self, name):
        st = self.res.setdefault(name, [[], []])
        o = Op()
        o.eng = None
        o.fn = None
        o.dmas = None
        o.key = None
        o.sig = None
        o.idx = len(self.ops)
        self.ops.append(o)
        o.deps = set(st[0]) | set(st[1])
        o.has_dep = False
        st[0] = [o.idx]
        st[1] = []

    def nb(self):
        b = self.bank
        self.bank = (b + 1) % 8
        return b

    def schedule(self, window=600):
        AFT = AF
        tblsets = {AFT.Gelu_apprx_tanh: {11}, AFT.Tanh: {0, 11, 18, 2}, AFT.Exp: {0}, AFT.Sqrt: {3}, AFT.Silu: {18},
                   AFT.Sigmoid: {2}}
        n = len(self.ops)
        succ = [[] for _ in range(n)]
        indeg = [0] * n
        for o in self.ops:
            indeg[o.idx] = len(o.deps)
            for d in o.deps:
                succ[d].append(o.idx)
        finish = [0.0] * n
        est = [0.0] * n
        engs = ("pe", "act", "dve", "pool", "sp")
        def ocost(o):
            if o.eng is None:
                return 0.0
            if o.dmas is not None:
                return 2.5
            return getattr(o.fn, "cost", 0.1) if o.fn is not None else 0.02
        bl = [0.0] * n
        for i in range(n - 1, -1, -1):
            m_ = 0.0
            for j in succ[i]:
                if bl[j] > m_:
                    m_ = bl[j]
            bl[i] = m_ + ocost(self.ops[i])
        TIE = self.tie
        free = {e: 0.0 for e in engs}
        ready = {e: [] for e in engs}
        order = {e: [] for e in engs}
        cur_tbl = [None]
        done = [False] * n
        low = 0
        nsched = [0]
        LAT = self.lat

        def release(i):
            for j in succ[i]:
                oj = self.ops[j]
                t_ = finish[i] + LAT
                if t_ > est[j]:
                    est[j] = t_
                indeg[j] -= 1
                if indeg[j] == 0:
                    if oj.eng is None:
                        finish[j] = est[j] - LAT
                        done[j] = True
                        nsched[0] += 1
                        release(j)
                    else:
                        ready[oj.eng].append(j)

        roots = [o for o in self.ops if indeg[o.idx] == 0]
        for o in roots:
            if not done[o.idx]:
                if o.eng is None:
                    done[o.idx] = True
                    nsched[0] += 1
                    release(o.idx)
                else:
                    ready[o.eng].append(o.idx)
        while nsched[0] < n:
            while low < n and done[low]:
                low += 1
            lim = low + window
            best = None
            for e in engs:
                fe = free[e]
                for i in ready[e]:
                    if i >= lim:
                        continue
                    st = est[i] if est[i] > fe else fe
                    pen = 0.0
                    if e == "act":
                        fn = self.ops[i].fn
                        tb = getattr(fn, "tbl", None) if fn is not None else None
                        acc = tblsets.get(tb)
                        if acc is not None and cur_tbl[0] not in acc:
                            pen = TBL_PEN
                    key = (st + pen, i)
                    if best is None or key[0] < best[0][0] - TIE or \
                            (key[0] <= best[0][0] + TIE and i < best[2]):
                        best = (key, e, i, st, pen)
            if best is None:
                raise RuntimeError("scheduler stuck")
            _, e, i, st, pen = best
            o = self.ops[i]
            ready[e].remove(i)
            if o.dmas is not None:
                nbytes = 0
                for (dst, src) in o.dmas:
                    nb_ = 1
                    for d_ in src.shape:
                        nb_ *= int(d_)
                    nbytes += nb_ * 4
                occ = 0.06 * len(o.dmas)
                dur = self.dma_lat + nbytes / self.dma_bw
            else:
                c = getattr(o.fn, "cost", 0.1) if o.fn is not None else 0.02
                if e in ("act", "dve"):
                    c = c * EW_SCALE
                occ = c + pen
                dur = occ
                if e == "act" and pen > 0:
                    cur_tbl[0] = min(tblsets.get(o.fn.tbl))
            free[e] = st + occ
            finish[i] = st + dur
            done[i] = True
            nsched[0] += 1
            order[e].append(i)
            release(i)
        self.makespan = max(finish) if n else 0.0
        return order

    def emit(self, nc, stack, sched=True):
        engs = ("pe", "act", "dve", "pool", "sp")
        sems = {}

        def sem(name):
            if name not in sems:
                sems[name] = stack.enter_context(nc.semaphore(name))
            return sems[name]

        if sched:
            order = self.schedule(self.window)
        else:
            order = {e: [o.idx for o in self.ops if o.eng == e] for e in engs}
        pos = {}
        for e in engs:
            for p_, i in enumerate(order[e]):
                pos[i] = p_
        vcache = {}

        def frontier(idxs):
            best = {}
            dm = set()
            for d in idxs:
                dd = self.ops[d]
                if dd.eng is None:
                    if d not in vcache:
                        vcache[d] = frontier(dd.deps)
                    b2, d2 = vcache[d]
                    for e_, i_ in b2.items():
                        if e_ not in best or pos[best[e_]] < pos[i_]:
                            best[e_] = i_
                    dm |= d2
                elif dd.dmas is not None:
                    dm.add(d)
                else:
                    if dd.eng not in best or pos[best[dd.eng]] < pos[d]:
                        best[dd.eng] = d
            return best, dm

        import sys
        sys.setrecursionlimit(10000)
        eff = {}
        for o in self.ops:
            if o.eng is None:
                continue
            b_, d_ = frontier(o.deps)
            lst = []
            for e_, i_ in b_.items():
                if e_ == "pe" and o.eng == "pe" and o.dmas is None:
                    continue
                lst.append(i_)
            lst.extend(d_)
            eff[o.idx] = lst
            for i_ in lst:
                self.ops[i_].has_dep = True
        seq = {e: 0 for e in engs}
        kcnt = {}
        for o in [self.ops[i] for e in engs for i in order[e]]:
            if o.dmas is not None:
                k = "d_" + o.key
                kcnt[k] = kcnt.get(k, 0) + 16 * len(o.dmas)
                o.sig = (k, kcnt[k])
                sem(k)
            elif o.has_dep:
                seq[o.eng] += 1
                o.sig = ("e_" + o.eng, seq[o.eng])
                sem("e_" + o.eng)
        block = stack.enter_context(nc.Block())
        deco = {"pe": block.tensor, "act": block.scalar, "dve": block.vector,
                "pool": block.gpsimd, "sp": block.sync}
        for en in engs:
            myops = [self.ops[i] for i in order[en]]
            if not myops:
                continue

            def body(e, myops=myops, en=en):
                waited = {}
                for o in myops:
                    need = {}
                    for d in eff[o.idx]:
                        s_, v = self.ops[d].sig
                        if need.get(s_, 0) < v:
                            need[s_] = v
                    for s_, v in need.items():
                        if waited.get(s_, 0) >= v:
                            continue
                        e.wait_ge(sems[s_], v)
                        waited[s_] = v
                    if o.dmas is not None:
                        for (dst, src) in o.dmas:
                            e.dma_start(out=dst, in_=src, **o.fn).then_inc(sems[o.sig[0]], 16)
                    elif o.fn is not None:
                        last = o.fn(e)
                        if o.sig is not None:
                            last.then_inc(sems[o.sig[0]], 1)

            deco[en](body)


def _nfree(ap):
    n = 1
    for d in ap.shape[1:]:
        n *= int(d)
    return n


def _mk(f, cost, tbl=None, scan=False):
    f.cost = cost
    f.tbl = tbl
    return f


def ACT(out, in_, func, **kw):
    return _mk(lambda e: e.activation(out=out, in_=in_, func=func, **kw), 0.22 + _nfree(out) / 1200.0, tbl=func)


def TT(out, in0, in1, op):
    return _mk(lambda e: e.tensor_tensor(out=out, in0=in0, in1=in1, op=op), 0.07 + _nfree(out) / 960.0)


def STT(out, in0, scalar, in1, op0, op1):
    return _mk(lambda e: e.scalar_tensor_tensor(out=out, in0=in0, scalar=scalar, in1=in1, op0=op0, op1=op1),
               0.07 + _nfree(out) / 960.0)


def TS(out, in0, s1, s2, op0, op1):
    return _mk(lambda e: e.tensor_scalar(out=out, in0=in0, scalar1=s1, scalar2=s2, op0=op0, op1=op1),
               0.07 + _nfree(out) / 960.0)


def TS1(out, in_, s, op):
    return _mk(lambda e: e.tensor_single_scalar(out=out, in_=in_, scalar=s, op=op), 0.07 + _nfree(out) / 960.0)


def CP(out, in_):
    return _mk(lambda e: e.tensor_copy(out=out, in_=in_), 0.07 + _nfree(out) / 960.0)


def PCP(out, in_):
    return _mk(lambda e: e.tensor_copy(out=out, in_=in_), 0.12 + _nfree(out) / 480.0)


def PSTT(out, in0, scalar, in1, op0, op1):
    return _mk(lambda e: e.scalar_tensor_tensor(out=out, in0=in0, scalar=scalar, in1=in1, op0=op0, op1=op1),
               0.12 + _nfree(out) / 480.0)


def MSET(ap, v):
    return _mk(lambda e: e.memset(ap, v), 0.07 + _nfree(ap) / 1900.0)


def MM(out, pairs):
    def f(e):
        n = len(pairs)
        last = None
        for i, (l, r) in enumerate(pairs):
            last = e.matmul(out, l, r, start=(i == 0), stop=(i == n - 1))
        return last
    nn = max(_nfree(out), 64)
    return _mk(f, len(pairs) * (nn / PE_RATE + 0.004))


def build(stage=99, debug=False):
    nc = bass.Bass("TRN2", target_bir_lowering=False)

    def din(name, shape):
        return nc.dram_tensor(name, list(shape), F32, kind="ExternalInput").ap()

    def dout(name, shape):
        return nc.dram_tensor(name, list(shape), F32, kind="ExternalOutput").ap()

    xp = din("xp", (SEQ, D))
    xs = din("xs", (NSAMP, D))
    sh = din("sh", (NSAMP, D))
    sc = din("sc", (NSAMP, 3, D))
    vecs = din("vecs", (NV, D))
    gvn = din("gvn", (1, D))
    fnm = din("fnm", (1, D))
    spb = din("spb", (1, D))
    spw = din("spw", (8, 128, 128))
    wr = din("wr", (16, 64, 64))
    wi = din("wi", (16, 64, 64))
    f1g = din("f1g", (D, DFF))
    f1u = din("f1u", (D, DFF))
    f1d = din("f1d", (DFF, D))
    win = din("win", (D, 6 * D))
    pa = din("pa", (D, D))
    pb = din("pb", (D, D))
    wo = din("wo", (D, D))
    f2g = din("f2g", (D, DFF))
    f2u = din("f2u", (D, DFF))
    f2d = din("f2d", (DFF, D))

    y_p = dout("y_p", (SEQ, D))
    y_s = dout("y_s", (NSAMP, D))
    h_p = dout("h_p", (1, D))
    c_p = dout("c_p", (3, D))
    h_s = dout("h_s", (NSAMP, D))
    c_s = dout("c_s", (NSAMP, 3, D))
    v_s = dout("v_s", (NSAMP, D))
    if debug:
        dbg = dout("dbg", (128, 8 * NTW))

    st_ = contextlib.ExitStack()
    with st_ as stack:
        def sb(name, shape, dt):
            return stack.enter_context(nc.sbuf_tensor(name, list(shape), dt))

        xT = sb("xT", (128, 8, NTW), F32)
        hT = sb("hT", (128, 8, NTW), BF16)
        arena = sb("arena", (128, 25856), BF16)
        wa = sb("wa", (128, 4, 8, 512), BF16)
        wbm = sb("wbm", (128, 8768), F32)
        stg = sb("stg", (128, 2, 1024), F32)
        vg = sb("vg", (128, 1024), F32)
        actS = sb("actS", (128, 2, 512), F32)
        rstd = sb("rstd", (128, 512), F32)
        ident = sb("ident", (128, 128), F32)
        maskf = sb("maskf", (128, 128), F32)
        onesb = sb("onesb", (128, 128), BF16)
        onesrow = sb("onesrow", (3, 128), BF16)
        WsT = sb("WsT", (128, 8, 128), BF16)
        wrbd = sb("wrbd", (128, 8, 128), BF16)
        wibd = sb("wibd", (128, 8, 128), BF16)
        vT = sb("vT", (128, 8, 16), F32)
        tv = sb("tv", (128, 8, 8), F32)
        nbuf = sb("nbuf", (128, 1024), F32)
        gvnb = nbuf
        fnb = nbuf
        stT = sb("stT", (128, 8, 64), F32)
        bhi = sb("bhi", (3, 1024), BF16)
        blo = sb("blo", (1, 1024), BF16)
        ws00 = sb("ws00", (16, 8), F32)
        Dg = sb("Dg", (16, 8, 16), BF16)
        ssv2 = sb("ssv2", (128, 2, 4), F32)
        ssq = sb("ssq", (128, 3, 9), F32)
        hc = sb("hc", (128, 8), F32)
        cv = sb("cv", (128, 8, 3), F32)
        fin = sb("fin", (128, 8, 36), F32)
        ps = stack.enter_context(nc.psum_tensor("ps", [128, 8, 512], F32))

        hid = arena[:, 0:22 * NTW].rearrange("p (f n) -> p f n", n=NTW)
        vtm = arena[:, 0:8192].rearrange("p (i n) -> p i n", n=1024)
        vtms = arena[:, 8192:9216]
        ybv = arena[:, 0:8 * NTW].rearrange("p (j n) -> p j n", n=NTW)
        ya = arena[:, 9216:9216 + 8 * NTW].rearrange("p (j n) -> p j n", n=NTW)
        mb_ = arena[:, 17536:17536 + 8 * NTW].rearrange("p (j n) -> p j n", n=NTW)
        wbb = wbm[:, 0:8448].bitcast(BF16)
        wb = wbb.rearrange("p (s f n) -> p s f n", s=3, f=22)
        XBW = 1056
        xb = wbm[:, 0:2 * XBW].rearrange("p (s n) -> p s n", n=XBW)
        Lsets = []
        xcbs = []
        for S_ in range(2):
            base = 2 * XBW + S_ * 3328
            Lt = wbm[:, base:base + 6 * 512].rearrange("p (s n) -> p s n", n=512)
            Lsets.append(tuple(Lt[:, i, :] for i in range(6)))
            xcbs.append(wbm[:, base + 3072:base + 3328].bitcast(BF16))
        Lsm = sb("Lsm", (128, 7, NSAMP), F32)
        Lsets.append(tuple(Lsm[:, i, :] for i in range(6)))
        xcbs.append(Lsm[:, 6, :].bitcast(BF16))

        if debug:
            try:
                print("SBUF remaining", nc.sbuf_bytes_remaining, "top", nc.sbuf_top, "base", nc.sbuf_base,
                      "part", nc.SBUF_PARTITION_SIZE_BYTES)
            except Exception as ex:
                print("sbuf introspection failed", ex)
        P = Prog()
        A = P.op
        actSf = actS[:].rearrange("p s n -> p (s n)")
        SLOT = [stg[:, 0, :], stg[:, 1, :], vg[:, :], actSf]
        SLOTR = [[("stg", 0)], [("stg", 1)], [("vg", 0), ("vg", 1)], [("actS", 0), ("actS", 1)]]
        ARX = [arena[:, k * 2048:(k + 1) * 2048].bitcast(F32) for k in range(8)]

        wa_ctr = [0]

        def wa_next():
            s = wa_ctr[0]
            wa_ctr[0] = (s + 1) % 4
            return s

        wb_ctr = [0]

        def wb_next():
            s = wb_ctr[0]
            wb_ctr[0] = (s + 1) % 3
            return s

        as_ctr = [0]

        def as_next():
            s = as_ctr[0]
            as_ctr[0] = (s + 1) % 2
            return s

        def load_wa(src3, c0, cw):
            s = wa_next()
            P.dma("pool", [(wa[:, s, :, 0:cw], src3[:, :, c0:c0 + cw])], w=[("wa", s)], key="wa%d" % s)
            return s

        def kview(w):
            return w.rearrange("(k p) n -> p k n", p=128)

        A("pool", MSET(ident[:], 1.0), w=[("c", "ident")])
        A("pool", lambda e: e.affine_select(out=ident[:], in_=ident[:], pattern=[[-1, 128]],
                                            compare_op=ALU.is_equal, fill=0.0, base=0, channel_multiplier=1),
          r=[("c", "ident")], w=[("c", "ident")])
        A("pool", MSET(maskf[:], 1.0), w=[("c", "maskf")])
        A("pool", lambda e: e.affine_select(out=maskf[:], in_=maskf[:], pattern=[[1, 128]],
                                            compare_op=ALU.is_ge, fill=0.0, base=0, channel_multiplier=-1),
          r=[("c", "maskf")], w=[("c", "maskf")])
        A("pool", MSET(onesb[:], 1.0 / 1024.0), w=[("c", "onesb")])
        A("pool", MSET(onesrow[:], 1.0), w=[("c", "onesrow")])
        A("pool", MSET(wrbd[:], 0.0), w=[("c", "wrbd")])
        A("pool", MSET(wibd[:], 0.0), w=[("c", "wibd")])
        A("pool", MSET(hc[:], 0.0), w=[("hc",)])
        A("pool", MSET(ssq[:], 1.0), w=[("ssq", i) for i in range(9)] + [("ssq", "n"), ("ssq", "r")])
        for (wsrc, wdst, nm) in ((wr, wrbd, "wrbd"), (wi, wibd, "wibd")):
            v = wsrc.rearrange("(j e) i o -> e i j o", e=2)
            P.dma("pool", [(wdst[0:64, :, 0:64], v[0]), (wdst[64:128, :, 64:128], v[1])],
                  r=[], w=[("c", nm)], key=nm)
        P.dma("sp", [(stg[0:NV, 0, :], vecs[:, :])], w=[("stg", 0)], key="stg0")
        bsrc = vg[0:1, :]
        bhf = stg[0:1, 1, :]
        P.dma("sp", [(ws00[:, :], bass.AP(spw.tensor, 0, [[0, 16], [16384, 8]]))], w=[("c", "ws00")],
              key="ws00", allow_slow_non_contiguous=True)
        b = P.nb()

        def tr_vecs(e, b=b):
            last = None
            for j in range(8):
                last = e.transpose(ps[:, b, j * 16:j * 16 + NV], stg[0:NV, 0, j * 128:(j + 1) * 128],
                                   ident[0:NV, 0:NV])
            return last
        A("pe", _mk(tr_vecs, 1.0), r=[("stg", 0), ("c", "ident")], w=[("ps", b)])
        A("act", ACT(vT[:, :, 0:NV], ps[:, b, 0:128].rearrange("p (j n) -> p j n", n=16)[:, :, 0:NV], AF.Copy),
          r=[("ps", b)], w=[("vT",)])
        def late_setup():
            lam = vT[:, :, 10]
            t0, t1, t2, t3, t4 = (tv[:, :, i] for i in range(5))
            A("dve", TS1(t0, lam, -1.0, ALU.mult), r=[("vT",)], w=[("tv",)])
            A("dve", TT(t0, t0, lam, ALU.max), r=[("vT",), ("tv",)], w=[("tv",)])
            A("act", ACT(t0, t0, AF.Exp, scale=-1.0), r=[("tv",)], w=[("tv",)])
            A("dve", TS1(t1, t0, 2.0, ALU.add), r=[("tv",)], w=[("tv",)])
            A("dve", lambda e: e.reciprocal(out=t1, in_=t1), r=[("tv",)], w=[("tv",)])
            A("dve", TT(t1, t0, t1, ALU.mult), r=[("tv",)], w=[("tv",)])
            A("dve", TT(t2, t1, t1, ALU.mult), r=[("tv",)], w=[("tv",)])
            A("dve", TS(t3, t2, 1.0 / 9.0, 1.0 / 7.0, ALU.mult, ALU.add), r=[("tv",)], w=[("tv",)])
            for cst in (1.0 / 5.0, 1.0 / 3.0, 1.0):
                A("dve", TT(t3, t3, t2, ALU.mult), r=[("tv",)], w=[("tv",)])
                A("dve", TS1(t3, t3, cst, ALU.add), r=[("tv",)], w=[("tv",)])
            A("dve", STT(t3, t1, 2.0, t3, ALU.mult, ALU.mult), r=[("tv",)], w=[("tv",)])
            A("dve", TS(t4, lam, -1.0, 0.0, ALU.mult, ALU.max), r=[("tv",), ("vT",)], w=[("tv",)])
            A("dve", TT(t3, t3, t4, ALU.add), r=[("tv",)], w=[("tv",)])
            A("dve", TS1(vT[:, :, 11], t3, -8.0, ALU.mult), r=[("tv",)], w=[("vT",)])
            A("dve", TS1(vT[:, :, 12], t3, -4.0, ALU.mult), r=[("tv",)], w=[("vT",)])
            A("dve", TS1(vT[:, :, 13], t3, -2.0, ALU.mult), r=[("tv",)], w=[("vT",)])
            A("dve", TS1(vT[:, :, 14], vT[:, :, 8], 0.5, ALU.mult), r=[("vT",)], w=[("vT",)])
            A("dve", TS1(vT[:, :, 15], vT[:, :, 9], 0.5, ALU.mult), r=[("vT",)], w=[("vT",)])
            P.dma("sp", [(bsrc, spb[:, :])], w=[("vg", 0), ("vg", 1)], key="bsrc")
            A("dve", CP(bhi[0:1, :], bsrc), r=[("vg", 0), ("vg", 1)], w=[("c", "bhi")])
            A("dve", CP(bhf, bhi[0:1, :]), r=[("c", "bhi")], w=[("stg", 1)])
            A("dve", TT(bsrc, bsrc, bhf, ALU.subtract), r=[("vg", 0), ("vg", 1), ("stg", 1)], w=[("vg", 0), ("vg", 1)])
            A("dve", CP(blo[:], bsrc), r=[("vg", 0), ("vg", 1)], w=[("c", "blo")])
            P.dma("sp", [(bhi[1:2, :], blo[:])], r=[("c", "blo")], w=[("c", "bhi1")], key="bmid")
            A("dve", CP(bhf, blo[:]), r=[("c", "blo")], w=[("stg", 1)])
            A("dve", TT(bsrc, bsrc, bhf, ALU.subtract), r=[("vg", 0), ("vg", 1), ("stg", 1)], w=[("vg", 0), ("vg", 1)])
            A("dve", CP(blo[:], bsrc), r=[("vg", 0), ("vg", 1), ("c", "bhi1")], w=[("c", "blo")])
            P.dma("sp", [(bhi[2:3, :], blo[:])], r=[("c", "blo")], w=[("c", "bhi2")], key="blo2")
            for g in range(8):
                A("dve", TS1(Dg[:, g, :], ident[0:16, 0:16], ws00[:, g:g + 1], ALU.mult),
                  r=[("c", "ident"), ("c", "ws00")], w=[("c", "Dg")])
            P.dma("sp", [(stg[:, 1, :].rearrange("p (g s) -> p g s", g=8), spw.rearrange("g t s -> t g s"))],
                  w=[("stg", 1)], key="stg1")
            for hh in range(2):
                b = P.nb()

                def tr_sp(e, b=b, hh=hh):
                    last = None
                    for q in range(4):
                        g = hh * 4 + q
                        last = e.transpose(ps[:, b, q * 128:(q + 1) * 128], stg[:, 1, g * 128:(g + 1) * 128], ident[:])
                    return last
                A("pe", _mk(tr_sp, 1.6), r=[("stg", 1), ("c", "ident")], w=[("ps", b)])
                for q in range(4):
                    A("dve", TT(WsT[:, hh * 4 + q, :], ps[:, b, q * 128:(q + 1) * 128], maskf[:], ALU.mult),
                      r=[("ps", b), ("c", "maskf")], w=[("c", "WsT")])
            P.dma("sp", [(stg[0:16, 0, :], sh[:, :]), (stg[16:64, 0, :], sc.rearrange("t k d -> (t k) d"))],
                  r=[], w=[("stg", 0)], key="stg0")
            b = P.nb()

            def tr_st(e, b=b):
                last = None
                for j in range(8):
                    last = e.transpose(ps[:, b, j * 64:(j + 1) * 64], stg[0:64, 0, j * 128:(j + 1) * 128],
                                       ident[0:64, 0:64])
                return last
            A("pe", _mk(tr_st, 1.6), r=[("stg", 0), ("c", "ident")], w=[("ps", b)])
            A("act", ACT(stT[:], ps[:, b, :].rearrange("p (j n) -> p j n", n=64), AF.Copy), r=[("ps", b)], w=[("stT",)])
            P.dma("sp", [(c_s[:, 0:2, :], sc[:, 1:3, :])], r=[], w=[("out", "cs01")], key="cs01")

        out_res = [("out", "cs01")]

        def hTr(t):
            return [("hT", j, t) for j in range(8)]

        def xTr(t):
            return [("xT", j, t) for j in range(8)]

        def rmsnorm(TTl, col):
            for t, (off, n) in enumerate(TTl):
                A("act", ACT(hT[:, :, off:off + n], xT[:, :, off:off + n], AF.Square), r=xTr(t), w=hTr(t))
                b = P.nb()
                A("pe", MM(ps[:, b, :n], [(onesb[:], hT[:, j, off:off + n]) for j in range(8)]),
                  r=hTr(t) + [("c", "onesb")], w=[("ps", b)])
                A("act", ACT(rstd[:, :n], ps[:, b, :n], AF.Sqrt, bias=epsb[:, :], scale=1.0),
                  r=[("ps", b), ("c", "epsb")], w=[("rstd",)])
                A("dve", _mk(lambda e, n=n: e.reciprocal(out=rstd[:, :n], in_=rstd[:, :n]), 0.07 + n / 960.0),
                  r=[("rstd",)], w=[("rstd",)])
                for j in range(8):
                    A("dve", STT(hT[:, j, off:off + n], xT[:, j, off:off + n], vT[:, j, col:col + 1],
                                 rstd[:, :n], ALU.mult, ALU.mult),
                      r=[("xT", j, t), ("rstd",), ("vT",)], w=[("hT", j, t)])

        def gelu_to(dst, src, n_res_r, n_res_w):
            if NATIVE_GELU:
                A("act", ACT(dst, src, AF.Gelu_apprx_tanh), r=n_res_r, w=n_res_w)
                return 1.0
            A("act", ACT(dst, src, AF.Square), r=n_res_r, w=n_res_w)
            A("act", ACT(dst, dst, AF.Identity, scale=GC, bias=gkb[0:dst.shape[0], :]), r=n_res_w + [("c", "gkb")], w=n_res_w)
            A("dve", TT(dst, dst, src, ALU.mult), r=n_res_r + n_res_w, w=n_res_w)
            A("act", ACT(dst, dst, AF.Tanh), r=n_res_w, w=n_res_w)
            A("dve", STT(dst, dst, 1.0, src, ALU.add, ALU.mult), r=n_res_r + n_res_w, w=n_res_w)
            return 2.0

        def ffn(TTl, wg, wu, wd):
            wgv, wuv = kview(wg), kview(wu)
            groups = [(0, 512), (512, 512), (1024, 512), (1536, 512), (2048, 512), (2560, 192)]
            for (c0, cw) in groups:
                sg_ = load_wa(wgv, c0, cw)
                su_ = load_wa(wuv, c0, cw)
                for fl in range((cw + 127) // 128):
                    f = c0 // 128 + fl
                    fw = min(128, cw - fl * 128)
                    for t, (off, n) in enumerate(TTl):
                        bg, bu = P.nb(), P.nb()
                        A("pe", MM(ps[:fw, bg, :n], [(wa[:, sg_, k, fl * 128:fl * 128 + fw], hT[:, k, off:off + n])
                                                     for k in range(8)]), r=hTr(t) + [("wa", sg_)], w=[("ps", bg)])
                        A("pe", MM(ps[:fw, bu, :n], [(wa[:, su_, k, fl * 128:fl * 128 + fw], hT[:, k, off:off + n])
                                                     for k in range(8)]), r=hTr(t) + [("wa", su_)], w=[("ps", bu)])
                        sl = as_next()
                        A("act", ACT(actS[:fw, sl, :n], ps[:fw, bg, :n], AF.Silu), r=[("ps", bg)], w=[("actS", sl)])
                        A("dve", TT(hid[:fw, f, off:off + n], actS[:fw, sl, :n], ps[:fw, bu, :n], ALU.mult),
                          r=[("actS", sl), ("ps", bu)], w=[("hid", f, t)])
            wd3 = wd[0:2688, :].rearrange("(f p) n -> p f n", p=128)
            passes = [[0], list(range(1, len(TTl)))]
            for tl in passes:
                for mp in range(4):
                    s = wb_next()
                    P.dma("pool", [(wb[:, s, 0:21, :], wd3[:, :, mp * 256:(mp + 1) * 256]),
                                   (wb[0:64, s, 21, :], wd[2688:2752, mp * 256:(mp + 1) * 256])],
                          w=[("wb", s)], key="wb%d" % s)
                    for ml in range(2):
                        m = mp * 2 + ml
                        for t in tl:
                            off, n = TTl[t]
                            b = P.nb()
                            pairs = [(wb[:, s, f, ml * 128:(ml + 1) * 128], hid[:, f, off:off + n]) for f in range(21)]
                            pairs.append((wb[0:64, s, 21, ml * 128:(ml + 1) * 128], hid[0:64, 21, off:off + n]))
                            A("pe", MM(ps[:, b, :n], pairs), r=[("wb", s)] + [("hid", f, t) for f in range(22)],
                              w=[("ps", b)])
                            A("dve", STT(xT[:, m, off:off + n], ps[:, b, :n], 0.5, xT[:, m, off:off + n],
                                         ALU.mult, ALU.add), r=[("ps", b), ("xT", m, t)], w=[("xT", m, t)])

        gkb = sb("gkb", (128, 1), F32)
        A("pool", MSET(gkb[:], GK), w=[("c", "gkb")])

        def mixer(hf, TTl):
            winv = kview(win)
            pav, pbv, wov = kview(pa), kview(pb), kview(wo)
            P.dma("sp", [(nbuf[:, :], bass.AP(gvn.tensor, 0, [[0, 128], [1, 1024]]))], w=[("c", "nbuf")], key="nbuf")
            sv0 = load_wa(winv, 1024, 512)
            sv1 = load_wa(winv, 1536, 512)
            tiles = list(range(8)) + ([8] if hf == 0 else [])
            for i in tiles:
                npt = 128 if i < 8 else NSAMP
                off = i * 128 if i < 8 else HALF
                t = off // 512
                b0, b1 = P.nb(), P.nb()

                def fv(e, b0=b0, b1=b1, off=off, npt=npt):
                    last = None
                    for k in range(8):
                        for (bb, sv) in ((b0, sv0), (b1, sv1)):
                            last = e.matmul(ps[:npt, bb, :], hT[:, k, off:off + npt], wa[:, sv, k, :],
                                            start=(k == 0), stop=(k == 7))
                    return last
                A("pe", _mk(fv, 4.2), r=hTr(t) + [("wa", sv0), ("wa", sv1)], w=[("ps", b0), ("ps", b1)])
                buf = SLOT[2 + (i % 2)]
                bufr = SLOTR[2 + (i % 2)]
                gs = 1.0
                for c, bb in enumerate((b0, b1)):
                    gs = gelu_to(buf[:npt, c * 512:(c + 1) * 512], ps[:npt, bb, :], [("ps", bb)], [bufr[c]])
                vdst = vtm[:, i, :] if i < 8 else vtms[0:NSAMP, :]
                A("act", ACT(mb_[:npt, 0, 0:1024], buf[:npt, :], AF.Square, accum_out=ssq[:npt, 0, i:i + 1]),
                  r=bufr, w=[("m", 0, 0), ("m", 0, 1), ("m", 0, 2), ("ssq", i)])
                A("dve", TT(vdst, buf[:npt, :], gvnb[:npt, :], ALU.mult), r=bufr + [("c", "nbuf")], w=[("vtm", i)])
            nt_ = len(tiles)
            A("dve", TS(ssq[:, 1, 0:nt_], ssq[:, 0, 0:nt_], 1.0 / 1024.0, EPS * gs * gs, ALU.mult, ALU.add),
              r=[("ssq", i) for i in tiles], w=[("ssq", "n")])
            A("act", ACT(ssq[:, 2, 0:nt_], ssq[:, 1, 0:nt_], AF.Sqrt), r=[("ssq", "n")], w=[("ssq", "r")])
            A("dve", lambda e, nt_=nt_: e.reciprocal(out=ssq[:, 2, 0:nt_], in_=ssq[:, 2, 0:nt_]), r=[("ssq", "r")],
              w=[("ssq", "r")])
            for i in tiles:
                if i < 8:
                    A("act", ACT(vtm[:, i, :], vtm[:, i, :], AF.Copy, scale=ssq[:, 2, i:i + 1]),
                      r=[("ssq", "r"), ("vtm", i)], w=[("vtm", i)])
                else:
                    buf = SLOT[2 + (i % 2)]
                    bufr = SLOTR[2 + (i % 2)]
                    A("act", ACT(vtms[0:NSAMP, :], vtms[0:NSAMP, :], AF.Copy, scale=ssq[0:NSAMP, 2, i:i + 1]),
                      r=[("ssq", "r"), ("vtm", i)], w=[("vtm", i)])
                    A("dve", STT(stg[0:NSAMP, 1, :], buf[0:NSAMP, :], ssq[0:NSAMP, 2, i:i + 1], gvnb[0:NSAMP, :],
                                 ALU.mult, ALU.mult), r=bufr + [("ssq", "r"), ("c", "nbuf")], w=[("stg", 1)])
                    P.dma("sp", [(v_s[:, :], stg[0:NSAMP, 1, :])], r=[("stg", 1)], w=[("out", "vs")], key="so1")
                    out_res.append(("out", "vs"))
            su = None
            for g in range(8):
                if g % 4 == 0:
                    su = load_wa(winv, (g // 4) * 512, 512)
                gl = g % 4
                for t, (off, n) in enumerate(TTl):
                    bu, bs = P.nb(), P.nb()
                    A("pe", MM(ps[:, bu, :n], [(wa[:, su, k, gl * 128:(gl + 1) * 128], hT[:, k, off:off + n])
                                               for k in range(8)]), r=hTr(t) + [("wa", su)], w=[("ps", bu)])
                    if n == 512:
                        def fs(e, bs=bs, g=g, off=off):
                            last = None
                            for c in range(4):
                                i = off // 128 + c
                                e.matmul(ps[:, bs, c * 128:(c + 1) * 128], onesrow[0:3, :],
                                         bhi[0:3, g * 128:(g + 1) * 128], start=True, stop=False)
                                last = e.matmul(ps[:, bs, c * 128:(c + 1) * 128], vtm[:, i, g * 128:(g + 1) * 128],
                                                WsT[:, g, :], start=False, stop=True)
                            return last
                        rr = [("vtm", off // 128 + c) for c in range(4)]
                    else:
                        def fs(e, bs=bs, g=g):
                            e.matmul(ps[:, bs, :NSAMP], onesrow[0:3, :],
                                     bhi[0:3, g * 128:g * 128 + 1].to_broadcast([3, NSAMP]), start=True, stop=False)
                            return e.matmul(ps[:, bs, :NSAMP], vtms[0:NSAMP, g * 128:(g + 1) * 128], Dg[:, g, :],
                                            start=False, stop=True)
                        rr = [("vtm", 8)]
                    A("pe", _mk(fs, 0.6), r=rr + [("c", "WsT"), ("c", "bhi"), ("c", "bhi1"), ("c", "bhi2"), ("c", "Dg"), ("c", "onesrow")],
                      w=[("ps", bs)])
                    sl = as_next()
                    gs = gelu_to(actS[:, sl, :n], ps[:, bu, :n], [("ps", bu)], [("actS", sl)])
                    A("dve", STT(ya[:, g, off:off + n], actS[:, sl, :n], 1.0 / gs, ps[:, bs, :n], ALU.mult, ALU.mult),
                      r=[("actS", sl), ("ps", bs)], w=[("ya", g, t)])
            P.fence("ARENA")
            def c_chain(j, spa, sga):
                jl = j % 4
                for t, (off, n) in enumerate(TTl):
                    bp, bg = P.nb(), P.nb()
                    A("pe", MM(ps[:, bp, :n], [(wa[:, spa, k, jl * 128:(jl + 1) * 128], ya[:, k, off:off + n])
                                               for k in range(8)]),
                      r=[("ya", k, t) for k in range(8)] + [("wa", spa)], w=[("ps", bp)])
                    A("pe", MM(ps[:, bg, :n], [(wa[:, sga, k, jl * 128:(jl + 1) * 128], hT[:, k, off:off + n])
                                               for k in range(8)]), r=hTr(t) + [("wa", sga)], w=[("ps", bg)])
                    yield
                    sl = as_next()
                    A("act", ACT(actS[:, sl, :n], ps[:, bg, :n], AF.Tanh, scale=0.5), r=[("ps", bg)], w=[("actS", sl)])
                    yield
                    A("dve", STT(mb_[:, j, off:off + n], actS[:, sl, :n], 1.0, ps[:, bp, :n], ALU.add, ALU.mult),
                      r=[("actS", sl), ("ps", bp)], w=[("m", j, t)])
                    yield
                    yield
                    yield
            def lru_chain(j, t, off, n, S, sx, sgb):
                jl = j % 4
                jb = j % 2
                L_xc, L_tr, L_ti, L_e, L_th, L_g = Lsets[S]
                xcb_ = xcbs[S]
                ydst = ybv

                def Ln(nm):
                    return ("L", S, nm)
                samp = (n == NSAMP)
                bx, bgt = P.nb(), P.nb()
                A("pe", MM(ps[:, bx, :n], [(wa[:, sx, k, jl * 128:(jl + 1) * 128], hT[:, k, off:off + n])
                                           for k in range(8)]), r=hTr(t) + [("wa", sx)], w=[("ps", bx)])
                A("pe", MM(ps[:, bgt, :n], [(wa[:, sgb, k, jl * 128:(jl + 1) * 128], hT[:, k, off:off + n])
                                            for k in range(8)]), r=hTr(t) + [("wa", sgb)], w=[("ps", bgt)])
                yield
                A("act", ACT(xb[:, jb, 3 + off:3 + off + n], ps[:, bx, :n], AF.Copy),
                  r=[("ps", bx)], w=[("xb", jb, t)])
                yield
                gs = gelu_to(L_g[:, :n], ps[:, bgt, :n], [("ps", bgt)], [Ln("g")])
                yield
                A("dve", TS(L_xc[:, :n], xb[:, jb, 3 + off:3 + off + n], vT[:, j, 6:7], vT[:, j, 7:8],
                            ALU.mult, ALU.add), r=[("xb", jb, t), ("vT",)], w=[Ln("xc")])
                yield
                for k in range(3):
                    if samp:
                        src = stT[:, j, 16 + k:64:3]
                        rr = [("stT",)]
                    else:
                        src = xb[:, jb, off + k:off + k + n]
                        rr = [("xb", jb, t)] + ([("xbh", jb)] if t == 0 else [("xb", jb, t - 1)])
                    A("dve", STT(L_xc[:, :n], src, vT[:, j, 3 + k:4 + k], L_xc[:, :n], ALU.mult, ALU.add),
                      r=rr + [Ln("xc"), ("vT",)], w=[Ln("xc")])
                    yield
                A("act", ACT(xcb_[:, :n], L_xc[:, :n], AF.Copy), r=[Ln("xc")], w=[Ln("xcb")])
                yield
                br, bi = P.nb(), P.nb()
                A("pe", MM(ps[:, br, :n], [(wrbd[:, j, :], xcb_[:, :n])]), r=[Ln("xcb"), ("c", "wrbd")],
                  w=[("ps", br)])
                A("pe", MM(ps[:, bi, :n], [(wibd[:, j, :], xcb_[:, :n])]), r=[Ln("xcb"), ("c", "wibd")],
                  w=[("ps", bi)])
                yield
                A("act", ACT(L_tr[:, :n], ps[:, br, :n], AF.Tanh, scale=0.5, bias=vT[:, j, 14:15]),
                  r=[("ps", br), ("vT",)], w=[Ln("tr")])
                yield
                A("act", ACT(L_ti[:, :n], ps[:, bi, :n], AF.Tanh, scale=0.5, bias=vT[:, j, 15:16]),
                  r=[("ps", bi), ("vT",)], w=[Ln("ti")])
                yield
                A("act", ACT(L_e[:, :n], L_tr[:, :n], AF.Exp, scale=vT[:, j, 12:13], bias=vT[:, j, 12:13]),
                  r=[Ln("tr"), ("vT",)], w=[Ln("e")])
                yield
                A("act", ACT(L_th[:, :n], L_tr[:, :n], AF.Tanh, scale=vT[:, j, 13:14], bias=vT[:, j, 13:14]),
                  r=[Ln("tr"), ("vT",)], w=[Ln("th")])
                yield
                A("dve", STT(L_ti[:, :n], L_ti[:, :n], 1.0, L_xc[:, :n], ALU.add, ALU.mult),
                  r=[Ln("ti"), Ln("xc")], w=[Ln("ti")])
                yield
                A("dve", STT(L_e[:, :n], L_e[:, :n], 1.0, L_th[:, :n], ALU.add, ALU.mult),
                  r=[Ln("e"), Ln("th")], w=[Ln("e")])
                yield
                A("dve", STT(L_th[:, :n], L_e[:, :n], 2.0, L_e[:, :n], ALU.add, ALU.mult),
                  r=[Ln("e"), Ln("th")], w=[Ln("th")])
                yield
                A("act", ACT(L_th[:, :n], L_th[:, :n], AF.Sqrt, scale=-0.25), r=[Ln("th")], w=[Ln("th")])
                yield
                A("dve", TS1(L_e[:, :n], L_e[:, :n], 1.0, ALU.add), r=[Ln("e")], w=[Ln("e")])
                yield
                A("dve", TT(L_ti[:, :n], L_ti[:, :n], L_th[:, :n], ALU.mult), r=[Ln("ti"), Ln("th")], w=[Ln("ti")])
                yield
                if not samp:
                    if t == 0:
                        init = 0.0 if hf == 0 else hc[:, j:j + 1]
                        ir = [("hc",)]
                    else:
                        init = Lsets[0][1][:, 511:512]
                        ir = [("L", 0, "tr")]
                    A("dve", _mk(lambda e, n=n, init=init: e.tensor_tensor_scan(
                        out=L_tr[:, :n], data0=L_e[:, :n], data1=L_ti[:, :n], initial=init,
                        op0=ALU.mult, op1=ALU.add), 0.07 + 2.2 * n / 960.0),
                      r=[Ln("e"), Ln("ti"), Ln("tr")] + ir, w=[Ln("tr")])
                    yield
                    if t == 1 and hf == 0:
                        A("act", ACT(hc[:, j:j + 1], L_tr[:, n - 1:n], AF.Copy), r=[Ln("tr")], w=[("hc",)])
                        A("act", ACT(cv[:, j, :], xb[:, jb, 3 + HALF - 3:3 + HALF], AF.Copy),
                          r=[("xb", jb, t)], w=[("cv", j)])
                    if t == 1 and hf == 1:
                        A("act", ACT(fin[:, j, 0:1], L_tr[:, n - 1:n], AF.Copy), r=[Ln("tr")], w=[("fin", j)])
                        A("act", ACT(fin[:, j, 1:4], xb[:, jb, 3 + HALF - 3:3 + HALF], AF.Copy),
                          r=[("xb", jb, t)], w=[("fin", j)])
                else:
                    A("dve", TT(L_tr[:, :n], L_e[:, :n], stT[:, j, 0:NSAMP], ALU.mult),
                      r=[Ln("e"), ("stT",), Ln("tr")], w=[Ln("tr")])
                    A("dve", TT(L_tr[:, :n], L_tr[:, :n], L_ti[:, :n], ALU.add), r=[Ln("tr"), Ln("ti")],
                      w=[Ln("tr")])
                    A("act", ACT(fin[:, j, 4:20], L_tr[:, :n], AF.Copy), r=[Ln("tr")], w=[("fin", j)])
                    A("act", ACT(fin[:, j, 20:36], xb[:, jb, 3 + HALF:3 + HALF + NSAMP], AF.Copy),
                      r=[("xb", jb, t)], w=[("fin", j)])
                yield
                A("dve", STT(ydst[:, j, off:off + n], L_g[:, :n], 1.0 / gs, L_tr[:, :n], ALU.mult, ALU.mult),
                  r=[Ln("g"), Ln("tr")], w=[("yb", j, t)])
                yield

            def run_interleaved(gens):
                gens = list(gens)
                for _ in range(STAGGER):
                    try:
                        next(gens[0])
                    except StopIteration:
                        gens.pop(0)
                        break
                while gens:
                    for g_ in list(gens):
                        try:
                            next(g_)
                        except StopIteration:
                            gens.remove(g_)

            for j in range(8):
                if j % 4 == 0:
                    sx = load_wa(winv, 2048 + (j // 4) * 512, 512)
                    sgb = load_wa(winv, 3072 + (j // 4) * 512, 512)
                    spa = load_wa(pav, (j // 4) * 512, 512)
                    sga = load_wa(winv, 4096 + (j // 4) * 512, 512)
                jb = j % 2
                if hf == 0:
                    A("dve", MSET(xb[:, jb, 0:3], 0.0), w=[("xbh", jb)])
                else:
                    A("dve", CP(xb[:, jb, 0:3], cv[:, j, :]), r=[("cv", j)], w=[("xbh", jb)])
                gl_ = [lru_chain(j, 0, 0, 512, 0, sx, sgb), lru_chain(j, 1, 512, 512, 1, sx, sgb)]
                if hf == 0:
                    gl_.append(lru_chain(j, 2, HALF, NSAMP, 2, sx, sgb))
                gl_.append(c_chain(j, spa, sga))
                run_interleaved(gl_)
            c0_, c1_ = (4, 36) if hf == 0 else (0, 4)
            ncol = c1_ - c0_
            b0, b1 = P.nb(), P.nb()

            def tr_fin(e, b0=b0, b1=b1, c0_=c0_, c1_=c1_, ncol=ncol):
                last = None
                for j in range(8):
                    bb = b0 if j < 4 else b1
                    last = e.transpose(ps[:ncol, bb, (j % 4) * 128:(j % 4 + 1) * 128], fin[:, j, c0_:c1_], ident[:])
                return last
            A("pe", _mk(tr_fin, 1.0), r=[("fin", j) for j in range(8)] + [("c", "ident")], w=[("ps", b0), ("ps", b1)])
            A("act", ACT(stg[:ncol, 0, 0:512], ps[:ncol, b0, :], AF.Copy), r=[("ps", b0)], w=[("stg", 0)])
            A("act", ACT(stg[:ncol, 0, 512:1024], ps[:ncol, b1, :], AF.Copy), r=[("ps", b1), ("stg", 0)],
              w=[("stg", 0)])
            if hf == 0:
                P.dma("sp", [(h_s[:, :], stg[0:16, 0, :]), (c_s[:, 2, :], stg[16:32, 0, :])], r=[("stg", 0)],
                      w=[("out", "hs")], key="so0")
                out_res.append(("out", "hs"))
            else:
                P.dma("sp", [(h_p[:, :], stg[0:1, 0, :]), (c_p[:, :], stg[1:4, 0, :])], r=[("stg", 0)],
                      w=[("out", "hp")], key="so0")
                out_res.append(("out", "hp"))
            for j in range(8):
                if j % 4 == 0:
                    spb_ = load_wa(pbv, (j // 4) * 512, 512)
                    sgb2 = load_wa(winv, 5120 + (j // 4) * 512, 512)
                jl = j % 4
                for t, (off, n) in enumerate(TTl):
                    bp, bg = P.nb(), P.nb()
                    A("pe", MM(ps[:, bp, :n], [(wa[:, spb_, k, jl * 128:(jl + 1) * 128], ybv[:, k, off:off + n])
                                               for k in range(8)]),
                      r=[("yb", k, t) for k in range(8)] + [("wa", spb_)], w=[("ps", bp)])
                    A("pe", MM(ps[:, bg, :n], [(wa[:, sgb2, k, jl * 128:(jl + 1) * 128], hT[:, k, off:off + n])
                                               for k in range(8)]), r=hTr(t) + [("wa", sgb2)], w=[("ps", bg)])
                    sl = as_next()
                    A("act", ACT(actS[:, sl, :n], ps[:, bg, :n], AF.Tanh, scale=0.5), r=[("ps", bg)], w=[("actS", sl)])
                    A("dve", STT(actS[:, sl, :n], actS[:, sl, :n], 1.0, ps[:, bp, :n], ALU.add, ALU.mult),
                      r=[("actS", sl), ("ps", bp)], w=[("actS", sl)])
                    A("dve", TT(mb_[:, j, off:off + n], actS[:, sl, :n], mb_[:, j, off:off + n], ALU.add),
                      r=[("actS", sl), ("m", j, t)], w=[("m", j, t)])
            so0 = load_wa(wov, 0, 512)
            so1 = load_wa(wov, 512, 512)
            for t, (off, n) in enumerate(TTl):
                for j in range(8):
                    so = so0 if j < 4 else so1
                    jl = j % 4
                    b = P.nb()
                    A("pe", MM(ps[:, b, :n], [(wa[:, so, k, jl * 128:(jl + 1) * 128], mb_[:, k, off:off + n])
                                              for k in range(8)]),
                      r=[("m", k, t) for k in range(8)] + [("wa", so)], w=[("ps", b)])
                    A("dve", STT(xT[:, j, off:off + n], ps[:, b, :n], 0.5, xT[:, j, off:off + n], ALU.mult, ALU.add),
                      r=[("ps", b), ("xT", j, t)], w=[("xT", j, t)])

        oneb = sb("oneb", (128, 1), F32)
        A("pool", MSET(oneb[:], 1.0), w=[("c", "oneb")])
        epsb = sb("epsb", (128, 1), F32)
        A("pool", MSET(epsb[:], EPS), w=[("c", "epsb")])

        for hf in range(2):
            TTl = [(0, 512), (512, 512)] + ([(HALF, NSAMP)] if hf == 0 else [])
            tiles = list(range(8)) + ([8] if hf == 0 else [])
            P.fence("ARENA")
            for i in tiles:
                npt = 128 if i < 8 else NSAMP
                off = i * 128 if i < 8 else HALF
                t = off // 512
                nreg = 4 if hf == 0 else 2
                if i < nreg:
                    s = i if hf == 0 else 2 + i
                    slot, slotr, skey = SLOT[s], SLOTR[s], "stg%d" % s
                else:
                    k_ = i - nreg
                    slot, slotr, skey = ARX[k_], [("arx", k_)], "arx%d" % k_
                src = xp[hf * HALF + i * 128:hf * HALF + (i + 1) * 128, :] if i < 8 else xs[:, :]
                P.dma("sp", [(slot[:npt, :], src)], w=slotr, key=skey)
                for hh in range(2):
                    b = P.nb()

                    def tr_x(e, b=b, hh=hh, slot=slot, npt=npt):
                        last = None
                        for q in range(4):
                            j = hh * 4 + q
                            last = e.transpose(ps[:, b, q * npt:(q + 1) * npt], slot[:npt, j * 128:(j + 1) * 128],
                                               ident[:npt, :npt])
                        return last
                    A("pe", _mk(tr_x, 1.2), r=slotr + [("c", "ident")], w=[("ps", b)])
                    A("act" if hh == 0 else "dve",
                      (ACT(xT[:, hh * 4:(hh + 1) * 4, off:off + npt],
                           ps[:, b, 0:4 * npt].rearrange("p (q n) -> p q n", q=4), AF.Copy) if hh == 0 else
                       CP(xT[:, hh * 4:(hh + 1) * 4, off:off + npt],
                          ps[:, b, 0:4 * npt].rearrange("p (q n) -> p q n", q=4))),
                      r=[("ps", b)], w=[("xT", j, t) for j in range(hh * 4, hh * 4 + 4)])
            P.fence("ARENA")
            if stage >= 1:
                rmsnorm(TTl, 0)
            if hf == 0:
                late_setup()
            if stage >= 1:
                ffn(TTl, f1g, f1u, f1d)
            if stage >= 2:
                P.fence("ARENA")
                P.fence("WBMEM")
                rmsnorm(TTl, 1)
                mixer(hf, TTl)
                P.fence("ARENA")
                P.fence("WBMEM")
            if stage >= 3:
                rmsnorm(TTl, 2)
                ffn(TTl, f2g, f2u, f2d)
            if debug and hf == 0:
                P.dma("sp", [(dbg[:, :], xT[:].rearrange("p j n -> p (j n)"))], r=xTr(0) + xTr(1) + xTr(2),
                      w=[("out", "dbg")], key="dbg")
                out_res.append(("out", "dbg"))
            P.dma("sp", [(nbuf[:, :], bass.AP(fnm.tensor, 0, [[0, 128], [1, 1024]]))], w=[("c", "nbuf")], key="nbuf")
            for i in tiles:
                npt = 128 if i < 8 else NSAMP
                off = i * 128 if i < 8 else HALF
                t = off // 512
                s = (i % 2) if hf == 0 else (i % 4)
                q_ = i % 2
                sv_ = ssv2[:, q_, :]
                b0, b1 = P.nb(), P.nb()

                def tr_o(e, b0=b0, b1=b1, off=off, npt=npt):
                    last = None
                    for j in range(8):
                        bb = b0 if j < 4 else b1
                        last = e.transpose(ps[:npt, bb, (j % 4) * 128:(j % 4 + 1) * 128], xT[:, j, off:off + npt],
                                           ident[:])
                    return last
                A("pe", _mk(tr_o, 2.0), r=xTr(t) + [("c", "ident")], w=[("ps", b0), ("ps", b1)])
                A("act", ACT(rstd[:npt, :], ps[:npt, b0, :], AF.Square, accum_out=sv_[:npt, 0:1]),
                  r=[("ps", b0)], w=[("rstd",), ("ssv", q_)])
                A("act", ACT(rstd[:npt, :], ps[:npt, b1, :], AF.Square, accum_out=sv_[:npt, 1:2]),
                  r=[("ps", b1)], w=[("rstd",), ("ssv", q_)])
                A("dve", TT(sv_[:npt, 2:3], sv_[:npt, 0:1], sv_[:npt, 1:2], ALU.add), r=[("ssv", q_)], w=[("ssv", q_)])
                A("dve", TS(sv_[:npt, 2:3], sv_[:npt, 2:3], 1.0 / 1024.0, EPS, ALU.mult, ALU.add), r=[("ssv", q_)],
                  w=[("ssv", q_)])
                A("act", ACT(sv_[:npt, 3:4], sv_[:npt, 2:3], AF.Sqrt), r=[("ssv", q_)], w=[("ssv", q_)])
                A("dve", lambda e, npt=npt, sv_=sv_: e.reciprocal(out=sv_[:npt, 3:4], in_=sv_[:npt, 3:4]),
                  r=[("ssv", q_)], w=[("ssv", q_)])
                A("dve", STT(SLOT[s][:npt, 0:512], ps[:npt, b0, :], sv_[:npt, 3:4], fnb[:npt, 0:512], ALU.mult,
                             ALU.mult), r=[("ps", b0), ("ssv", q_), ("c", "nbuf")], w=SLOTR[s])
                A("dve", STT(SLOT[s][:npt, 512:1024], ps[:npt, b1, :], sv_[:npt, 3:4], fnb[:npt, 512:1024], ALU.mult,
                             ALU.mult), r=[("ps", b1), ("ssv", q_), ("c", "nbuf")] + SLOTR[s], w=SLOTR[s])
                dst = y_p[hf * HALF + i * 128:hf * HALF + (i + 1) * 128, :] if i < 8 else y_s[:, :]
                rk = ("out", "y%d_%d" % (hf, i))
                P.dma("sp", [(dst, SLOT[s][:npt, :])], r=SLOTR[s], w=[rk], key="so%d" % s)
                out_res.append(rk)
        A("sp", None, r=out_res)
        P.emit(nc, stack)
    return nc


_CACHE = {}


def make_in_maps(inp):
    f = lambda a: np.ascontiguousarray(np.asarray(a, dtype=np.float32))
    vec = np.stack([f(inp["ffn1_norm"])[0], f(inp["mix_norm"])[0], f(inp["ffn2_norm"])[0],
                    f(inp["conv_w"])[0, 0], f(inp["conv_w"])[0, 1], f(inp["conv_w"])[0, 2], f(inp["conv_w"])[0, 3],
                    f(inp["conv_b"])[0], f(inp["lru_b_r"])[0].reshape(-1), f(inp["lru_b_i"])[0].reshape(-1),
                    f(inp["lru_lambda"])[0]], axis=0)
    shared = {
        "vecs": f(vec), "gvn": f(inp["gmlp_v_norm"]).reshape(1, D), "fnm": f(inp["final_norm"]).reshape(1, D),
        "spb": f(inp["spatial_b"]).reshape(1, D), "spw": f(inp["spatial_w"])[0],
        "wr": f(inp["lru_w_r"])[0], "wi": f(inp["lru_w_i"])[0],
        "f1g": f(inp["ffn1_w_gate"])[0], "f1u": f(inp["ffn1_w_up"])[0], "f1d": f(inp["ffn1_w_down"])[0],
        "win": f(inp["w_in"])[0], "pa": f(inp["proj_a"])[0], "pb": f(inp["proj_b"])[0], "wo": f(inp["w_out"])[0],
        "f2g": f(inp["ffn2_w_gate"])[0], "f2u": f(inp["ffn2_w_up"])[0], "f2d": f(inp["ffn2_w_down"])[0],
    }
    xpr = f(inp["x_prompt"])
    xsm = f(inp["x_sample"])
    slh = f(inp["state_lru_h"])
    scv = f(inp["state_conv"])
    maps = []
    for c in range(8):
        m = dict(shared)
        m["xp"] = xpr[c]
        m["xs"] = f(xsm[c * NSAMP:(c + 1) * NSAMP, 0, :])
        m["sh"] = f(slh[0, c * NSAMP:(c + 1) * NSAMP, :])
        m["sc"] = f(scv[0, c * NSAMP:(c + 1) * NSAMP])
        maps.append(m)
    return maps


def kernel(**inputs):
    if "nc" not in _CACHE:
        _CACHE["nc"] = build()
    nc = _CACHE["nc"]
    maps = make_in_maps(inputs)
    res = run_bass_kernel_spmd(nc, maps, core_ids=list(range(8)))
    R = res.results
    y_prompt = np.stack([R[c]["y_p"] for c in range(8)], axis=0)
    y_sample = np.concatenate([R[c]["y_s"] for c in range(8)], axis=0)[:, None, :]
    h_pr = np.concatenate([R[c]["h_p"] for c in range(8)], axis=0)[None]
    c_pr = np.stack([R[c]["c_p"] for c in range(8)], axis=0)[None]
    h_sm = np.concatenate([R[c]["h_s"] for c in range(8)], axis=0)[None]
    c_sm = np.concatenate([R[c]["c_s"] for c in range(8)], axis=0)[None]
    v_sm = np.concatenate([R[c]["v_s"] for c in range(8)], axis=0)[None, :, None, :]
    return (y_prompt.astype(np.float32), y_sample.astype(np.float32), h_pr.astype(np.float32),
            c_pr.astype(np.float32), h_sm.astype(np.float32), c_sm.astype(np.float32), v_sm.astype(np.float32))
```
